# Optimizing a Trainium2 kernel written in Bass

```python
import jax, jax.numpy as jnp
from jax import lax

D_MODEL = 1024
BATCH = 4
SEQ = 8192
DEPTH = 1
DEC_BATCH = 32
DEC_SEQ = 16
PAST_LEN = 4096

CHUNK = 64
CONV_CH = 512
CONV_WIDTH = 31
N_HEADS = 8
N_KV_HEADS = 2
HEAD_DIM = 64
Q_PER_KV = N_HEADS // N_KV_HEADS
ATTN_WIDTH = N_HEADS * HEAD_DIM
KV_WIDTH = N_KV_HEADS * HEAD_DIM
MIX_WIDTH = CONV_CH + ATTN_WIDTH
IN_COLS = 2 * CONV_CH + ATTN_WIDTH + 2 * KV_WIDTH
SPLITS = [CONV_CH, 2 * CONV_CH, 2 * CONV_CH + ATTN_WIDTH, 2 * CONV_CH + ATTN_WIDTH + KV_WIDTH]
WINDOW = 128
WINDOW_CHUNKS = WINDOW // CHUNK
ROPE_DIM = HEAD_DIM // 4
ROPE_THETA = 500000.0
D_FF = 2816
EPS = 1e-6

kernel_name = "hymba_conformer_swa_sink_stream_step"


def rms_norm(x, g):
    xf = x.astype(jnp.float32)
    y = xf * lax.rsqrt(jnp.mean(xf * xf, axis=-1, keepdims=True) + EPS)
    return (y * g.astype(jnp.float32)).astype(x.dtype)


def layer_norm(x, g, b):
    xf = x.astype(jnp.float32)
    mu = jnp.mean(xf, axis=-1, keepdims=True)
    xc = xf - mu
    y = xc * lax.rsqrt(jnp.mean(xc * xc, axis=-1, keepdims=True) + EPS)
    return (y * g.astype(jnp.float32) + b.astype(jnp.float32)).astype(x.dtype)


def swiglu_ffn(x, g, w_up, w_down):
    a, b = jnp.split(rms_norm(x, g) @ w_up, 2, axis=-1)
    return (jax.nn.silu(a) * b) @ w_down


def partial_rope(x, pos):
    half = ROPE_DIM // 2
    inv = jnp.power(jnp.float32(ROPE_THETA), -jnp.arange(half, dtype=jnp.float32) / half)
    ang = pos.astype(jnp.float32)[:, None] * inv[None, :]
    cos = jnp.cos(ang)[:, None, :]
    sin = jnp.sin(ang)[:, None, :]
    xf = x.astype(jnp.float32)
    x1, x2 = xf[..., :half], xf[..., half:ROPE_DIM]
    out = jnp.concatenate([x1 * cos - x2 * sin, x2 * cos + x1 * sin, xf[..., ROPE_DIM:]], axis=-1)
    return out.astype(x.dtype)


def sink_attend(q, k, v, key_valid, sinks):
    s = jnp.einsum('bnqhgd,bnkhd->bnhgqk', q, k, preferred_element_type=jnp.float32) * (HEAD_DIM ** -0.5)
    if key_valid is not None:
        s = jnp.where(key_valid[None, :, None, None, None, :], s, -jnp.inf)
    sink = jnp.broadcast_to(sinks.astype(jnp.float32).reshape(1, 1, N_KV_HEADS, Q_PER_KV, 1, 1), s.shape[:-1] + (1,))
    p = jax.nn.softmax(jnp.concatenate([s, sink], axis=-1), axis=-1)[..., :-1]
    return jnp.einsum('bnhgqk,bnkhd->bnqhgd', p.astype(v.dtype), v)


def banded_window_attention(q, k, v, sinks):
    B, T = q.shape[:2]
    n_c = T // CHUNK
    qb = q.reshape(B, n_c, CHUNK, N_KV_HEADS, Q_PER_KV, HEAD_DIM)

    def band(t):
        tc = t.reshape(B, n_c, CHUNK, N_KV_HEADS, HEAD_DIM)
        tp = jnp.pad(tc, ((0, 0), (WINDOW_CHUNKS, 0), (0, 0), (0, 0), (0, 0)))
        return jnp.concatenate([tp[:, j:j + n_c] for j in range(WINDOW_CHUNKS + 1)], axis=2)

    kb, vb = band(k), band(v)
    key_chunk = jnp.arange(n_c)[:, None] - WINDOW_CHUNKS + jnp.arange(WINDOW_CHUNKS + 1)[None, :]
    valid = jnp.repeat(key_chunk >= 0, CHUNK, axis=1)
    o = sink_attend(qb, kb, vb, valid, sinks)
    return o.reshape(B, T, ATTN_WIDTH)


def conv_tail(u_hist, w_dw, b_dw, g_cn, b_cn):
    y = lax.conv_general_dilated(u_hist, w_dw[:, None, :], window_strides=(1,), padding='VALID',
                                 dimension_numbers=('NWC', 'WIO', 'NWC'), feature_group_count=CONV_CH)
    return jax.nn.silu(layer_norm(y + b_dw, g_cn, b_cn))


def encoder_layer(x, pos, conv_prev, k_prev, v_prev, g_ff1, w_ff1_in, w_ff1_out, g_mix, w_in, g_q, g_k,
                  sinks, w_dw, b_dw, g_cn, b_cn, w_out, g_ff2, w_ff2_in, w_ff2_out):
    B, T, _ = x.shape
    x = x + 0.5 * swiglu_ffn(x, g_ff1, w_ff1_in, w_ff1_out)
    z = rms_norm(x, g_mix) @ w_in
    cv, cg, q, k, v = jnp.split(z, SPLITS, axis=-1)
    u = cv * jax.nn.sigmoid(cg)
    if conv_prev is None:
        conv_prev = jnp.zeros((B, CONV_WIDTH - 1, CONV_CH), u.dtype)
    u_hist = jnp.concatenate([conv_prev, u], axis=1)
    c_out = conv_tail(u_hist, w_dw, b_dw, g_cn, b_cn)
    new_conv = u_hist[:, -(CONV_WIDTH - 1):]
    q = partial_rope(rms_norm(q.reshape(B, T, N_HEADS, HEAD_DIM), g_q), pos)
    k = partial_rope(rms_norm(k.reshape(B, T, N_KV_HEADS, HEAD_DIM), g_k), pos)
    v = v.reshape(B, T, N_KV_HEADS, HEAD_DIM)
    if k_prev is None:
        a_out = banded_window_attention(q, k, v, sinks)
        k_all, v_all = k, v
    else:
        k_all = jnp.concatenate([k_prev, k], axis=1)
        v_all = jnp.concatenate([v_prev, v], axis=1)
        qs = q.reshape(B, 1, T, N_KV_HEADS, Q_PER_KV, HEAD_DIM)
        a_out = sink_attend(qs, k_all[:, None], v_all[:, None], None, sinks).reshape(B, T, ATTN_WIDTH)
    x = x + jnp.concatenate([c_out, a_out], axis=-1) @ w_out
    x = x + 0.5 * swiglu_ffn(x, g_ff2, w_ff2_in, w_ff2_out)
    return x, new_conv, k_all[:, -WINDOW:], v_all[:, -WINDOW:]


def setup_inputs(seed: int = 0) -> dict:
    key = jax.random.key(seed)
    ks = jax.random.split(key, 24)
    f32 = jnp.float32
    nrm = lambda k, shape, s: jax.random.normal(k, shape, f32) * s
    L = DEPTH
    return {
        "x_prompt": nrm(ks[0], (BATCH, SEQ, D_MODEL), 1.0),
        "x_sample": nrm(ks[1], (DEC_BATCH, DEC_SEQ, D_MODEL), 1.0),
        "state_conv": nrm(ks[2], (L, DEC_BATCH, CONV_WIDTH - 1, CONV_CH), 0.5),
        "cache_k_win": nrm(ks[3], (L, DEC_BATCH, WINDOW, N_KV_HEADS, HEAD_DIM), 1.0),
        "cache_v_win": nrm(ks[4], (L, DEC_BATCH, WINDOW, N_KV_HEADS, HEAD_DIM), 1.0),
        "g_ff1": 1.0 + nrm(ks[5], (L, D_MODEL), 0.02),
        "w_ff1_in": nrm(ks[6], (L, D_MODEL, 2 * D_FF), D_MODEL ** -0.5),
        "w_ff1_out": nrm(ks[7], (L, D_FF, D_MODEL), D_FF ** -0.5),
        "g_mix": 1.0 + nrm(ks[8], (L, D_MODEL), 0.02),
        "w_in": nrm(ks[9], (L, D_MODEL, IN_COLS), D_MODEL ** -0.5),
        "g_q": 1.0 + nrm(ks[10], (L, HEAD_DIM), 0.02),
        "g_k": 1.0 + nrm(ks[11], (L, HEAD_DIM), 0.02),
        "sinks": nrm(ks[12], (L, N_HEADS), 0.5),
        "w_dw": nrm(ks[13], (L, CONV_WIDTH, CONV_CH), CONV_WIDTH ** -0.5),
        "b_dw": nrm(ks[14], (L, CONV_CH), 0.02),
        "g_cn": 1.0 + nrm(ks[15], (L, CONV_CH), 0.02),
        "b_cn": nrm(ks[16], (L, CONV_CH), 0.02),
        "w_out": nrm(ks[17], (L, MIX_WIDTH, D_MODEL), MIX_WIDTH ** -0.5),
        "g_ff2": 1.0 + nrm(ks[18], (L, D_MODEL), 0.02),
        "w_ff2_in": nrm(ks[19], (L, D_MODEL, 2 * D_FF), D_MODEL ** -0.5),
        "w_ff2_out": nrm(ks[20], (L, D_FF, D_MODEL), D_FF ** -0.5),
    }


def reference(x_prompt, x_sample, state_conv, cache_k_win, cache_v_win, g_ff1, w_ff1_in, w_ff1_out,
              g_mix, w_in, g_q, g_k, sinks, w_dw, b_dw, g_cn, b_cn, w_out, g_ff2, w_ff2_in, w_ff2_out):
    pos_p = jnp.arange(x_prompt.shape[1], dtype=jnp.int32)
    pos_s = PAST_LEN + jnp.arange(x_sample.shape[1], dtype=jnp.int32)
    hp, hs = x_prompt, x_sample
    cp_l, kp_l, vp_l, cs_l, kss_l, vs_l = [], [], [], [], [], []
    for l in range(DEPTH):
        w = (g_ff1[l], w_ff1_in[l], w_ff1_out[l], g_mix[l], w_in[l], g_q[l], g_k[l], sinks[l], w_dw[l],
             b_dw[l], g_cn[l], b_cn[l], w_out[l], g_ff2[l], w_ff2_in[l], w_ff2_out[l])
        hp, cp, kp, vp = encoder_layer(hp, pos_p, None, None, None, *w)
        hs, cs, kss, vs = encoder_layer(hs, pos_s, state_conv[l], cache_k_win[l], cache_v_win[l], *w)
        cp_l.append(cp); kp_l.append(kp); vp_l.append(vp)
        cs_l.append(cs); kss_l.append(kss); vs_l.append(vs)
    conv_state_prompt = jnp.stack(cp_l)
    k_win_prompt = jnp.stack(kp_l)
    v_win_prompt = jnp.stack(vp_l)
    conv_state_sample = jnp.stack(cs_l)
    k_win_sample = jnp.stack(kss_l)
    v_win_sample = jnp.stack(vs_l)
    return (hp, hs, conv_state_prompt, k_win_prompt, v_win_prompt, conv_state_sample, k_win_sample, v_win_sample)
```

```python
import numpy as np
from contextlib import ExitStack
import concourse.bass as bass
import concourse.mybir as mybir
from concourse.bass_utils import run_bass_kernel_spmd

F32 = mybir.dt.float32
BF16 = mybir.dt.bfloat16
AF = mybir.ActivationFunctionType
ALU = mybir.AluOpType

ENGS = ("sync", "scalar", "vector", "gpsimd", "tensor")
SEM_MAX = 30000


class _Op:
    __slots__ = ("eng", "fn", "deps", "idx", "dma_sem", "tok", "signal")

    def __init__(self, eng, fn, idx, dma_sem):
        self.eng, self.fn, self.idx, self.dma_sem = eng, fn, idx, dma_sem
        self.deps = set()
        self.tok = None
        self.signal = False


class Prog:
    def __init__(self):
        self.ops = []
        self.last_w = {}
        self.readers = {}
        self.last_dma_on_sem = {}

    def add(self, eng, fn, reads=(), writes=(), dma=None):
        op = _Op(eng, fn, len(self.ops), dma)
        self.ops.append(op)
        for k in reads:
            w = self.last_w.get(k)
            if w is not None:
                op.deps.add(w)
        for k in writes:
            w = self.last_w.get(k)
            if w is not None:
                op.deps.add(w)
            r = self.readers.get(k)
            if r:
                op.deps.update(r)
        for k in reads:
            self.readers.setdefault(k, []).append(op.idx)
        for k in writes:
            self.last_w[k] = op.idx
            self.readers[k] = []
        if dma is not None:
            p = self.last_dma_on_sem.get(dma)
            if p is not None:
                op.deps.add(p)
            self.last_dma_on_sem[dma] = op.idx
        op.deps.discard(op.idx)
        return op

    def emit(self, nc, stack):
        ops = self.ops
        for op in ops:
            keep = set()
            for d in op.deps:
                p = ops[d]
                if p.dma_sem is None and op.dma_sem is None and p.eng == op.eng == "tensor":
                    continue
                keep.add(d)
            best = {}
            for d in keep:
                p = ops[d]
                k = ("d", p.dma_sem) if p.dma_sem is not None else ("e", p.eng)
                if k not in best or best[k] < d:
                    best[k] = d
            op.deps = set(best.values())
            for d in op.deps:
                ops[d].signal = True
        dma_names = sorted({op.dma_sem for op in ops if op.dma_sem is not None})
        dma_sems = {n: stack.enter_context(nc.semaphore("d_" + n)) for n in dma_names}
        dma_cnt = {n: 0 for n in dma_names}
        eng_sems = {e: [] for e in ENGS}
        eng_cnt = {e: 0 for e in ENGS}
        for op in ops:
            if op.dma_sem is not None:
                dma_cnt[op.dma_sem] += 16
                op.tok = (dma_sems[op.dma_sem], dma_cnt[op.dma_sem])
            elif op.signal:
                c = eng_cnt[op.eng]
                k, v = divmod(c, SEM_MAX)
                if k >= len(eng_sems[op.eng]):
                    eng_sems[op.eng].append(stack.enter_context(nc.semaphore("e_%s%d" % (op.eng, k))))
                op.tok = (eng_sems[op.eng][k], v + 1)
                eng_cnt[op.eng] = c + 1
        print('signals', eng_cnt, 'dma', dma_cnt)
        block = stack.enter_context(nc.Block())
        per_eng = {e: [op for op in ops if op.eng == e] for e in ENGS}
        final_dma = {}
        for op in ops:
            if op.dma_sem is not None:
                final_dma[op.dma_sem] = op.tok

        def make(e):
            def body(eng):
                waited = {}
                for op in per_eng[e]:
                    need = {}
                    for d in op.deps:
                        s, v = ops[d].tok
                        if need.get(id(s), (None, 0))[1] < v:
                            need[id(s)] = (s, v)
                    for s, v in need.values():
                        if waited.get(id(s), 0) < v:
                            eng.wait_ge(s, v)
                            waited[id(s)] = v
                    ins = op.fn(eng)
                    if op.tok is not None:
                        ins.then_inc(op.tok[0], 16 if op.dma_sem is not None else 1)
                if e == "sync":
                    for s, v in final_dma.values():
                        if waited.get(id(s), 0) < v:
                            eng.wait_ge(s, v)
                            waited[id(s)] = v
            return body

        block.sync(make("sync"))
        block.scalar(make("scalar"))
        block.vector(make("vector"))
        block.gpsimd(make("gpsimd"))
        block.tensor(make("tensor"))
        return {e: len(per_eng[e]) for e in ENGS}


D = 1024
DFF = 2816
NJ = 22
SEQ = 8192
HALF = 4096
NT = 4
TM = 1024
TC = 1216
HALO0, SAMP0 = 1024, 1152
NROWS = HALF + 128 + 64
EPS = 1e-6
NSLOT = 4
W_FF1U, W_FF1D, W_IN, W_OUT, W_FF2U, W_FF2D, NCHUNK = 0, 22, 38, 45, 49, 71, 87
PG1, PGM, PG2, PBDW, PGCN, PBCN, PGQ, PGK, PNG, PNB = 0, 8, 16, 24, 28, 32, 36, 37, 40, 44


LEVEL = 9
MIXL = 99
SKIP = set()
NTILES = NT


def build_nc():
    nc = bass.Bass("TRN2", target_bir_lowering=False)
    din = lambda n, s: nc.dram_tensor(n, s, F32, kind="ExternalInput").ap()
    dout = lambda n, s: nc.dram_tensor(n, s, F32, kind="ExternalOutput").ap()
    xin = din("xin", [NROWS, D])
    wall = din("wall", [NCHUNK, 128, 2048])
    rope = din("rope", [2, 128, NROWS])
    setup_rows = din("setup_rows", [38, 128])
    wdw = din("wdw", [31, 512])
    sinks = din("sinks", [1, 8])
    cmat = din("cmat", [4, 128, 128])
    opad = din("opad", [2, 128, 192])
    smask = din("smask", [64, 256])
    sconv = din("sconv", [4, 30, 512])
    ck = din("ck", [4, 128, 128])
    cv = din("cv", [4, 128, 128])
    yout = dout("yout", [HALF + 64, D])
    ocs_p = dout("ocs_p", [30, 512])
    okw_p = dout("okw_p", [128, 128])
    ovw_p = dout("ovw_p", [128, 128])
    ocs_s = dout("ocs_s", [4, 30, 512])
    okw_s = dout("okw_s", [4, 128, 128])
    ovw_s = dout("ovw_s", [4, 128, 128])

    P = Prog()
    with ExitStack() as st:
        def sb(name, shape, dt):
            return st.enter_context(nc.sbuf_tensor(name, shape, dt))

        RT = sb("RT", [128, 8, TC], F32)
        XNR = sb("XNR", [128, 8 * TC], BF16)
        HR = sb("HR", [128, NJ * TC], BF16)
        XN = XNR[:, :].rearrange("p (c n) -> p c n", c=8)
        MIX = XNR[:, 0:8 * 1152].rearrange("p (c n) -> p c n", c=8)
        H = HR[:, :].rearrange("p (c n) -> p c n", c=NJ)
        hoff = [0]

        def carve(nelem_bf16):
            a = HR[:, hoff[0]:hoff[0] + nelem_bf16]
            hoff[0] += nelem_bf16
            return a

        U = carve(4 * 1056).rearrange("p (c n) -> p c n", c=4)
        Q = carve(4 * 1088).rearrange("p (c n) -> p c n", c=4)
        KT = carve(2 * 1152).rearrange("p (g n) -> p g n", g=2)
        VP = carve(9 * 2 * 192).rearrange("p (k g n) -> p k g n", k=9, g=2)
        YFraw = carve(2 * 4 * 512)
        YF = YFraw.bitcast(F32).rearrange("p (c n) -> p c n", c=4)
        PT2 = YFraw[:, 0:4 * 512].rearrange("p (c n) -> p c n", c=4)
        PT = carve(4 * 512).rearrange("p (c n) -> p c n", c=4)
        SF = [carve(2 * 512).bitcast(F32) for _ in range(4)]
        SB = [carve(512) for _ in range(3)]
        assert hoff[0] <= NJ * TC, hoff[0]

        ring = [sb("ring%d" % i, [128, 2048], BF16) for i in range(NSLOT)]
        XS = [sb("XS%d" % i, [128, D], F32) for i in range(2)]
        YS = [sb("YS%d" % i, [128, D], F32) for i in range(2)]
        OST = sb("OST", [128, 512], F32)
        DIAG = sb("DIAG", [128, 4, 31, 128], BF16)
        CT = sb("CT", [128, TC], BF16)
        ST = sb("ST", [128, TC], BF16)
        CM = sb("CM", [128, 4, 128], F32)
        IDF = CM[:, 0, :]
        IDB = sb("IDB", [128, 128], BF16)
        ONES = sb("ONES", [128, 128], BF16)
        ONESBD = sb("ONESBD", [128, 128], BF16)
        PROT = sb("PROT", [128, 128], BF16)
        OPAD = sb("OPAD", [128, 2, 192], BF16)
        PARAMS = sb("PARAMS", [128, 48], F32)
        WDW = sb("WDW", [128, 4, 31], F32)
        ESINK = sb("ESINK", [128, 4], F32)
        SKROW = sb("SKROW", [1, 8], F32)
        CONSTS = sb("CONSTS", [128, 2], F32)
        DUMMY = sb("DUMMY", [128, 2], F32)
        SQ = [sb("SQ%d" % i, [128, 512], BF16) for i in range(2)]
        RSTD = sb("RSTD", [128, 512], F32)
        SA = [sb("SA%d" % i, [128, 512], F32) for i in range(2)]
        UCAR = sb("UCAR", [128, 4, 32], BF16)
        KCAR = sb("KCAR", [128, 2, 128], BF16)
        VCAR = sb("VCAR", [128, 2, 192], BF16)
        K32 = sb("K32", [128, 192], F32)
        U32M = sb("U32M", [128, 4, 32], F32)
        V32L = sb("V32L", [128, 128], F32)
        USAMP = sb("USAMP", [128, 4, 8, 46], BF16)
        USAMP32 = sb("USAMP32", [128, 4, 64], F32)
        KCT = sb("KCT", [128, 2, 4, 128], BF16)
        KST = sb("KST", [128, 2, 128], BF16)
        VC = sb("VC", [128, 4, 2, 192], BF16)
        VS = sb("VS", [128, 2, 192], BF16)
        V32S = sb("V32S", [128, 128], F32)
        SMASK = sb("SMASK", [128, 256], BF16)
        pb = [st.enter_context(nc.psum_tensor("pb%d" % i, [128, 512], F32)) for i in range(8)]
        EPSC = CONSTS[:, 0:1]
        ONEC = CONSTS[:, 1:2]

        def op(eng, meth, reads, writes, **kw):
            P.add(eng, lambda e: getattr(e, meth)(**kw), reads, writes)

        def act(reads, writes, **kw):
            op("scalar", "activation", reads, writes, **kw)

        def mm(reads, writes, out, lhsT, rhs, start=True, stop=True):
            P.add("tensor", lambda e: e.matmul(out, lhsT=lhsT, rhs=rhs, start=start, stop=stop), reads, writes)

        def tr(reads, writes, out, in_, identity):
            P.add("tensor", lambda e: e.transpose(out=out, in_=in_, identity=identity), reads, writes)

        def dma(eng, sem, reads, writes, out, in_):
            P.add(eng, lambda e: e.dma_start(out=out, in_=in_), reads, writes, dma=sem)

        def gk(name, c0, c1):
            return ["%s.%d" % (name, g) for g in range(c0 // 128, (c1 - 1) // 128 + 1)]

        def pk(i):
            return "pb%d" % i

        dma("sync", "su0", [], ["CM"], CM[:], cmat.rearrange("k p n -> p k n"))
        SROW = XS[0][0:38, 0:128]
        WROW = XS[1][0:31, 0:512]
        dma("sync", "xs0", [], ["XS0"], SROW, setup_rows[:, :])
        dma("sync", "xs1", [], ["XS1"], WROW, wdw[:, :])
        dma("sync", "su3", [], ["SKROW"], SKROW[:], sinks[:, :])
        if "opad" not in SKIP:
            dma("gpsimd", "su4", [], ["OPAD"], OPAD[:], opad.rearrange("k p n -> p k n"))
        op("vector", "memset", [], ["CONSTS"], ap=CONSTS[:, 0:1], constant=EPS)
        op("vector", "memset", [], ["CONSTS"], ap=CONSTS[:, 1:2], constant=1.0)
        op("vector", "memset", [], ["DUMMY"], ap=DUMMY[:], constant=0.0)
        op("vector", "tensor_copy", ["CM"], ["IDB"], out=IDB[:], in_=CM[:, 0, :])
        op("vector", "tensor_copy", ["CM"], ["ONES"], out=ONES[:], in_=CM[:, 1, :])
        op("vector", "tensor_copy", ["CM"], ["ONESBD"], out=ONESBD[:], in_=CM[:, 2, :])
        op("vector", "tensor_copy", ["CM"], ["PROT"], out=PROT[:], in_=CM[:, 3, :])
        if "params" not in SKIP:
            tr(["XS0", "CM"], [pk(0)], pb[0][:, 0:38], SROW, CM[0:38, 0, 0:38])
            op("vector", "tensor_copy", [pk(0)], ["PARAMS"], out=PARAMS[:, 0:38], in_=pb[0][:, 0:38])
            op("vector", "tensor_scalar", ["PARAMS"], ["PARAMS"], out=PARAMS[:, PNG:PNG + 8], in0=PARAMS[:, PGCN:PGCN + 8],
               scalar1=-1.0, scalar2=None, op0=ALU.mult)
            for c in range(4):
                tr(["XS1", "CM"], [pk(1)], pb[1][:, c * 31:(c + 1) * 31], WROW[:, c * 128:(c + 1) * 128], CM[0:31, 0, 0:31])
            op("vector", "tensor_copy", [pk(1)], ["WDW"], out=WDW[:].rearrange("p c t -> p (c t)"), in_=pb[1][:, 0:124])
        if "esink" not in SKIP:
            act(["SKROW"], ["SKROW"], out=SKROW[:], in_=SKROW[:], func=AF.Exp)
            mm(["SKROW", "CM"], [pk(2)], pb[2][:, 0:8], CM[0:1, 1, :], SKROW[0:1, :])
            op("vector", "tensor_copy", [pk(2)], ["ESINK"], out=ESINK[0:64, :], in_=pb[2][0:64, 0:4])
            op("vector", "tensor_copy", [pk(2)], ["ESINK"], out=ESINK[64:128, :], in_=pb[2][64:128, 4:8])
        diag_todo = [(c, t) for c in range(4) for t in range(31)]

        def diag_some(n):
            for _ in range(n):
                if diag_todo:
                    c, t = diag_todo.pop(0)
                    op("vector", "tensor_scalar", ["WDW", "IDB"], ["DIAG"], out=DIAG[:, c, t, :], in0=IDB[:],
                       scalar1=WDW[:, c, t:t + 1], scalar2=None, op0=ALU.mult)

        def late_setup():
            diag_some(len(diag_todo))
            if "kct" not in SKIP:
                for s in range(4):
                    dma("sync", "xs0", [], ["XS0"], XS[0][:, s * 128:(s + 1) * 128], ck[s])
                for s in range(4):
                    tr(["XS0", "CM"], [pk(3)], pb[3][:, s * 128:(s + 1) * 128], XS[0][:, s * 128:(s + 1) * 128], IDF)
                op("vector", "memset", [], ["KCT"], ap=KCT[:].rearrange("p g s n -> p (g s n)"), constant=0.0)
                act([pk(3)], ["KCT"], out=KCT[0:64, 0, :, :], in_=pb[3][0:64, :].rearrange("p (s n) -> p s n", s=4), func=AF.Copy)
                act([pk(3)], ["KCT"], out=KCT[64:128, 1, :, :], in_=pb[3][64:128, :].rearrange("p (s n) -> p s n", s=4), func=AF.Copy)
            if "vc" not in SKIP:
                op("vector", "memset", [], ["VC"], ap=VC[:].rearrange("p s g n -> p (s g n)"), constant=0.0)
                op("vector", "memset", [], ["VS"], ap=VS[:].rearrange("p g n -> p (g n)"), constant=0.0)
                dma("gpsimd", "su5", [], ["SMASK"], SMASK[64:128, :], smask[:, :])
                op("vector", "memset", [], ["KST"], ap=KST[:].rearrange("p g n -> p (g n)"), constant=0.0)
                for s in range(4):
                    dma("sync", "xs0", [], ["XS0"], XS[0][:, s * 128:(s + 1) * 128], cv[s])
                vsrc = XS[0][:, 0:512].rearrange("p (s g n) -> p s g n", s=4, g=2)
                op("vector", "tensor_copy", ["XS0"], ["VC"], out=VC[:, :, :, 0:64], in_=vsrc)
                op("vector", "tensor_copy", ["XS0"], ["VC"], out=VC[:, :, :, 128:192], in_=vsrc)
            if "usamp" not in SKIP:
                op("vector", "memset", [], ["USAMP"], ap=USAMP[:].rearrange("p c s n -> p (c s n)"), constant=0.0)
                for s in range(4):
                    dma("sync", "xs1", [], ["XS1"], XS[1][0:30, 0:512], sconv[s])
                    for c in range(4):
                        tr(["XS1", "CM"], [pk(4)], pb[4][:, (s * 4 + c) * 30:(s * 4 + c + 1) * 30],
                           XS[1][0:30, c * 128:(c + 1) * 128], CM[0:30, 0, 0:30])
                    op("vector", "tensor_copy", [pk(4)], ["USAMP"], out=USAMP[:, :, s, 0:30],
                       in_=pb[4][:, s * 120:(s + 1) * 120].rearrange("p (c t) -> p c t", c=4))


        rcnt = [0]

        def ring_load(idx, ncols=2048):
            s = rcnt[0] % NSLOT
            rcnt[0] += 1
            dma("gpsimd", "ring%d" % s, [], ["ring%d" % s], ring[s][:, 0:ncols], wall[idx, :, 0:ncols])
            return s

        bcnt = {"ab": 0, "d": 0, "x": 0, "y": 0}

        def load_dma(row0, n):
            i = bcnt["x"]
            bcnt["x"] += 1
            sl = i % 2
            dma("sync", "xs%d" % sl, [], ["XS%d" % sl], XS[sl][0:n, :], xin[row0:row0 + n, :])
            return i

        def load_compute(i, n, col0):
            sl = i % 2
            xk = "XS%d" % sl
            bA, bB = (4, 5) if i % 2 == 0 else (6, 7)
            for c in range(8):
                bank = pb[bA] if c < 4 else pb[bB]
                tr([xk, "CM"], [pk(bA if c < 4 else bB)], bank[:, (c % 4) * 128:(c % 4) * 128 + n],
                   XS[sl][0:n, c * 128:(c + 1) * 128], CM[0:n, 0, 0:n])
            g = col0 // 128
            act([pk(bA)], ["RT%d.%d" % (c, g) for c in range(4)], out=RT[:, 0:4, col0:col0 + n],
                in_=pb[bA][:, :].rearrange("p (c n) -> p c n", c=4)[:, :, 0:n], func=AF.Copy)
            op("vector", "tensor_copy", [pk(bB)], ["RT%d.%d" % (c, g) for c in range(4, 8)],
               out=RT[:, 4:8, col0:col0 + n], in_=pb[bB][:, :].rearrange("p (c n) -> p c n", c=4)[:, :, 0:n])

        def load_group(row0, n, col0):
            load_compute(load_dma(row0, n), n, col0)

        def store_group(row0, n, col0):
            i = bcnt["y"]
            bcnt["y"] += 1
            bA, bB = (0, 1) if i % 2 == 0 else (2, 3)
            g = col0 // 128
            for c in range(8):
                bi = bA if c < 4 else bB
                tr(["RT%d.%d" % (c, g), "CM"], [pk(bi)], pb[bi][0:n, (c % 4) * 128:(c % 4 + 1) * 128],
                   RT[:, c, col0:col0 + n], IDF)
            ys, yk_ = YS[i % 2], "YS%d" % (i % 2)
            act([pk(bA)], [yk_], out=ys[0:n, 0:512], in_=pb[bA][0:n, :], func=AF.Copy)
            op("vector", "tensor_copy", [pk(bB)], [yk_], out=ys[0:n, 512:1024], in_=pb[bB][0:n, :])
            dma("sync", "ys%d" % (i % 2), [yk_], [], yout[row0:row0 + n, :], ys[0:n, :])

        def norm(blocks, gcol, extra_r=(), extra_w=()):
            for (c0, c1) in blocks:
                N = c1 - c0
                for c in range(8):
                    q = SQ[c % 2]
                    act(["RT%d.%d" % (c, g) for g in range(c0 // 128, (c1 - 1) // 128 + 1)], ["SQ%d" % (c % 2)],
                        out=q[:, 0:N], in_=RT[:, c, c0:c1], func=AF.Square, scale=1.0 / 32.0)
                    mm(["SQ%d" % (c % 2), "ONES"], [pk(7)], pb[7][:, 0:N], ONES[:], q[:, 0:N], start=(c == 0), stop=(c == 7))
                act([pk(7), "CONSTS"], ["RSTD"], out=RSTD[:, 0:N], in_=pb[7][:, 0:N], func=AF.Ln, bias=EPSC)
                act(["RSTD"], ["RSTD"], out=RSTD[:, 0:N], in_=RSTD[:, 0:N], func=AF.Exp, scale=-0.5)
                for c in range(8):
                    op("vector", "scalar_tensor_tensor",
                       ["RSTD", "PARAMS"] + gk("RT%d" % c, c0, c1) + list(extra_r),
                       gk("XN%d" % c, c0, c1) + list(extra_w),
                       out=XN[:, c, c0:c1], in0=RT[:, c, c0:c1], scalar=PARAMS[:, gcol + c:gcol + c + 1],
                       in1=RSTD[:, 0:N], op0=ALU.mult, op1=ALU.mult)

        def ff(blocks, wu, wd, scale, hook=None):
            for j in range(NJ):
                s = ring_load(wu + j)
                W = ring[s][:, :].rearrange("p (k n) -> p k n", k=8)
                rk = "ring%d" % s
                for (c0, c1) in blocks:
                    N = c1 - c0
                    i = bcnt["ab"]
                    bcnt["ab"] += 1
                    a, b = i % 2, 2 + i % 2
                    for kc in range(8):
                        mm([rk] + gk("XN%d" % kc, c0, c1), [pk(a)], pb[a][:, 0:N], W[:, kc, 0:128], XN[:, kc, c0:c1],
                           start=(kc == 0), stop=(kc == 7))
                    for kc in range(8):
                        mm([rk] + gk("XN%d" % kc, c0, c1), [pk(b)], pb[b][:, 0:N], W[:, kc, 128:256], XN[:, kc, c0:c1],
                           start=(kc == 0), stop=(kc == 7))
                    act([pk(a)], ["SA%d" % (i % 2)], out=SA[i % 2][:, 0:N], in_=pb[a][:, 0:N], func=AF.Silu)
                    op("vector", "tensor_tensor", [pk(b), "SA%d" % (i % 2)], gk("H%d" % j, c0, c1),
                       out=H[:, j, c0:c1], in0=pb[b][:, 0:N], in1=SA[i % 2][:, 0:N], op=ALU.mult)
                    if hook is not None:
                        hook()
            for m in range(8):
                s1 = ring_load(wd + 2 * m)
                s2 = ring_load(wd + 2 * m + 1, 768)
                W1 = ring[s1][:, :].rearrange("p (k n) -> p k n", k=16)
                W2 = ring[s2][:, 0:768].rearrange("p (k n) -> p k n", k=6)
                for (c0, c1) in blocks:
                    N = c1 - c0
                    i = bcnt["d"]
                    bcnt["d"] += 1
                    bi = 4 + i % 4
                    for j in range(NJ):
                        w = W1[:, j, :] if j < 16 else W2[:, j - 16, :]
                        mm(["ring%d" % (s1 if j < 16 else s2)] + gk("H%d" % j, c0, c1), [pk(bi)], pb[bi][:, 0:N], w,
                           H[:, j, c0:c1], start=(j == 0), stop=(j == NJ - 1))
                    op("vector", "scalar_tensor_tensor", [pk(bi)] + gk("RT%d" % m, c0, c1), gk("RT%d" % m, c0, c1),
                       out=RT[:, m, c0:c1], in0=pb[bi][:, 0:N], scalar=scale, in1=RT[:, m, c0:c1],
                       op0=ALU.mult, op1=ALU.add)

        ALLH = ["H%d.%d" % (j, g) for j in range(NJ) for g in range(10)]
        ALLXN = ["XN%d.%d" % (c, g) for c in range(8) for g in range(10)]

        def fence(writes):
            op("vector", "memset", [], list(writes) + ["DUMMY"], ap=DUMMY[:, 0:1], constant=0.0)

        def sigmoid_inplace(buf, key, N, r=(), scale_ap=None, bias_ap=None, src=None, src_keys=()):
            kw = {}
            if scale_ap is not None:
                kw = dict(scale=scale_ap, bias=bias_ap)
            else:
                kw = dict(scale=-1.0)
            act(list(src_keys) + [key] + list(r), [key], out=buf, in_=(src if src is not None else buf), func=AF.Exp, **kw)
            act([key, "CONSTS"] + list(r), [key], out=buf, in_=buf, func=AF.Ln, bias=ONEC[0:buf.shape[0], :])
            act([key] + list(r), [key], out=buf, in_=buf, func=AF.Exp, scale=-1.0)

        def qk_sets(ci):
            if ci == 0:
                return dict(QF=SF[0], RS=SF[1], T1=SF[2], T2=SF[3], kQF="SF0", kRS="SF1", kT1="SF2", kT2="SF3",
                            SQb=SB[0], QNb=SB[1], kSQ="SB0", kQN="SB1", b1=6, b2=7)
            return dict(QF=YF[:, 0, :], RS=YF[:, 1, :], T1=YF[:, 2, :], T2=YF[:, 3, :], kQF="YF0", kRS="YF1", kT1="YF2", kT2="YF3",
                        SQb=PT[:, 0, :], QNb=PT[:, 1, :], kSQ="PT0", kQN="PT1", b1=4, b2=5)

        def qk_s1(ch, R):
            z = qk_sets(ch["ci"])
            bi, N, gcol = ch["bank"], ch["N"], ch["gcol"]
            gs = PARAMS[:, gcol:gcol + 1]
            ch["proj"](bi)
            act([pk(bi), "PARAMS"] + R, [z["kQN"]], out=z["QNb"][:, 0:N], in_=pb[bi][:, 0:N], func=AF.Identity, scale=gs)
            act([pk(bi)] + R, [z["kSQ"]], out=z["SQb"][:, 0:N], in_=pb[bi][:, 0:N], func=AF.Square, scale=0.125)
            act([pk(bi), "PARAMS"] + R, [z["kQF"]], out=z["QF"][:, 0:N], in_=pb[bi][:, 0:N], func=AF.Identity, scale=gs)
            mm([z["kQN"], "PROT"] + R, [pk(z["b2"])], pb[z["b2"]][:, 0:N], PROT[:], z["QNb"][:, 0:N])
            mm([z["kSQ"], "ONESBD"] + R, [pk(z["b1"])], pb[z["b1"]][:, 0:N], ONESBD[:], z["SQb"][:, 0:N])

        def qk_s2(ch, R):
            z = qk_sets(ch["ci"])
            N, rc0, dest, dkeys = ch["N"], ch["rc0"], ch["dest"], ch["dkeys"]
            QF, RS, T2, b1, b2 = z["QF"], z["RS"], z["T2"], z["b1"], z["b2"]
            kQF, kRS, kT2 = z["kQF"], z["kRS"], z["kT2"]
            act([pk(b1), "CONSTS"] + R, [kRS], out=RS[:, 0:N], in_=pb[b1][:, 0:N], func=AF.Ln, bias=EPSC)
            act([kRS] + R, [kRS], out=RS[:, 0:N], in_=RS[:, 0:N], func=AF.Exp, scale=-0.5)
            op("vector", "tensor_tensor", [pk(b2), "ST"] + R, [kT2], out=T2[:, 0:N], in0=pb[b2][:, 0:N], in1=ST[:, rc0:rc0 + N],
               op=ALU.mult)
            op("vector", "tensor_tensor", [kQF, "CT"] + R, [kQF], out=QF[:, 0:N], in0=QF[:, 0:N], in1=CT[:, rc0:rc0 + N],
               op=ALU.mult)
            op("vector", "tensor_tensor", [kQF, kT2] + R, [kQF], out=QF[:, 0:N], in0=QF[:, 0:N], in1=T2[:, 0:N], op=ALU.add)
            if isinstance(dest, tuple):
                for hh_, dd in enumerate(dest):
                    op("vector", "tensor_tensor", [kQF, kRS] + R, list(dkeys), out=dd, in0=QF[hh_ * 64:(hh_ + 1) * 64, 0:N],
                       in1=RS[hh_ * 64:(hh_ + 1) * 64, 0:N], op=ALU.mult)
            else:
                op("vector", "tensor_tensor", [kQF, kRS] + R, list(dkeys), out=dest, in0=QF[:, 0:N], in1=RS[:, 0:N], op=ALU.mult)
            if ch.get("dest32") is not None:
                lo, hi, d32 = ch["dest32"]
                op("vector", "tensor_tensor", [kQF, kRS] + R, list(ch["d32keys"]), out=d32, in0=QF[:, lo:hi], in1=RS[:, lo:hi],
                   op=ALU.mult)

        def qk_run(chains, R):
            for i, ch in enumerate(chains):
                ch["ci"] = i % 2
                ch["bank"] = i % 4
            for i in range(len(chains) + 1):
                if i < len(chains):
                    qk_s1(chains[i], R)
                if i >= 1:
                    qk_s2(chains[i - 1], R)

        def ln_stats(sub, R):
            N = sub["N"]
            M1, MSQ, VR = SF[0], SF[1], SF[2]
            op("vector", "tensor_scalar", [pk(4)] + R, ["SF0"], out=M1[:, 0:N], in0=pb[4][:, 0:N], scalar1=1.0 / 512, scalar2=None,
               op0=ALU.mult)
            op("vector", "tensor_tensor", ["SF0"] + R, ["SF1"], out=MSQ[:, 0:N], in0=M1[:, 0:N], in1=M1[:, 0:N], op=ALU.mult)
            op("vector", "scalar_tensor_tensor", [pk(5), "SF1"] + R, ["SF2"], out=VR[:, 0:N], in0=pb[5][:, 0:N], scalar=1.0 / 512,
               in1=MSQ[:, 0:N], op0=ALU.mult, op1=ALU.subtract)
            act(["SF2", "CONSTS"] + R, ["SF2"], out=VR[:, 0:N], in_=VR[:, 0:N], func=AF.Ln, bias=EPSC)
            act(["SF2"] + R, ["SF2"], out=VR[:, 0:N], in_=VR[:, 0:N], func=AF.Exp, scale=-0.5)

        def ln_chunk(sub, c, R, RX):
            N, h, mixc0 = sub["N"], sub["h"], sub["mixc0"]
            M1, VR, E = SF[0], SF[2], SF[3]
            yv = YF[:, c, h * 256:h * 256 + N]
            yk = "YF%d.%d" % (c, h)
            op("vector", "tensor_tensor", [yk, "SF0"] + R, [yk], out=yv, in0=yv, in1=M1[:, 0:N], op=ALU.subtract)
            op("vector", "scalar_tensor_tensor", [yk, "SF2", "PARAMS"] + R, [yk], out=yv, in0=yv, scalar=PARAMS[:, PGCN + c:PGCN + c + 1],
               in1=VR[:, 0:N], op0=ALU.mult, op1=ALU.mult)
            sigmoid_inplace(E[:, 0:N], "SF3", N, r=R + ["PARAMS"], scale_ap=-1.0,
                            bias_ap=PARAMS[:, PNB + c:PNB + c + 1], src=yv, src_keys=[yk])
            op("vector", "scalar_tensor_tensor", [yk, "SF3", "PARAMS"] + R + RX,
               ["MIX%d.%d" % (c, g) for g in range(mixc0 // 128, (mixc0 + N - 1) // 128 + 1)],
               out=MIX[:, c, mixc0:mixc0 + N], in0=yv, scalar=PARAMS[:, PBCN + c:PBCN + c + 1], in1=E[:, 0:N],
               op0=ALU.add, op1=ALU.mult)

        conv_pending = []

        def conv_flush():
            for f in conv_pending:
                f()
            del conv_pending[:]

        def conv_chunk(sub, c, R):
            N, h, rhs_fn, ukeys = sub["N"], sub["h"], sub["rhs_fn"], sub["ukeys"]
            i = bcnt["ab"]
            bcnt["ab"] += 1
            yb = i % 2
            for t in range(31):
                mm(["DIAG"] + ukeys(c) + R, [pk(yb)], pb[yb][:, 0:N], DIAG[:, c, t, :], rhs_fn(c, t), start=(t == 0), stop=(t == 30))
            conv_flush()
            yv = YF[:, c, h * 256:h * 256 + N]
            yk = "YF%d.%d" % (c, h)
            act([pk(yb), "PARAMS", "YF%d" % c] + R, [yk], out=yv, in_=pb[yb][:, 0:N], func=AF.Identity,
                bias=PARAMS[:, PBDW + c:PBDW + c + 1])
            op("vector", "tensor_copy", [yk] + R, ["SB%d" % yb], out=SB[yb][:, 0:N], in_=yv)
            act([yk] + R, ["SB2"], out=SB[2][:, 0:N], in_=yv, func=AF.Square)
            def stats(yb=yb, c=c, N=N):
                mm(["SB%d" % yb, "ONES"] + R, [pk(4)], pb[4][:, 0:N], ONES[:], SB[yb][:, 0:N], start=(c == 0), stop=(c == 3))
                mm(["SB2", "ONES"] + R, [pk(5)], pb[5][:, 0:N], ONES[:], SB[2][:, 0:N], start=(c == 0), stop=(c == 3))
            conv_pending.append(stats)

        def conv_run(subs, R, RX, tail=None):
            for i in range(len(subs) + 1):
                cur = subs[i] if i < len(subs) else None
                prev = subs[i - 1] if i >= 1 else None
                if prev is not None:
                    conv_flush()
                    ln_stats(prev, R)
                for c in range(4):
                    if cur is not None:
                        conv_chunk(cur, c, R)
                    elif tail is not None:
                        tail(c, "pre")
                    if prev is not None:
                        ln_chunk(prev, c, R, RX)
                    if cur is None and tail is not None:
                        tail(c, "post")

        def mixer(t):
            first, last = (t == 0), (t == NT - 1)
            main_blocks = [(0, 512), (512, 1024)]
            all_blocks = main_blocks + ([(HALO0, SAMP0), (SAMP0, TC)] if first else [])
            R = ["HF"]
            RX = ["XF"]
            fence(ALLH + ["HF"])
            dma("gpsimd", "rope0", [], ["CT"], CT[:, 0:TM], rope[0, :, t * TM:(t + 1) * TM])
            dma("gpsimd", "rope1", [], ["ST"], ST[:, 0:TM], rope[1, :, t * TM:(t + 1) * TM])
            if first:
                dma("gpsimd", "rope0", [], ["CT"], CT[:, TM:TC], rope[0, :, HALF:NROWS])
                dma("gpsimd", "rope1", [], ["ST"], ST[:, TM:TC], rope[1, :, HALF:NROWS])
            norm([(0, 512), (512, 1024)] + ([(HALO0, TC)] if first else []), PGM)
            op("vector", "memset", R, ["VP"], ap=VP[:].rearrange("p k g n -> p (k g n)"), constant=0.0)
            op("vector", "memset", R, ["KT.%d" % g for g in range(9)], ap=KT[:].rearrange("p g n -> p (g n)"), constant=0.0)
            if not first:
                op("vector", "tensor_copy", ["UCAR"] + R, ["Upre"], out=U[:, :, 0:32], in_=UCAR[:])
                op("vector", "tensor_copy", ["KCAR"] + R, ["KT.0"], out=KT[:, :, 0:128], in_=KCAR[:])
                op("vector", "tensor_copy", ["VCAR"] + R, ["VP"], out=VP[:, 0, :, :], in_=VCAR[:])
            for i in range(4):
                s = ring_load(W_IN + i)
                W = ring[s][:, :].rearrange("p (k n) -> p k n", k=8)
                rk = "ring%d" % s
                for (c0, c1) in all_blocks:
                    N = c1 - c0
                    k = bcnt["ab"]
                    bcnt["ab"] += 1
                    a, b = k % 2, 2 + k % 2
                    for kc in range(8):
                        mm([rk] + gk("XN%d" % kc, c0, c1) + R, [pk(a)], pb[a][:, 0:N], W[:, kc, 0:128], XN[:, kc, c0:c1],
                           start=(kc == 0), stop=(kc == 7))
                    for kc in range(8):
                        mm([rk] + gk("XN%d" % kc, c0, c1) + R, [pk(b)], pb[b][:, 0:N], W[:, kc, 128:256], XN[:, kc, c0:c1],
                           start=(kc == 0), stop=(kc == 7))
                    E = SF[3]
                    sigmoid_inplace(E[:, 0:N], "SF3", N, r=R, src=pb[b][:, 0:N], src_keys=[pk(b)])
                    if c0 < 1024:
                        op("vector", "tensor_tensor", [pk(a), "SF3"] + R, ["U%d.%d" % (i, g) for g in range(c0 // 128, c1 // 128)],
                           out=U[:, i, 32 + c0:32 + c1], in0=pb[a][:, 0:N], in1=E[:, 0:N], op=ALU.mult)
                        if last and c1 == 1024:
                            op("vector", "tensor_tensor", [pk(a), "SF3"] + R, ["U32M"], out=U32M[:, i, :], in0=pb[a][:, N - 32:N],
                               in1=E[:, N - 32:N], op=ALU.mult)
                    elif c0 == HALO0:
                        op("vector", "tensor_tensor", [pk(a), "SF3"] + R, ["Upre"], out=U[:, i, 0:32], in0=pb[a][:, 96:128],
                           in1=E[:, 96:128], op=ALU.mult)
                    else:
                        op("vector", "tensor_tensor", [pk(a), "SF3"] + R, ["USAMP"], out=USAMP[:, i, 0:4, 30:46],
                           in0=pb[a][:, 0:64].rearrange("p (s n) -> p s n", s=4), in1=E[:, 0:64].rearrange("p (s n) -> p s n", s=4),
                           op=ALU.mult)
                        op("vector", "tensor_tensor", [pk(a), "SF3"] + R, ["USAMP32"], out=USAMP32[:, i, :], in0=pb[a][:, 0:64],
                           in1=E[:, 0:64], op=ALU.mult)
            if MIXL < 2:
                fence(ALLXN + ["XF"]); fence(ALLH + ["HF"]); return
            chains = []

            def mkproj(rk, W, col0, c0, c1):
                def proj(bi):
                    for kc in range(8):
                        mm([rk] + gk("XN%d" % kc, c0, c1) + R, [pk(bi)], pb[bi][:, 0:c1 - c0], W[:, kc, col0:col0 + 128],
                           XN[:, kc, c0:c1], start=(kc == 0), stop=(kc == 7))
                return proj

            for i in range(2):
                s = ring_load(W_IN + 4 + i)
                W = ring[s][:, :].rearrange("p (k n) -> p k n", k=8)
                rk = "ring%d" % s
                for (c0, c1) in main_blocks + ([(SAMP0, TC)] if first else []):
                    N = c1 - c0
                    qc0 = c0 if c0 < 1024 else 1024
                    for hh in range(2):
                        j = 2 * i + hh
                        chains.append(dict(proj=mkproj(rk, W, hh * 128, c0, c1), N=N, gcol=PGQ, rc0=c0, dest=Q[:, j, qc0:qc0 + N],
                                           dkeys=["Q%d.%d" % (j, g) for g in range(qc0 // 128, (qc0 + N - 1) // 128 + 1)]))
            s = ring_load(W_IN + 6)
            W = ring[s][:, :].rearrange("p (k n) -> p k n", k=8)
            rk = "ring%d" % s
            for (c0, c1) in all_blocks:
                N = c1 - c0
                ch = dict(proj=mkproj(rk, W, 0, c0, c1), N=N, gcol=PGK, rc0=c0)
                if c0 < 1024:
                    ch.update(dest=(KT[0:64, 0, 128 + c0:128 + c1], KT[64:128, 1, 128 + c0:128 + c1]),
                              dkeys=["KT.%d" % g for g in range(1 + c0 // 128, 1 + c1 // 128)],
                              dest32=((N - 128, N, K32[:, 0:128]) if (last and c1 == 1024) else None), d32keys=["K32a"])
                elif c0 == HALO0:
                    ch.update(dest=(KT[0:64, 0, 0:128], KT[64:128, 1, 0:128]), dkeys=["KT.0"])
                else:
                    ch.update(dest=(KST[0:64, 0, 64:128], KST[64:128, 1, 64:128]), dkeys=["KST"], dest32=(0, 64, K32[:, 128:192]),
                              d32keys=["K32b"])
                chains.append(ch)
            qk_run(chains, R)
            vgroups = [(g * 128, g + 1) for g in range(8)] + ([(HALO0, 0)] if first else [])
            for (c0, kg) in ([] if "nov" in SKIP else vgroups):
                k = bcnt["ab"]
                bcnt["ab"] += 1
                a = k % 4
                for kc in range(8):
                    mm([rk] + gk("XN%d" % kc, c0, c0 + 128) + R, [pk(a)], pb[a][:, 0:128], XN[:, kc, c0:c0 + 128], W[:, kc, 128:256],
                       start=(kc == 0), stop=(kc == 7))
                pv = pb[a][:, 0:128].rearrange("p (g n) -> p g n", g=2)
                act([pk(a)] + R, ["VP"], out=VP[:, kg, :, 0:64], in_=pv, func=AF.Copy)
                op("vector", "tensor_copy", [pk(a)] + R, ["VP"], out=VP[:, kg, :, 128:192], in_=pv)
                if last and kg == 8:
                    op("vector", "tensor_copy", [pk(a)] + R, ["V32L"], out=V32L[:], in_=pb[a][:, 0:128])
            if first and "novs" not in SKIP:
                k = bcnt["ab"]
                bcnt["ab"] += 1
                a = k % 4
                for kc in range(8):
                    mm([rk] + gk("XN%d" % kc, 1088, TC) + R, [pk(a)], pb[a][:, 0:128], XN[:, kc, 1088:TC],
                       W[:, kc, 128:256], start=(kc == 0), stop=(kc == 7))
                pv = pb[a][64:128, 0:128].rearrange("p (g n) -> p g n", g=2)
                if "vsA" not in SKIP:
                    act([pk(a)] + R, ["VS"], out=VS[64:128, :, 0:64], in_=pv, func=AF.Copy)
                if "vsB" not in SKIP:
                    op("vector", "tensor_copy", [pk(a)] + R, ["VS"], out=VS[64:128, :, 128:192], in_=pv)
                if "vsC" not in SKIP:
                    op("vector", "tensor_copy", [pk(a)] + R, ["V32S"], out=V32S[64:128, :], in_=pb[a][64:128, 0:128])
            if MIXL < 4:
                fence(ALLXN + ["XF"]); fence(ALLH + ["HF"]); return
            fence(ALLXN + ["XF"])
            if not last:
                op("vector", "tensor_copy", ["U%d.7" % c for c in range(4)] + R, ["UCAR"], out=UCAR[:], in_=U[:, :, 1024:1056])
                op("vector", "tensor_copy", ["KT.8"] + R, ["KCAR"], out=KCAR[:], in_=KT[:, :, 1024:1152])
                op("vector", "tensor_copy", ["VP"] + R, ["VCAR"], out=VCAR[:], in_=VP[:, 8, :, :])

            def att_scores(G):
                PTb, pkey = (PT, "PT%d") if G % 2 == 0 else (PT2, "PT2.%d")
                for kgi in range(2):
                    kg = G + kgi
                    for g in range(2):
                        bi = kgi * 2 + g
                        mm(["KT.%d" % kg] + ["Q%d.%d" % (j, G) for j in range(4)] + R, [pk(bi)], pb[bi][:, :],
                           KT[:, g, kg * 128:(kg + 1) * 128], Q[:, :, G * 128:(G + 1) * 128])
                        part = (0, 64) if kgi == 0 else (64, 128)
                        qmask = 1 if kgi == 0 else 0
                        act([pk(bi)] + R, [pkey % bi], out=PTb[:, bi, :], in_=pb[bi][:, :], func=AF.Exp, scale=0.125)
                        op("vector", "memset", R, [pkey % bi],
                           ap=PTb[part[0]:part[1], bi, :].rearrange("p (j q n) -> p j q n", j=4, q=2)[:, :, qmask, :], constant=0.0)

            def att_pv(G):
                PTb, pkey = (PT, "PT%d") if G % 2 == 0 else (PT2, "PT2.%d")
                bn, bd = (6, 7) if G % 2 == 0 else (4, 5)
                n_ = 0
                for g in range(2):
                    for kgi in range(2):
                        kg = G + kgi
                        bi = kgi * 2 + g
                        sel = slice(0, 128) if g == 0 else slice(64, 192)
                        rhs = PTb[:, bi, :]
                        mm(["VP", pkey % bi] + R, [pk(bn)], pb[bn][:, :], VP[:, kg, g, sel], rhs, start=(n_ == 0), stop=(n_ == 3))
                        ow = OPAD[:, (1 if (first and kg == 0) else 0), sel]
                        mm(["OPAD", pkey % bi] + R, [pk(bd)], pb[bd][:, :], ow, rhs, start=(n_ == 0), stop=(n_ == 3))
                        n_ += 1

            def att_norm(G):
                bn, bd = (6, 7) if G % 2 == 0 else (4, 5)
                Rr, rkey = (SF[1], "SF1") if G % 2 == 0 else (SF[0], "SF0")
                for j in range(4):
                    act([pk(bd), "ESINK"] + R, [rkey], out=Rr[:, j * 128:(j + 1) * 128], in_=pb[bd][:, j * 128:(j + 1) * 128],
                        func=AF.Ln, bias=ESINK[:, j:j + 1])
                act([rkey] + R, [rkey], out=Rr[:, :], in_=Rr[:, :], func=AF.Exp, scale=-1.0)
                op("vector", "tensor_tensor", [pk(bn), rkey] + R + RX, ["MIX%d.%d" % (4 + j, G) for j in range(4)],
                   out=MIX[:, 4:8, G * 128:(G + 1) * 128], in0=pb[bn][:, :].rearrange("p (j n) -> p j n", j=4),
                   in1=Rr[:, :].rearrange("p (j n) -> p j n", j=4), op=ALU.mult)

            subs = []
            for c0 in range(0, 1024, 256):
                subs.append(dict(N=256, h=len(subs) % 2, mixc0=c0,
                                 rhs_fn=(lambda c, tt, c0=c0: U[:, c, c0 + 2 + tt:c0 + 2 + tt + 256]),
                                 ukeys=(lambda c, c0=c0: ["Upre"] + ["U%d.%d" % (c, g) for g in range(max(0, c0 // 128 - 1), (c0 + 256) // 128)])))
            if first:
                subs.append(dict(N=128, h=len(subs) % 2, mixc0=1024, rhs_fn=(lambda c, tt: USAMP[:, c, :, tt:tt + 16]),
                                 ukeys=(lambda c: ["USAMP"])))
            op("vector", "memset", R, ["PT%d" % i for i in range(4)], ap=PT[:].rearrange("p c n -> p (c n)"), constant=0.0)

            def att_tail(c, when):
                if c == 0 and when == "pre":
                    att_scores(0)
                elif c == 1 and when == "post":
                    op("vector", "memset", R, ["YF0", "YF1"] + ["YF%d.%d" % (i, hh) for i in range(2) for hh in range(2)] + ["PT2.%d" % i for i in range(4)],
                       ap=PT2[:].rearrange("p c n -> p (c n)"), constant=0.0)
                    att_scores(1)
                elif c == 2 and when == "pre":
                    att_pv(0)

            conv_run(subs, R, RX, tail=att_tail)
            if MIXL < 5:
                fence(ALLXN + ["XF"]); fence(ALLH + ["HF"]); return
            for G in range(1, 8):
                if G + 1 < 8:
                    att_scores(G + 1)
                att_pv(G)
                att_norm(G - 1)
            att_norm(7)
            if MIXL < 6:
                fence(ALLXN + ["XF"]); fence(ALLH + ["HF"]); return
            if first:
                for s4 in range(4):
                    for g in range(2):
                        oc = (s4 * 2 + g) * 64
                        qv = Q[:, :, 1024 + s4 * 16:1024 + s4 * 16 + 16]
                        mm(["KCT"] + ["Q%d.8" % j for j in range(4)] + R, [pk(0)], pb[0][:, oc:oc + 64], KCT[:, g, s4, :], qv)
                for g in range(2):
                    mm(["KST"] + ["Q%d.8" % j for j in range(4)] + R, [pk(1)], pb[1][:, g * 256:(g + 1) * 256],
                       KST[:, g, 0:128], Q[:, :, 1024:1088])
                act([pk(0)] + R, ["PT0"], out=PT[:, 0, :], in_=pb[0][:, :], func=AF.Exp, scale=0.125)
                op("vector", "memset", R, ["PT1"], ap=PT[0:64, 1, :], constant=0.0)
                act([pk(1)] + R, ["PT1"], out=PT[64:128, 1, :], in_=pb[1][64:128, :], func=AF.Exp, scale=0.125)
                for g in range(2):
                    op("vector", "tensor_tensor", ["PT1", "SMASK"] + R, ["PT1"], out=PT[64:128, 1, g * 256:(g + 1) * 256],
                       in0=PT[64:128, 1, g * 256:(g + 1) * 256], in1=SMASK[64:128, :], op=ALU.mult)
                for j in range(4):
                    for s4 in range(4):
                        ocol = slice(j * 64 + s4 * 16, j * 64 + s4 * 16 + 16)
                        for g in range(2):
                            sel = slice(0, 128) if g == 0 else slice(64, 192)
                            pc = (s4 * 2 + g) * 64 + j * 16
                            mm(["VC", "PT0"] + R, [pk(6)], pb[6][:, ocol], VC[:, s4, g, sel], PT[:, 0, pc:pc + 16], start=(g == 0), stop=(g == 1))
                            mm(["OPAD", "PT0"] + R, [pk(7)], pb[7][:, ocol], OPAD[:, 0, sel], PT[:, 0, pc:pc + 16], start=(g == 0), stop=(g == 1))
                    for g in range(2):
                        sel = slice(0, 128) if g == 0 else slice(64, 192)
                        pc = g * 256 + j * 64
                        mm(["VS", "PT1"] + R, [pk(4)], pb[4][:, j * 64:(j + 1) * 64], VS[:, g, sel], PT[:, 1, pc:pc + 64], start=(g == 0), stop=(g == 1))
                        mm(["OPAD", "PT1"] + R, [pk(5)], pb[5][:, j * 64:(j + 1) * 64], OPAD[:, 0, sel], PT[:, 1, pc:pc + 64], start=(g == 0), stop=(g == 1))
                Rr, T1, NS, T3 = SF[0], SF[1], SF[2], SF[3]
                op("vector", "tensor_copy", [pk(4)] + R, ["SF1"], out=T1[:, 0:256], in_=pb[4][:, 0:256])
                op("vector", "tensor_tensor", [pk(6), "SF1"] + R, ["SF2"], out=NS[:, 0:256], in0=pb[6][:, 0:256], in1=T1[:, 0:256], op=ALU.add)
                op("vector", "tensor_copy", [pk(5)] + R, ["SF3"], out=T3[:, 0:256], in_=pb[5][:, 0:256])
                op("vector", "tensor_tensor", [pk(7), "SF3"] + R, ["SF0"], out=Rr[:, 0:256], in0=pb[7][:, 0:256], in1=T3[:, 0:256], op=ALU.add)
                for j in range(4):
                    op("vector", "tensor_scalar", ["SF0", "ESINK"] + R, ["SF0"], out=Rr[:, j * 64:(j + 1) * 64],
                       in0=Rr[:, j * 64:(j + 1) * 64], scalar1=ESINK[:, j:j + 1], scalar2=None, op0=ALU.add)
                act(["SF0"] + R, ["SF0"], out=Rr[:, 0:256], in_=Rr[:, 0:256], func=AF.Ln)
                act(["SF0"] + R, ["SF0"], out=Rr[:, 0:256], in_=Rr[:, 0:256], func=AF.Exp, scale=-1.0)
                op("vector", "tensor_tensor", ["SF2", "SF0"] + R + RX, ["MIX%d.8" % (4 + j) for j in range(4)],
                   out=MIX[:, 4:8, 1024:1088], in0=NS[:, 0:256].rearrange("p (j n) -> p j n", j=4),
                   in1=Rr[:, 0:256].rearrange("p (j n) -> p j n", j=4), op=ALU.mult)
            if MIXL < 7:
                fence(ALLXN + ["XF"]); fence(ALLH + ["HF"]); return
            for n in range(4):
                s = ring_load(W_OUT + n)
                W = ring[s][:, :].rearrange("p (k n) -> p k n", k=8)
                rk = "ring%d" % s
                for (c0, c1) in main_blocks + ([(SAMP0, TC)] if first else []):
                    N = c1 - c0
                    mc0 = c0 if c0 < 1024 else 1024
                    for mi in range(2):
                        m = 2 * n + mi
                        i = bcnt["d"]
                        bcnt["d"] += 1
                        bi = 4 + i % 4
                        for kc in range(8):
                            mm([rk] + ["MIX%d.%d" % (kc, g) for g in range(mc0 // 128, (mc0 + N - 1) // 128 + 1)] + R + RX, [pk(bi)],
                               pb[bi][:, 0:N], W[:, kc, mi * 128:(mi + 1) * 128], MIX[:, kc, mc0:mc0 + N], start=(kc == 0), stop=(kc == 7))
                        op("vector", "tensor_tensor", [pk(bi)] + gk("RT%d" % m, c0, c1), gk("RT%d" % m, c0, c1),
                           out=RT[:, m, c0:c1], in0=pb[bi][:, 0:N], in1=RT[:, m, c0:c1], op=ALU.add)
            if MIXL < 8:
                fence(ALLXN + ["XF"]); fence(ALLH + ["HF"]); return
            if first:
                for c in range(4):
                    tr(["USAMP32", "CM"] + R, [pk(0)], pb[0][0:64, c * 128:(c + 1) * 128], USAMP32[:, c, :], IDF)
                op("vector", "tensor_copy", [pk(0)], ["OST"], out=OST[0:64, :], in_=pb[0][0:64, :])
                for s4 in range(4):
                    dma("sync", "o0", ["OST"], [], ocs_s[s4, 14:30, :], OST[s4 * 16:(s4 + 1) * 16, :])
                tr(["K32b", "CM"], [pk(1)], pb[1][0:64, 0:128], K32[:, 128:192], IDF)
                op("vector", "tensor_copy", [pk(1)], ["V32L"], out=V32L[0:64, :], in_=pb[1][0:64, 0:128])
                for s4 in range(4):
                    dma("sync", "o1", ["V32L"], [], okw_s[s4, 112:128, :], V32L[s4 * 16:(s4 + 1) * 16, :])
                    dma("sync", "o2", ["V32S"], [], ovw_s[s4, 112:128, :], V32S[64 + s4 * 16:64 + (s4 + 1) * 16, :])
            if last:
                for c in range(4):
                    tr(["U32M", "CM"], [pk(0)], pb[0][0:32, c * 128:(c + 1) * 128], U32M[:, c, :], IDF)
                op("vector", "tensor_copy", [pk(0)], ["OST"], out=OST[0:32, :], in_=pb[0][0:32, :])
                dma("sync", "o0", ["OST"], [], ocs_p[:, :], OST[2:32, :])
                tr(["K32a", "CM"], [pk(1)], pb[1][:, 0:128], K32[:, 0:128], IDF)
                op("vector", "tensor_copy", [pk(1)], ["USAMP32"], out=USAMP32[:, 0:2, :].rearrange("p a n -> p (a n)"), in_=pb[1][:, 0:128])
                dma("sync", "o1", ["USAMP32"], [], okw_p[:, :], USAMP32[:, 0:2, :].rearrange("p a n -> p (a n)"))
                dma("sync", "o2", ["V32L"], [], ovw_p[:, :], V32L[:, :])
            fence(ALLXN + ["XF"])
            fence(ALLH + ["HF"])

        for t in range(NTILES):
            first = (t == 0)
            if first:
                hs = [load_dma(0, 128), load_dma(128, 128)]
                for g in range(8):
                    load_compute(hs[g], 128, g * 128)
                    if g + 2 < 8:
                        hs.append(load_dma((g + 2) * 128, 128))
            if first:
                load_group(HALF, 128, HALO0)
                load_group(HALF + 128, 64, SAMP0)
            if first and "pass" not in SKIP:
                for s in range(4):
                    dma("sync", "o0", [], [], ocs_s[s, 0:14, :], sconv[s, 16:30, :])
                    dma("sync", "o1", [], [], okw_s[s, 0:112, :], ck[s, 16:128, :])
                    dma("sync", "o2", [], [], ovw_s[s, 0:112, :], cv[s, 16:128, :])
            b1 = [(0, 512), (512, 1024)] + ([(HALO0, TC)] if first else [])
            b2 = [(0, 512), (512, 1024)] + ([(SAMP0, TC)] if first else [])
            if LEVEL >= 1:
                norm(b1, PG1)
            if LEVEL >= 2:
                ff(b1, W_FF1U, W_FF1D, 0.5, hook=((lambda: diag_some(2)) if first else None))
            if first:
                late_setup()
            if LEVEL >= 3:
                mixer(t)
            if LEVEL >= 4:
                norm(b2, PG2)
                ff(b2, W_FF2U, W_FF2D, 0.5)
            if first:
                store_group(HALF, 64, SAMP0)
            nxt = t + 1 < NTILES
            hs = [load_dma((t + 1) * TM, 128), load_dma((t + 1) * TM + 128, 128)] if nxt else []
            for g in range(8):
                store_group(t * TM + g * 128, 128, g * 128)
                if nxt:
                    load_compute(hs[g], 128, g * 128)
                    if g + 2 < 8:
                        hs.append(load_dma((t + 1) * TM + (g + 2) * 128, 128))
        print("sbuf remaining", nc.sbuf_bytes_remaining, "ops", len(P.ops))
        counts = P.emit(nc, st)
        print(counts)
    return nc, counts


_CACHE = {}


def _pack_weights(w_ff1_in, w_ff1_out, w_in, w_out, w_ff2_in, w_ff2_out):
    wall = np.zeros((NCHUNK, 128, 2048), np.float32)

    def up(w, base):
        w4 = w.reshape(8, 128, 2, NJ, 128)
        wall[base:base + NJ] = w4.transpose(3, 1, 0, 2, 4).reshape(NJ, 128, 2048)

    def down(w, base):
        w4 = w.reshape(NJ, 128, 8, 128)
        for m in range(8):
            wall[base + 2 * m] = w4[0:16, :, m, :].transpose(1, 0, 2).reshape(128, 2048)
            wall[base + 2 * m + 1, :, 0:768] = w4[16:22, :, m, :].transpose(1, 0, 2).reshape(128, 768)

    up(w_ff1_in, W_FF1U)
    down(w_ff1_out, W_FF1D)
    up(w_ff2_in, W_FF2U)
    down(w_ff2_out, W_FF2D)
    wi = w_in.reshape(8, 128, 1792)
    for i in range(4):
        blk = np.concatenate([wi[:, :, i * 128:(i + 1) * 128], wi[:, :, 512 + i * 128:512 + (i + 1) * 128]], axis=2)
        wall[W_IN + i] = blk.transpose(1, 0, 2).reshape(128, 2048)
    qcols = []
    for j in range(4):
        qcols += list(range(1024 + j * 64, 1024 + (j + 1) * 64)) + list(range(1024 + (4 + j) * 64, 1024 + (5 + j) * 64))
    qcols = np.array(qcols)
    for i in range(2):
        blk = wi[:, :, qcols[i * 256:(i + 1) * 256]]
        wall[W_IN + 4 + i] = blk.transpose(1, 0, 2).reshape(128, 2048)
    wall[W_IN + 6] = wi[:, :, 1536:1792].transpose(1, 0, 2).reshape(128, 2048)
    rows = list(range(512))
    for j in range(4):
        rows += list(range(512 + j * 64, 512 + (j + 1) * 64)) + list(range(512 + (4 + j) * 64, 512 + (5 + j) * 64))
    wo = w_out[np.array(rows)].reshape(8, 128, 1024)
    for n in range(4):
        wall[W_OUT + n] = wo[:, :, n * 256:(n + 1) * 256].transpose(1, 0, 2).reshape(128, 2048)
    return wall


def _rope_tables(base):
    half = 8
    inv = np.power(np.float32(500000.0), -np.arange(half, dtype=np.float32) / np.float32(half)).astype(np.float32)
    pos = np.concatenate([base + np.arange(HALF), base - 128 + np.arange(128), 4096 + (np.arange(64) % 16)]).astype(np.float32)
    ang = pos[None, :] * inv[:, None]
    cos = np.cos(ang).astype(np.float32)
    sin = np.sin(ang).astype(np.float32)
    tab = np.zeros((2, 128, NROWS), np.float32)
    tab[0] = 1.0
    for h in range(2):
        tab[0, h * 64:h * 64 + 8] = cos
        tab[0, h * 64 + 8:h * 64 + 16] = cos
        tab[1, h * 64:h * 64 + 8] = sin
        tab[1, h * 64 + 8:h * 64 + 16] = sin
    return tab


def kernel(x_prompt, x_sample, state_conv, cache_k_win, cache_v_win, g_ff1, w_ff1_in, w_ff1_out, g_mix, w_in, g_q, g_k,
           sinks, w_dw, b_dw, g_cn, b_cn, w_out, g_ff2, w_ff2_in, w_ff2_out):
    f = lambda a: np.ascontiguousarray(np.asarray(a, dtype=np.float32))
    x_prompt, x_sample, state_conv, cache_k_win, cache_v_win = map(f, (x_prompt, x_sample, state_conv, cache_k_win, cache_v_win))
    if "nc" not in _CACHE:
        _CACHE["nc"] = build_nc()[0]
    nc = _CACHE["nc"]
    wall = _pack_weights(f(w_ff1_in)[0], f(w_ff1_out)[0], f(w_in)[0], f(w_out)[0], f(w_ff2_in)[0], f(w_ff2_out)[0])
    gq, gk_ = f(g_q)[0], f(g_k)[0]
    setup_rows = np.concatenate([f(g_ff1)[0].reshape(8, 128), f(g_mix)[0].reshape(8, 128), f(g_ff2)[0].reshape(8, 128),
                                 f(b_dw)[0].reshape(4, 128), f(g_cn)[0].reshape(4, 128), f(b_cn)[0].reshape(4, 128),
                                 np.concatenate([gq, gq])[None], np.concatenate([gk_, gk_])[None]], axis=0)
    cmat = np.zeros((4, 128, 128), np.float32)
    cmat[0] = np.eye(128)
    cmat[1] = 1.0
    cmat[2, 0:64, 0:64] = 1.0
    cmat[2, 64:128, 64:128] = 1.0
    for h in range(2):
        for m in range(8):
            cmat[3, h * 64 + m + 8, h * 64 + m] = -1.0
            cmat[3, h * 64 + m, h * 64 + m + 8] = 1.0
    opad1 = np.zeros((128, 192), np.float32)
    opad1[:, 0:64] = 1.0
    opad1[:, 128:192] = 1.0
    smask = np.zeros((64, 256), np.float32)
    for s_ in range(4):
        for j in range(4):
            smask[s_ * 16:(s_ + 1) * 16, j * 64 + s_ * 16:j * 64 + (s_ + 1) * 16] = 1.0
    in_maps = []
    for c in range(8):
        b, hf = c // 2, c % 2
        base = hf * HALF
        xin = np.zeros((NROWS, D), np.float32)
        xin[0:HALF] = x_prompt[b, base:base + HALF]
        if hf == 1:
            xin[HALF:HALF + 128] = x_prompt[b, base - 128:base]
        xin[HALF + 128:] = x_sample[4 * c:4 * c + 4].reshape(64, D)
        opad = np.stack([opad1, opad1 * np.float32(hf)])
        in_maps.append(dict(
            xin=xin, wall=wall, rope=_rope_tables(base), setup_rows=setup_rows, wdw=f(w_dw)[0], sinks=f(sinks),
            cmat=cmat, opad=opad, smask=smask, sconv=state_conv[0, 4 * c:4 * c + 4], ck=cache_k_win[0, 4 * c:4 * c + 4].reshape(4, 128, 128),
            cv=cache_v_win[0, 4 * c:4 * c + 4].reshape(4, 128, 128)))
    res = run_bass_kernel_spmd(nc, in_maps, core_ids=list(range(8)))
    r = res.results
    y_prompt = np.zeros((4, SEQ, D), np.float32)
    y_sample = np.zeros((32, 16, D), np.float32)
    csp = np.zeros((1, 4, 30, 512), np.float32)
    kwp = np.zeros((1, 4, 128, 2, 64), np.float32)
    vwp = np.zeros((1, 4, 128, 2, 64), np.float32)
    css = np.zeros((1, 32, 30, 512), np.float32)
    kws = np.zeros((1, 32, 128, 2, 64), np.float32)
    vws = np.zeros((1, 32, 128, 2, 64), np.float32)
    for c in range(8):
        b, hf = c // 2, c % 2
        y_prompt[b, hf * HALF:(hf + 1) * HALF] = r[c]["yout"][0:HALF]
        y_sample[4 * c:4 * c + 4] = r[c]["yout"][HALF:].reshape(4, 16, D)
        if hf == 1:
            csp[0, b] = r[c]["ocs_p"]
            kwp[0, b] = r[c]["okw_p"].reshape(128, 2, 64)
            vwp[0, b] = r[c]["ovw_p"].reshape(128, 2, 64)
        css[0, 4 * c:4 * c + 4] = r[c]["ocs_s"]
        kws[0, 4 * c:4 * c + 4] = r[c]["okw_s"].reshape(4, 128, 2, 64)
        vws[0, 4 * c:4 * c + 4] = r[c]["ovw_s"].reshape(4, 128, 2, 64)
    return (y_prompt, y_sample, csp, kwp, vwp, css, kws, vws)
```

```python
import numpy as np
from contextlib import ExitStack
import concourse.bass as bass
import concourse.mybir as mybir
from concourse.bass_utils import run_bass_kernel_spmd

F32 = mybir.dt.float32
BF16 = mybir.dt.bfloat16
AF = mybir.ActivationFunctionType
ALU = mybir.AluOpType

ENGS = ("sync", "scalar", "vector", "gpsimd", "tensor")
SEM_MAX = 30000


class _Op:
    __slots__ = ("eng", "fn", "deps", "idx", "dma_sem", "tok", "signal")

    def __init__(self, eng, fn, idx, dma_sem):
        self.eng, self.fn, self.idx, self.dma_sem = eng, fn, idx, dma_sem
        self.deps = set()
        self.tok = None
        self.signal = False


class Prog:
    def __init__(self):
        self.ops = []
        self.last_w = {}
        self.readers = {}
        self.last_dma_on_sem = {}

    def add(self, eng, fn, reads=(), writes=(), dma=None):
        op = _Op(eng, fn, len(self.ops), dma)
        self.ops.append(op)
        for k in reads:
            w = self.last_w.get(k)
            if w is not None:
                op.deps.add(w)
        for k in writes:
            w = self.last_w.get(k)
            if w is not None:
                op.deps.add(w)
            r = self.readers.get(k)
            if r:
                op.deps.update(r)
        for k in reads:
            self.readers.setdefault(k, []).append(op.idx)
        for k in writes:
            self.last_w[k] = op.idx
            self.readers[k] = []
        if dma is not None:
            p = self.last_dma_on_sem.get(dma)
            if p is not None:
                op.deps.add(p)
            self.last_dma_on_sem[dma] = op.idx
        op.deps.discard(op.idx)
        return op

    def emit(self, nc, stack):
        ops = self.ops
        for op in ops:
            keep = set()
            for d in op.deps:
                p = ops[d]
                if p.dma_sem is None and op.dma_sem is None and p.eng == op.eng == "tensor":
                    continue
                keep.add(d)
            best = {}
            for d in keep:
                p = ops[d]
                k = ("d", p.dma_sem) if p.dma_sem is not None else ("e", p.eng)
                if k not in best or best[k] < d:
                    best[k] = d
            op.deps = set(best.values())
            for d in op.deps:
                ops[d].signal = True
        dma_names = sorted({op.dma_sem for op in ops if op.dma_sem is not None})
        dma_sems = {n: stack.enter_context(nc.semaphore("d_" + n)) for n in dma_names}
        dma_cnt = {n: 0 for n in dma_names}
        eng_sems = {e: [] for e in ENGS}
        eng_cnt = {e: 0 for e in ENGS}
        for op in ops:
            if op.dma_sem is not None:
                dma_cnt[op.dma_sem] += 16
                op.tok = (dma_sems[op.dma_sem], dma_cnt[op.dma_sem])
            elif op.signal:
                c = eng_cnt[op.eng]
                k, v = divmod(c, SEM_MAX)
                if k >= len(eng_sems[op.eng]):
                    eng_sems[op.eng].append(stack.enter_context(nc.semaphore("e_%s%d" % (op.eng, k))))
                op.tok = (eng_sems[op.eng][k], v + 1)
                eng_cnt[op.eng] = c + 1
        print('signals', eng_cnt, 'dma', dma_cnt)
        block = stack.enter_context(nc.Block())
        per_eng = {e: [op for op in ops if op.eng == e] for e in ENGS}
        final_dma = {}
        for op in ops:
            if op.dma_sem is not None:
                final_dma[op.dma_sem] = op.tok

        def make(e):
            def body(eng):
                waited = {}
                for op in per_eng[e]:
                    need = {}
                    for d in op.deps:
                        s, v = ops[d].tok
                        if need.get(id(s), (None, 0))[1] < v:
                            need[id(s)] = (s, v)
                    for s, v in need.values():
                        if waited.get(id(s), 0) < v:
                            eng.wait_ge(s, v)
                            waited[id(s)] = v
                    ins = op.fn(eng)
                    if op.tok is not None:
                        ins.then_inc(op.tok[0], 16 if op.dma_sem is not None else 1)
                if e == "sync":
                    for s, v in final_dma.values():
                        if waited.get(id(s), 0) < v:
                            eng.wait_ge(s, v)
                            waited[id(s)] = v
            return body

        block.sync(make("sync"))
        block.scalar(make("scalar"))
        block.vector(make("vector"))
        block.gpsimd(make("gpsimd"))
        block.tensor(make("tensor"))
        return {e: len(per_eng[e]) for e in ENGS}


D = 1024
DFF = 2816
NJ = 22
SEQ = 8192
HALF = 4096
NT = 4
TM = 1024
TC = 1216
HALO0, SAMP0 = 1024, 1152
NROWS = HALF + 128 + 64
EPS = 1e-6
NSLOT = 4
W_FF1U, W_FF1D, W_IN, W_OUT, W_FF2U, W_FF2D, NCHUNK = 0, 22, 38, 45, 49, 71, 87
PG1, PGM, PG2, PBDW, PGCN, PBCN, PGQ, PGK, PNG, PNB = 0, 8, 16, 24, 28, 32, 36, 37, 40, 44


LEVEL = 9
MIXL = 99
SKIP = set()
NTILES = NT


def build_nc():
    nc = bass.Bass("TRN2", target_bir_lowering=False)
    din = lambda n, s: nc.dram_tensor(n, s, F32, kind="ExternalInput").ap()
    dout = lambda n, s: nc.dram_tensor(n, s, F32, kind="ExternalOutput").ap()
    xin = din("xin", [NROWS, D])
    wall = din("wall", [NCHUNK, 128, 2048])
    rope = din("rope", [2, 128, NROWS])
    setup_rows = din("setup_rows", [38, 128])
    wdw = din("wdw", [31, 512])
    sinks = din("sinks", [1, 8])
    cmat = din("cmat", [4, 128, 128])
    opad = din("opad", [2, 128, 192])
    smask = din("smask", [64, 256])
    sconv = din("sconv", [4, 30, 512])
    ck = din("ck", [4, 128, 128])
    cv = din("cv", [4, 128, 128])
    yout = dout("yout", [HALF + 64, D])
    ocs_p = dout("ocs_p", [30, 512])
    okw_p = dout("okw_p", [128, 128])
    ovw_p = dout("ovw_p", [128, 128])
    ocs_s = dout("ocs_s", [4, 30, 512])
    okw_s = dout("okw_s", [4, 128, 128])
    ovw_s = dout("ovw_s", [4, 128, 128])

    P = Prog()
    with ExitStack() as st:
        def sb(name, shape, dt):
            return st.enter_context(nc.sbuf_tensor(name, shape, dt))

        RT = sb("RT", [128, 8, TC], F32)
        XNR = sb("XNR", [128, 8 * TC], BF16)
        HR = sb("HR", [128, NJ * TC], BF16)
        XN = XNR[:, :].rearrange("p (c n) -> p c n", c=8)
        MIX = XNR[:, 0:8 * 1152].rearrange("p (c n) -> p c n", c=8)
        H = HR[:, :].rearrange("p (c n) -> p c n", c=NJ)
        hoff = [0]

        def carve(nelem_bf16):
            a = HR[:, hoff[0]:hoff[0] + nelem_bf16]
            hoff[0] += nelem_bf16
            return a

        U = carve(4 * 1056).rearrange("p (c n) -> p c n", c=4)
        Q = carve(4 * 1088).rearrange("p (c n) -> p c n", c=4)
        KT = carve(2 * 1152).rearrange("p (g n) -> p g n", g=2)
        VP = carve(9 * 2 * 192).rearrange("p (k g n) -> p k g n", k=9, g=2)
        YFraw = carve(2 * 4 * 512)
        YF = YFraw.bitcast(F32).rearrange("p (c n) -> p c n", c=4)
        PT2 = YFraw[:, 0:4 * 512].rearrange("p (c n) -> p c n", c=4)
        PT = carve(4 * 512).rearrange("p (c n) -> p c n", c=4)
        SF = [carve(2 * 512).bitcast(F32) for _ in range(4)]
        SB = [carve(512) for _ in range(3)]
        assert hoff[0] <= NJ * TC, hoff[0]

        ring = [sb("ring%d" % i, [128, 2048], BF16) for i in range(NSLOT)]
        XS = [sb("XS%d" % i, [128, D], F32) for i in range(2)]
        YS = [sb("YS%d" % i, [128, D], F32) for i in range(2)]
        OST = sb("OST", [128, 512], F32)
        DIAG = sb("DIAG", [128, 4, 31, 128], BF16)
        CT = sb("CT", [128, TC], BF16)
        ST = sb("ST", [128, TC], BF16)
        CM = sb("CM", [128, 4, 128], F32)
        IDF = CM[:, 0, :]
        IDB = sb("IDB", [128, 128], BF16)
        ONES = sb("ONES", [128, 128], BF16)
        ONESBD = sb("ONESBD", [128, 128], BF16)
        PROT = sb("PROT", [128, 128], BF16)
        OPAD = sb("OPAD", [128, 2, 192], BF16)
        PARAMS = sb("PARAMS", [128, 48], F32)
        WDW = sb("WDW", [128, 4, 31], F32)
        ESINK = sb("ESINK", [128, 4], F32)
        SKROW = sb("SKROW", [1, 8], F32)
        CONSTS = sb("CONSTS", [128, 2], F32)
        DUMMY = sb("DUMMY", [128, 2], F32)
        SQ = [sb("SQ%d" % i, [128, 512], BF16) for i in range(2)]
        RSTD = sb("RSTD", [128, 512], F32)
        SA = [sb("SA%d" % i, [128, 512], F32) for i in range(2)]
        UCAR = sb("UCAR", [128, 4, 32], BF16)
        KCAR = sb("KCAR", [128, 2, 128], BF16)
        VCAR = sb("VCAR", [128, 2, 192], BF16)
        K32 = sb("K32", [128, 192], F32)
        U32M = sb("U32M", [128, 4, 32], F32)
        V32L = sb("V32L", [128, 128], F32)
        USAMP = sb("USAMP", [128, 4, 8, 46], BF16)
        USAMP32 = sb("USAMP32", [128, 4, 64], F32)
        KCT = sb("KCT", [128, 2, 4, 128], BF16)
        KST = sb("KST", [128, 2, 128], BF16)
        VC = sb("VC", [128, 4, 2, 192], BF16)
        VS = sb("VS", [128, 2, 192], BF16)
        V32S = sb("V32S", [128, 128], F32)
        SMASK = sb("SMASK", [128, 256], BF16)
        pb = [st.enter_context(nc.psum_tensor("pb%d" % i, [128, 512], F32)) for i in range(8)]
        EPSC = CONSTS[:, 0:1]
        ONEC = CONSTS[:, 1:2]

        def op(eng, meth, reads, writes, **kw):
            P.add(eng, lambda e: getattr(e, meth)(**kw), reads, writes)

        def act(reads, writes, **kw):
            op("scalar", "activation", reads, writes, **kw)

        def mm(reads, writes, out, lhsT, rhs, start=True, stop=True):
            P.add("tensor", lambda e: e.matmul(out, lhsT=lhsT, rhs=rhs, start=start, stop=stop), reads, writes)

        def tr(reads, writes, out, in_, identity):
            P.add("tensor", lambda e: e.transpose(out=out, in_=in_, identity=identity), reads, writes)

        def dma(eng, sem, reads, writes, out, in_):
            P.add(eng, lambda e: e.dma_start(out=out, in_=in_), reads, writes, dma=sem)

        def gk(name, c0, c1):
            return ["%s.%d" % (name, g) for g in range(c0 // 128, (c1 - 1) // 128 + 1)]

        def pk(i):
            return "pb%d" % i

        dma("sync", "su0", [], ["CM"], CM[:], cmat.rearrange("k p n -> p k n"))
        SROW = XS[0][0:38, 0:128]
        WROW = XS[1][0:31, 0:512]
        dma("sync", "xs0", [], ["XS0"], SROW, setup_rows[:, :])
        dma("sync", "xs1", [], ["XS1"], WROW, wdw[:, :])
        dma("sync", "su3", [], ["SKROW"], SKROW[:], sinks[:, :])
        if "opad" not in SKIP:
            dma("gpsimd", "su4", [], ["OPAD"], OPAD[:], opad.rearrange("k p n -> p k n"))
        op("vector", "memset", [], ["CONSTS"], ap=CONSTS[:, 0:1], constant=EPS)
        op("vector", "memset", [], ["CONSTS"], ap=CONSTS[:, 1:2], constant=1.0)
        op("vector", "memset", [], ["DUMMY"], ap=DUMMY[:], constant=0.0)
        op("vector", "tensor_copy", ["CM"], ["IDB"], out=IDB[:], in_=CM[:, 0, :])
        op("vector", "tensor_copy", ["CM"], ["ONES"], out=ONES[:], in_=CM[:, 1, :])
        op("vector", "tensor_copy", ["CM"], ["ONESBD"], out=ONESBD[:], in_=CM[:, 2, :])
        op("vector", "tensor_copy", ["CM"], ["PROT"], out=PROT[:], in_=CM[:, 3, :])
        if "params" not in SKIP:
            tr(["XS0", "CM"], [pk(0)], pb[0][:, 0:38], SROW, CM[0:38, 0, 0:38])
            op("vector", "tensor_copy", [pk(0)], ["PARAMS"], out=PARAMS[:, 0:38], in_=pb[0][:, 0:38])
            op("vector", "tensor_scalar", ["PARAMS"], ["PARAMS"], out=PARAMS[:, PNG:PNG + 8], in0=PARAMS[:, PGCN:PGCN + 8],
               scalar1=-1.0, scalar2=None, op0=ALU.mult)
            for c in range(4):
                tr(["XS1", "CM"], [pk(1)], pb[1][:, c * 31:(c + 1) * 31], WROW[:, c * 128:(c + 1) * 128], CM[0:31, 0, 0:31])
            op("vector", "tensor_copy", [pk(1)], ["WDW"], out=WDW[:].rearrange("p c t -> p (c t)"), in_=pb[1][:, 0:124])
        if "esink" not in SKIP:
            act(["SKROW"], ["SKROW"], out=SKROW[:], in_=SKROW[:], func=AF.Exp)
            mm(["SKROW", "CM"], [pk(2)], pb[2][:, 0:8], CM[0:1, 1, :], SKROW[0:1, :])
            op("vector", "tensor_copy", [pk(2)], ["ESINK"], out=ESINK[0:64, :], in_=pb[2][0:64, 0:4])
            op("vector", "tensor_copy", [pk(2)], ["ESINK"], out=ESINK[64:128, :], in_=pb[2][64:128, 4:8])
        diag_todo = [(c, t) for c in range(4) for t in range(31)]

        def diag_some(n):
            for _ in range(n):
                if diag_todo:
                    c, t = diag_todo.pop(0)
                    op("vector", "tensor_scalar", ["WDW", "IDB"], ["DIAG"], out=DIAG[:, c, t, :], in0=IDB[:],
                       scalar1=WDW[:, c, t:t + 1], scalar2=None, op0=ALU.mult)

        def late_setup():
            diag_some(len(diag_todo))
            if "kct" not in SKIP:
                for s in range(4):
                    dma("sync", "xs0", [], ["XS0"], XS[0][:, s * 128:(s + 1) * 128], ck[s])
                for s in range(4):
                    tr(["XS0", "CM"], [pk(3)], pb[3][:, s * 128:(s + 1) * 128], XS[0][:, s * 128:(s + 1) * 128], IDF)
                op("vector", "memset", [], ["KCT"], ap=KCT[:].rearrange("p g s n -> p (g s n)"), constant=0.0)
                act([pk(3)], ["KCT"], out=KCT[0:64, 0, :, :], in_=pb[3][0:64, :].rearrange("p (s n) -> p s n", s=4), func=AF.Copy)
                act([pk(3)], ["KCT"], out=KCT[64:128, 1, :, :], in_=pb[3][64:128, :].rearrange("p (s n) -> p s n", s=4), func=AF.Copy)
            if "vc" not in SKIP:
                op("vector", "memset", [], ["VC"], ap=VC[:].rearrange("p s g n -> p (s g n)"), constant=0.0)
                op("vector", "memset", [], ["VS"], ap=VS[:].rearrange("p g n -> p (g n)"), constant=0.0)
                dma("gpsimd", "su5", [], ["SMASK"], SMASK[64:128, :], smask[:, :])
                op("vector", "memset", [], ["KST"], ap=KST[:].rearrange("p g n -> p (g n)"), constant=0.0)
                for s in range(4):
                    dma("sync", "xs0", [], ["XS0"], XS[0][:, s * 128:(s + 1) * 128], cv[s])
                vsrc = XS[0][:, 0:512].rearrange("p (s g n) -> p s g n", s=4, g=2)
                op("vector", "tensor_copy", ["XS0"], ["VC"], out=VC[:, :, :, 0:64], in_=vsrc)
                op("vector", "tensor_copy", ["XS0"], ["VC"], out=VC[:, :, :, 128:192], in_=vsrc)
            if "usamp" not in SKIP:
                op("vector", "memset", [], ["USAMP"], ap=USAMP[:].rearrange("p c s n -> p (c s n)"), constant=0.0)
                for s in range(4):
                    dma("sync", "xs1", [], ["XS1"], XS[1][0:30, 0:512], sconv[s])
                    for c in range(4):
                        tr(["XS1", "CM"], [pk(4)], pb[4][:, (s * 4 + c) * 30:(s * 4 + c + 1) * 30],
                           XS[1][0:30, c * 128:(c + 1) * 128], CM[0:30, 0, 0:30])
                    op("vector", "tensor_copy", [pk(4)], ["USAMP"], out=USAMP[:, :, s, 0:30],
                       in_=pb[4][:, s * 120:(s + 1) * 120].rearrange("p (c t) -> p c t", c=4))


        rcnt = [0]

        def ring_load(idx, ncols=2048):
            s = rcnt[0] % NSLOT
            rcnt[0] += 1
            dma("gpsimd", "ring%d" % s, [], ["ring%d" % s], ring[s][:, 0:ncols], wall[idx, :, 0:ncols])
            return s

        bcnt = {"ab": 0, "d": 0, "x": 0, "y": 0}

        def load_dma(row0, n):
            i = bcnt["x"]
            bcnt["x"] += 1
            sl = i % 2
            dma("sync", "xs%d" % sl, [], ["XS%d" % sl], XS[sl][0:n, :], xin[row0:row0 + n, :])
            return i

        def load_compute(i, n, col0):
            sl = i % 2
            xk = "XS%d" % sl
            bA, bB = (4, 5) if i % 2 == 0 else (6, 7)
            for c in range(8):
                bank = pb[bA] if c < 4 else pb[bB]
                tr([xk, "CM"], [pk(bA if c < 4 else bB)], bank[:, (c % 4) * 128:(c % 4) * 128 + n],
                   XS[sl][0:n, c * 128:(c + 1) * 128], CM[0:n, 0, 0:n])
            g = col0 // 128
            act([pk(bA)], ["RT%d.%d" % (c, g) for c in range(4)], out=RT[:, 0:4, col0:col0 + n],
                in_=pb[bA][:, :].rearrange("p (c n) -> p c n", c=4)[:, :, 0:n], func=AF.Copy)
            op("vector", "tensor_copy", [pk(bB)], ["RT%d.%d" % (c, g) for c in range(4, 8)],
               out=RT[:, 4:8, col0:col0 + n], in_=pb[bB][:, :].rearrange("p (c n) -> p c n", c=4)[:, :, 0:n])

        def load_group(row0, n, col0):
            load_compute(load_dma(row0, n), n, col0)

        def store_group(row0, n, col0):
            i = bcnt["y"]
            bcnt["y"] += 1
            bA, bB = (0, 1) if i % 2 == 0 else (2, 3)
            g = col0 // 128
            for c in range(8):
                bi = bA if c < 4 else bB
                tr(["RT%d.%d" % (c, g), "CM"], [pk(bi)], pb[bi][0:n, (c % 4) * 128:(c % 4 + 1) * 128],
                   RT[:, c, col0:col0 + n], IDF)
            ys, yk_ = YS[i % 2], "YS%d" % (i % 2)
            act([pk(bA)], [yk_], out=ys[0:n, 0:512], in_=pb[bA][0:n, :], func=AF.Copy)
            op("vector", "tensor_copy", [pk(bB)], [yk_], out=ys[0:n, 512:1024], in_=pb[bB][0:n, :])
            dma("sync", "ys%d" % (i % 2), [yk_], [], yout[row0:row0 + n, :], ys[0:n, :])

        def norm(blocks, gcol, extra_r=(), extra_w=()):
            for (c0, c1) in blocks:
                N = c1 - c0
                for c in range(8):
                    q = SQ[c % 2]
                    act(["RT%d.%d" % (c, g) for g in range(c0 // 128, (c1 - 1) // 128 + 1)], ["SQ%d" % (c % 2)],
                        out=q[:, 0:N], in_=RT[:, c, c0:c1], func=AF.Square, scale=1.0 / 32.0)
                    mm(["SQ%d" % (c % 2), "ONES"], [pk(7)], pb[7][:, 0:N], ONES[:], q[:, 0:N], start=(c == 0), stop=(c == 7))
                act([pk(7), "CONSTS"], ["RSTD"], out=RSTD[:, 0:N], in_=pb[7][:, 0:N], func=AF.Ln, bias=EPSC)
                act(["RSTD"], ["RSTD"], out=RSTD[:, 0:N], in_=RSTD[:, 0:N], func=AF.Exp, scale=-0.5)
                for c in range(8):
                    op("vector", "scalar_tensor_tensor",
                       ["RSTD", "PARAMS"] + gk("RT%d" % c, c0, c1) + list(extra_r),
                       gk("XN%d" % c, c0, c1) + list(extra_w),
                       out=XN[:, c, c0:c1], in0=RT[:, c, c0:c1], scalar=PARAMS[:, gcol + c:gcol + c + 1],
                       in1=RSTD[:, 0:N], op0=ALU.mult, op1=ALU.mult)

        def ff(blocks, wu, wd, scale, hook=None):
            for j in range(NJ):
                s = ring_load(wu + j)
                W = ring[s][:, :].rearrange("p (k n) -> p k n", k=8)
                rk = "ring%d" % s
                for (c0, c1) in blocks:
                    N = c1 - c0
                    i = bcnt["ab"]
                    bcnt["ab"] += 1
                    a, b = i % 2, 2 + i % 2
                    for kc in range(8):
                        mm([rk] + gk("XN%d" % kc, c0, c1), [pk(a)], pb[a][:, 0:N], W[:, kc, 0:128], XN[:, kc, c0:c1],
                           start=(kc == 0), stop=(kc == 7))
                    for kc in range(8):
                        mm([rk] + gk("XN%d" % kc, c0, c1), [pk(b)], pb[b][:, 0:N], W[:, kc, 128:256], XN[:, kc, c0:c1],
                           start=(kc == 0), stop=(kc == 7))
                    act([pk(a)], ["SA%d" % (i % 2)], out=SA[i % 2][:, 0:N], in_=pb[a][:, 0:N], func=AF.Silu)
                    op("vector", "tensor_tensor", [pk(b), "SA%d" % (i % 2)], gk("H%d" % j, c0, c1),
                       out=H[:, j, c0:c1], in0=pb[b][:, 0:N], in1=SA[i % 2][:, 0:N], op=ALU.mult)
                    if hook is not None:
                        hook()
            for m in range(8):
                s1 = ring_load(wd + 2 * m)
                s2 = ring_load(wd + 2 * m + 1, 768)
                W1 = ring[s1][:, :].rearrange("p (k n) -> p k n", k=16)
                W2 = ring[s2][:, 0:768].rearrange("p (k n) -> p k n", k=6)
                for (c0, c1) in blocks:
                    N = c1 - c0
                    i = bcnt["d"]
                    bcnt["d"] += 1
                    bi = 4 + i % 4
                    for j in range(NJ):
                        w = W1[:, j, :] if j < 16 else W2[:, j - 16, :]
                        mm(["ring%d" % (s1 if j < 16 else s2)] + gk("H%d" % j, c0, c1), [pk(bi)], pb[bi][:, 0:N], w,
                           H[:, j, c0:c1], start=(j == 0), stop=(j == NJ - 1))
                    op("vector", "scalar_tensor_tensor", [pk(bi)] + gk("RT%d" % m, c0, c1), gk("RT%d" % m, c0, c1),
                       out=RT[:, m, c0:c1], in0=pb[bi][:, 0:N], scalar=scale, in1=RT[:, m, c0:c1],
                       op0=ALU.mult, op1=ALU.add)

        ALLH = ["H%d.%d" % (j, g) for j in range(NJ) for g in range(10)]
        ALLXN = ["XN%d.%d" % (c, g) for c in range(8) for g in range(10)]

        def fence(writes):
            op("vector", "memset", [], list(writes) + ["DUMMY"], ap=DUMMY[:, 0:1], constant=0.0)

        def sigmoid_inplace(buf, key, N, r=(), scale_ap=None, bias_ap=None, src=None, src_keys=()):
            kw = {}
            if scale_ap is not None:
                kw = dict(scale=scale_ap, bias=bias_ap)
            else:
                kw = dict(scale=-1.0)
            act(list(src_keys) + [key] + list(r), [key], out=buf, in_=(src if src is not None else buf), func=AF.Exp, **kw)
            act([key, "CONSTS"] + list(r), [key], out=buf, in_=buf, func=AF.Ln, bias=ONEC[0:buf.shape[0], :])
            act([key] + list(r), [key], out=buf, in_=buf, func=AF.Exp, scale=-1.0)

        def qk_sets(ci):
            if ci == 0:
                return dict(QF=SF[0], RS=SF[1], T1=SF[2], T2=SF[3], kQF="SF0", kRS="SF1", kT1="SF2", kT2="SF3",
                            SQb=SB[0], QNb=SB[1], kSQ="SB0", kQN="SB1", b1=6, b2=7)
            return dict(QF=YF[:, 0, :], RS=YF[:, 1, :], T1=YF[:, 2, :], T2=YF[:, 3, :], kQF="YF0", kRS="YF1", kT1="YF2", kT2="YF3",
                        SQb=PT[:, 0, :], QNb=PT[:, 1, :], kSQ="PT0", kQN="PT1", b1=4, b2=5)

        def qk_s1(ch, R):
            z = qk_sets(ch["ci"])
            bi, N, gcol = ch["bank"], ch["N"], ch["gcol"]
            gs = PARAMS[:, gcol:gcol + 1]
            ch["proj"](bi)
            act([pk(bi), "PARAMS"] + R, [z["kQN"]], out=z["QNb"][:, 0:N], in_=pb[bi][:, 0:N], func=AF.Identity, scale=gs)
            act([pk(bi)] + R, [z["kSQ"]], out=z["SQb"][:, 0:N], in_=pb[bi][:, 0:N], func=AF.Square, scale=0.125)
            act([pk(bi), "PARAMS"] + R, [z["kQF"]], out=z["QF"][:, 0:N], in_=pb[bi][:, 0:N], func=AF.Identity, scale=gs)
            mm([z["kQN"], "PROT"] + R, [pk(z["b2"])], pb[z["b2"]][:, 0:N], PROT[:], z["QNb"][:, 0:N])
            mm([z["kSQ"], "ONESBD"] + R, [pk(z["b1"])], pb[z["b1"]][:, 0:N], ONESBD[:], z["SQb"][:, 0:N])

        def qk_s2(ch, R):
            z = qk_sets(ch["ci"])
            N, rc0, dest, dkeys = ch["N"], ch["rc0"], ch["dest"], ch["dkeys"]
            QF, RS, T2, b1, b2 = z["QF"], z["RS"], z["T2"], z["b1"], z["b2"]
            kQF, kRS, kT2 = z["kQF"], z["kRS"], z["kT2"]
            act([pk(b1), "CONSTS"] + R, [kRS], out=RS[:, 0:N], in_=pb[b1][:, 0:N], func=AF.Ln, bias=EPSC)
            act([kRS] + R, [kRS], out=RS[:, 0:N], in_=RS[:, 0:N], func=AF.Exp, scale=-0.5)
            op("vector", "tensor_tensor", [pk(b2), "ST"] + R, [kT2], out=T2[:, 0:N], in0=pb[b2][:, 0:N], in1=ST[:, rc0:rc0 + N],
               op=ALU.mult)
            op("vector", "tensor_tensor", [kQF, "CT"] + R, [kQF], out=QF[:, 0:N], in0=QF[:, 0:N], in1=CT[:, rc0:rc0 + N],
               op=ALU.mult)
            op("vector", "tensor_tensor", [kQF, kT2] + R, [kQF], out=QF[:, 0:N], in0=QF[:, 0:N], in1=T2[:, 0:N], op=ALU.add)
            if isinstance(dest, tuple):
                for hh_, dd in enumerate(dest):
                    op("vector", "tensor_tensor", [kQF, kRS] + R, list(dkeys), out=dd, in0=QF[hh_ * 64:(hh_ + 1) * 64, 0:N],
                       in1=RS[hh_ * 64:(hh_ + 1) * 64, 0:N], op=ALU.mult)
            else:
                op("vector", "tensor_tensor", [kQF, kRS] + R, list(dkeys), out=dest, in0=QF[:, 0:N], in1=RS[:, 0:N], op=ALU.mult)
            if ch.get("dest32") is not None:
                lo, hi, d32 = ch["dest32"]
                op("vector", "tensor_tensor", [kQF, kRS] + R, list(ch["d32keys"]), out=d32, in0=QF[:, lo:hi], in1=RS[:, lo:hi],
                   op=ALU.mult)

        def qk_run(chains, R):
            for i, ch in enumerate(chains):
                ch["ci"] = i % 2
                ch["bank"] = i % 4
            for i in range(len(chains) + 1):
                if i < len(chains):
                    qk_s1(chains[i], R)
                if i >= 1:
                    qk_s2(chains[i - 1], R)

        def ln_stats(sub, R):
            N = sub["N"]
            M1, MSQ, VR = SF[0], SF[1], SF[2]
            op("vector", "tensor_scalar", [pk(4)] + R, ["SF0"], out=M1[:, 0:N], in0=pb[4][:, 0:N], scalar1=1.0 / 512, scalar2=None,
               op0=ALU.mult)
            op("vector", "tensor_tensor", ["SF0"] + R, ["SF1"], out=MSQ[:, 0:N], in0=M1[:, 0:N], in1=M1[:, 0:N], op=ALU.mult)
            op("vector", "scalar_tensor_tensor", [pk(5), "SF1"] + R, ["SF2"], out=VR[:, 0:N], in0=pb[5][:, 0:N], scalar=1.0 / 512,
               in1=MSQ[:, 0:N], op0=ALU.mult, op1=ALU.subtract)
            act(["SF2", "CONSTS"] + R, ["SF2"], out=VR[:, 0:N], in_=VR[:, 0:N], func=AF.Ln, bias=EPSC)
            act(["SF2"] + R, ["SF2"], out=VR[:, 0:N], in_=VR[:, 0:N], func=AF.Exp, scale=-0.5)

        def ln_parts(sub, c, R, RX):
            N, h, mixc0 = sub["N"], sub["h"], sub["mixc0"]
            M1, VR = SF[0], SF[2]
            eh = c % 2
            E = SF[3][:, eh * 256:eh * 256 + N]
            ek = "SF3.%d" % eh
            yv = YF[:, c, h * 256:h * 256 + N]
            yk = "YF%d.%d" % (c, h)

            def A():
                op("vector", "tensor_tensor", [yk, "SF0"] + R, [yk], out=yv, in0=yv, in1=M1[:, 0:N], op=ALU.subtract)
                op("vector", "scalar_tensor_tensor", [yk, "SF2", "PARAMS"] + R, [yk], out=yv, in0=yv,
                   scalar=PARAMS[:, PGCN + c:PGCN + c + 1], in1=VR[:, 0:N], op0=ALU.mult, op1=ALU.mult)

            def B():
                sigmoid_inplace(E, ek, N, r=R + ["PARAMS", "SF3"], scale_ap=-1.0,
                                bias_ap=PARAMS[:, PNB + c:PNB + c + 1], src=yv, src_keys=[yk])

            def C():
                op("vector", "scalar_tensor_tensor", [yk, ek, "SF3", "PARAMS"] + R + RX,
                   ["MIX%d.%d" % (c, g) for g in range(mixc0 // 128, (mixc0 + N - 1) // 128 + 1)],
                   out=MIX[:, c, mixc0:mixc0 + N], in0=yv, scalar=PARAMS[:, PBCN + c:PBCN + c + 1], in1=E,
                   op0=ALU.add, op1=ALU.mult)
            return A, B, C

        conv_pending = []

        def conv_flush():
            for f in conv_pending:
                f()
            del conv_pending[:]

        def conv_chunk(sub, c, R):
            N, h, rhs_fn, ukeys = sub["N"], sub["h"], sub["rhs_fn"], sub["ukeys"]
            i = bcnt["ab"]
            bcnt["ab"] += 1
            yb = i % 2
            for t in range(31):
                mm(["DIAG"] + ukeys(c) + R, [pk(yb)], pb[yb][:, 0:N], DIAG[:, c, t, :], rhs_fn(c, t), start=(t == 0), stop=(t == 30))
            conv_flush()
            yv = YF[:, c, h * 256:h * 256 + N]
            yk = "YF%d.%d" % (c, h)
            act([pk(yb), "PARAMS", "YF%d" % c] + R, [yk], out=yv, in_=pb[yb][:, 0:N], func=AF.Identity,
                bias=PARAMS[:, PBDW + c:PBDW + c + 1])
            op("vector", "tensor_copy", [yk] + R, ["SB%d" % yb], out=SB[yb][:, 0:N], in_=yv)
            act([yk] + R, ["SB2"], out=SB[2][:, 0:N], in_=yv, func=AF.Square)
            def stats(yb=yb, c=c, N=N):
                mm(["SB%d" % yb, "ONES"] + R, [pk(4)], pb[4][:, 0:N], ONES[:], SB[yb][:, 0:N], start=(c == 0), stop=(c == 3))
                mm(["SB2", "ONES"] + R, [pk(5)], pb[5][:, 0:N], ONES[:], SB[2][:, 0:N], start=(c == 0), stop=(c == 3))
            conv_pending.append(stats)

        def conv_run(subs, R, RX, tail=None):
            for i in range(len(subs) + 1):
                cur = subs[i] if i < len(subs) else None
                prev = subs[i - 1] if i >= 1 else None
                if prev is not None:
                    conv_flush()
                    ln_stats(prev, R)
                pendC = None
                for c in range(4):
                    if cur is not None:
                        conv_chunk(cur, c, R)
                    elif tail is not None:
                        tail(c, "pre")
                    if prev is not None:
                        A, B, C = ln_parts(prev, c, R, RX)
                        A()
                        B()
                        if pendC is not None:
                            pendC()
                        pendC = C
                    if cur is None and tail is not None:
                        tail(c, "post")
                if pendC is not None:
                    pendC()

        def mixer(t):
            first, last = (t == 0), (t == NT - 1)
            main_blocks = [(0, 512), (512, 1024)]
            all_blocks = main_blocks + ([(HALO0, SAMP0), (SAMP0, TC)] if first else [])
            R = ["HF"]
            RX = ["XF"]
            fence(ALLH + ["HF"])
            dma("gpsimd", "rope0", [], ["CT"], CT[:, 0:TM], rope[0, :, t * TM:(t + 1) * TM])
            dma("gpsimd", "rope1", [], ["ST"], ST[:, 0:TM], rope[1, :, t * TM:(t + 1) * TM])
            if first:
                dma("gpsimd", "rope0", [], ["CT"], CT[:, TM:TC], rope[0, :, HALF:NROWS])
                dma("gpsimd", "rope1", [], ["ST"], ST[:, TM:TC], rope[1, :, HALF:NROWS])
            norm([(0, 512), (512, 1024)] + ([(HALO0, TC)] if first else []), PGM)
            op("vector", "memset", R, ["VP"], ap=VP[:].rearrange("p k g n -> p (k g n)"), constant=0.0)
            op("vector", "memset", R, ["KT.%d" % g for g in range(9)], ap=KT[:].rearrange("p g n -> p (g n)"), constant=0.0)
            if not first:
                op("vector", "tensor_copy", ["UCAR"] + R, ["Upre"], out=U[:, :, 0:32], in_=UCAR[:])
                op("vector", "tensor_copy", ["KCAR"] + R, ["KT.0"], out=KT[:, :, 0:128], in_=KCAR[:])
                op("vector", "tensor_copy", ["VCAR"] + R, ["VP"], out=VP[:, 0, :, :], in_=VCAR[:])
            for i in range(4):
                s = ring_load(W_IN + i)
                W = ring[s][:, :].rearrange("p (k n) -> p k n", k=8)
                rk = "ring%d" % s
                for (c0, c1) in all_blocks:
                    N = c1 - c0
                    k = bcnt["ab"]
                    bcnt["ab"] += 1
                    a, b = k % 2, 2 + k % 2
                    for kc in range(8):
                        mm([rk] + gk("XN%d" % kc, c0, c1) + R, [pk(a)], pb[a][:, 0:N], W[:, kc, 0:128], XN[:, kc, c0:c1],
                           start=(kc == 0), stop=(kc == 7))
                    for kc in range(8):
                        mm([rk] + gk("XN%d" % kc, c0, c1) + R, [pk(b)], pb[b][:, 0:N], W[:, kc, 128:256], XN[:, kc, c0:c1],
                           start=(kc == 0), stop=(kc == 7))
                    E = SF[3]
                    sigmoid_inplace(E[:, 0:N], "SF3", N, r=R, src=pb[b][:, 0:N], src_keys=[pk(b)])
                    if c0 < 1024:
                        op("vector", "tensor_tensor", [pk(a), "SF3"] + R, ["U%d.%d" % (i, g) for g in range(c0 // 128, c1 // 128)],
                           out=U[:, i, 32 + c0:32 + c1], in0=pb[a][:, 0:N], in1=E[:, 0:N], op=ALU.mult)
                        if last and c1 == 1024:
                            op("vector", "tensor_tensor", [pk(a), "SF3"] + R, ["U32M"], out=U32M[:, i, :], in0=pb[a][:, N - 32:N],
                               in1=E[:, N - 32:N], op=ALU.mult)
                    elif c0 == HALO0:
                        op("vector", "tensor_tensor", [pk(a), "SF3"] + R, ["Upre"], out=U[:, i, 0:32], in0=pb[a][:, 96:128],
                           in1=E[:, 96:128], op=ALU.mult)
                    else:
                        op("vector", "tensor_tensor", [pk(a), "SF3"] + R, ["USAMP"], out=USAMP[:, i, 0:4, 30:46],
                           in0=pb[a][:, 0:64].rearrange("p (s n) -> p s n", s=4), in1=E[:, 0:64].rearrange("p (s n) -> p s n", s=4),
                           op=ALU.mult)
                        op("vector", "tensor_tensor", [pk(a), "SF3"] + R, ["USAMP32"], out=USAMP32[:, i, :], in0=pb[a][:, 0:64],
                           in1=E[:, 0:64], op=ALU.mult)
            if MIXL < 2:
                fence(ALLXN + ["XF"]); fence(ALLH + ["HF"]); return
            chains = []

            def mkproj(rk, W, col0, c0, c1):
                def proj(bi):
                    for kc in range(8):
                        mm([rk] + gk("XN%d" % kc, c0, c1) + R, [pk(bi)], pb[bi][:, 0:c1 - c0], W[:, kc, col0:col0 + 128],
                           XN[:, kc, c0:c1], start=(kc == 0), stop=(kc == 7))
                return proj

            for i in range(2):
                s = ring_load(W_IN + 4 + i)
                W = ring[s][:, :].rearrange("p (k n) -> p k n", k=8)
                rk = "ring%d" % s
                for (c0, c1) in main_blocks + ([(SAMP0, TC)] if first else []):
                    N = c1 - c0
                    qc0 = c0 if c0 < 1024 else 1024
                    for hh in range(2):
                        j = 2 * i + hh
                        chains.append(dict(proj=mkproj(rk, W, hh * 128, c0, c1), N=N, gcol=PGQ, rc0=c0, dest=Q[:, j, qc0:qc0 + N],
                                           dkeys=["Q%d.%d" % (j, g) for g in range(qc0 // 128, (qc0 + N - 1) // 128 + 1)]))
            s = ring_load(W_IN + 6)
            W = ring[s][:, :].rearrange("p (k n) -> p k n", k=8)
            rk = "ring%d" % s
            for (c0, c1) in all_blocks:
                N = c1 - c0
                ch = dict(proj=mkproj(rk, W, 0, c0, c1), N=N, gcol=PGK, rc0=c0)
                if c0 < 1024:
                    ch.update(dest=(KT[0:64, 0, 128 + c0:128 + c1], KT[64:128, 1, 128 + c0:128 + c1]),
                              dkeys=["KT.%d" % g for g in range(1 + c0 // 128, 1 + c1 // 128)],
                              dest32=((N - 128, N, K32[:, 0:128]) if (last and c1 == 1024) else None), d32keys=["K32a"])
                elif c0 == HALO0:
                    ch.update(dest=(KT[0:64, 0, 0:128], KT[64:128, 1, 0:128]), dkeys=["KT.0"])
                else:
                    ch.update(dest=(KST[0:64, 0, 64:128], KST[64:128, 1, 64:128]), dkeys=["KST"], dest32=(0, 64, K32[:, 128:192]),
                              d32keys=["K32b"])
                chains.append(ch)
            qk_run(chains, R)
            vgroups = [(g * 128, g + 1) for g in range(8)] + ([(HALO0, 0)] if first else [])
            for (c0, kg) in ([] if "nov" in SKIP else vgroups):
                k = bcnt["ab"]
                bcnt["ab"] += 1
                a = k % 4
                for kc in range(8):
                    mm([rk] + gk("XN%d" % kc, c0, c0 + 128) + R, [pk(a)], pb[a][:, 0:128], XN[:, kc, c0:c0 + 128], W[:, kc, 128:256],
                       start=(kc == 0), stop=(kc == 7))
                pv = pb[a][:, 0:128].rearrange("p (g n) -> p g n", g=2)
                act([pk(a)] + R, ["VP"], out=VP[:, kg, :, 0:64], in_=pv, func=AF.Copy)
                op("vector", "tensor_copy", [pk(a)] + R, ["VP"], out=VP[:, kg, :, 128:192], in_=pv)
                if last and kg == 8:
                    op("vector", "tensor_copy", [pk(a)] + R, ["V32L"], out=V32L[:], in_=pb[a][:, 0:128])
            if first and "novs" not in SKIP:
                k = bcnt["ab"]
                bcnt["ab"] += 1
                a = k % 4
                for kc in range(8):
                    mm([rk] + gk("XN%d" % kc, 1088, TC) + R, [pk(a)], pb[a][:, 0:128], XN[:, kc, 1088:TC],
                       W[:, kc, 128:256], start=(kc == 0), stop=(kc == 7))
                pv = pb[a][64:128, 0:128].rearrange("p (g n) -> p g n", g=2)
                if "vsA" not in SKIP:
                    act([pk(a)] + R, ["VS"], out=VS[64:128, :, 0:64], in_=pv, func=AF.Copy)
                if "vsB" not in SKIP:
                    op("vector", "tensor_copy", [pk(a)] + R, ["VS"], out=VS[64:128, :, 128:192], in_=pv)
                if "vsC" not in SKIP:
                    op("vector", "tensor_copy", [pk(a)] + R, ["V32S"], out=V32S[64:128, :], in_=pb[a][64:128, 0:128])
            if MIXL < 4:
                fence(ALLXN + ["XF"]); fence(ALLH + ["HF"]); return
            fence(ALLXN + ["XF"])
            if not last:
                op("vector", "tensor_copy", ["U%d.7" % c for c in range(4)] + R, ["UCAR"], out=UCAR[:], in_=U[:, :, 1024:1056])
                op("vector", "tensor_copy", ["KT.8"] + R, ["KCAR"], out=KCAR[:], in_=KT[:, :, 1024:1152])
                op("vector", "tensor_copy", ["VP"] + R, ["VCAR"], out=VCAR[:], in_=VP[:, 8, :, :])

            def att_scores(G):
                PTb, pkey = (PT, "PT%d") if G % 2 == 0 else (PT2, "PT2.%d")
                for kgi in range(2):
                    kg = G + kgi
                    for g in range(2):
                        bi = kgi * 2 + g
                        mm(["KT.%d" % kg] + ["Q%d.%d" % (j, G) for j in range(4)] + R, [pk(bi)], pb[bi][:, :],
                           KT[:, g, kg * 128:(kg + 1) * 128], Q[:, :, G * 128:(G + 1) * 128])
                        part = (0, 64) if kgi == 0 else (64, 128)
                        qmask = 1 if kgi == 0 else 0
                        act([pk(bi)] + R, [pkey % bi], out=PTb[:, bi, :], in_=pb[bi][:, :], func=AF.Exp, scale=0.125)
                        op("vector", "memset", R, [pkey % bi],
                           ap=PTb[part[0]:part[1], bi, :].rearrange("p (j q n) -> p j q n", j=4, q=2)[:, :, qmask, :], constant=0.0)

            def att_pv(G):
                PTb, pkey = (PT, "PT%d") if G % 2 == 0 else (PT2, "PT2.%d")
                bn, bd = (6, 7) if G % 2 == 0 else (4, 5)
                n_ = 0
                for g in range(2):
                    for kgi in range(2):
                        kg = G + kgi
                        bi = kgi * 2 + g
                        sel = slice(0, 128) if g == 0 else slice(64, 192)
                        rhs = PTb[:, bi, :]
                        mm(["VP", pkey % bi] + R, [pk(bn)], pb[bn][:, :], VP[:, kg, g, sel], rhs, start=(n_ == 0), stop=(n_ == 3))
                        ow = OPAD[:, (1 if (first and kg == 0) else 0), sel]
                        mm(["OPAD", pkey % bi] + R, [pk(bd)], pb[bd][:, :], ow, rhs, start=(n_ == 0), stop=(n_ == 3))
                        n_ += 1

            def att_norm(G):
                bn, bd = (6, 7) if G % 2 == 0 else (4, 5)
                Rr, rkey = (SF[1], "SF1") if G % 2 == 0 else (SF[0], "SF0")
                for j in range(4):
                    act([pk(bd), "ESINK"] + R, [rkey], out=Rr[:, j * 128:(j + 1) * 128], in_=pb[bd][:, j * 128:(j + 1) * 128],
                        func=AF.Ln, bias=ESINK[:, j:j + 1])
                act([rkey] + R, [rkey], out=Rr[:, :], in_=Rr[:, :], func=AF.Exp, scale=-1.0)
                op("vector", "tensor_tensor", [pk(bn), rkey] + R + RX, ["MIX%d.%d" % (4 + j, G) for j in range(4)],
                   out=MIX[:, 4:8, G * 128:(G + 1) * 128], in0=pb[bn][:, :].rearrange("p (j n) -> p j n", j=4),
                   in1=Rr[:, :].rearrange("p (j n) -> p j n", j=4), op=ALU.mult)

            subs = []
            for c0 in range(0, 1024, 256):
                subs.append(dict(N=256, h=len(subs) % 2, mixc0=c0,
                                 rhs_fn=(lambda c, tt, c0=c0: U[:, c, c0 + 2 + tt:c0 + 2 + tt + 256]),
                                 ukeys=(lambda c, c0=c0: ["Upre"] + ["U%d.%d" % (c, g) for g in range(max(0, c0 // 128 - 1), (c0 + 256) // 128)])))
            if first:
                subs.append(dict(N=128, h=len(subs) % 2, mixc0=1024, rhs_fn=(lambda c, tt: USAMP[:, c, :, tt:tt + 16]),
                                 ukeys=(lambda c: ["USAMP"])))
            op("vector", "memset", R, ["PT%d" % i for i in range(4)], ap=PT[:].rearrange("p c n -> p (c n)"), constant=0.0)

            def att_tail(c, when):
                if c == 0 and when == "pre":
                    att_scores(0)
                elif c == 2 and when == "post":
                    op("vector", "memset", R, ["YF0", "YF1"] + ["YF%d.%d" % (i, hh) for i in range(2) for hh in range(2)] + ["PT2.%d" % i for i in range(4)],
                       ap=PT2[:].rearrange("p c n -> p (c n)"), constant=0.0)
                    att_scores(1)
                elif c == 3 and when == "post":
                    att_pv(0)

            conv_run(subs, R, RX, tail=att_tail)
            if MIXL < 5:
                fence(ALLXN + ["XF"]); fence(ALLH + ["HF"]); return
            for G in range(1, 8):
                if G + 1 < 8:
                    att_scores(G + 1)
                att_pv(G)
                att_norm(G - 1)
            att_norm(7)
            if MIXL < 6:
                fence(ALLXN + ["XF"]); fence(ALLH + ["HF"]); return
            if first:
                for s4 in range(4):
                    for g in range(2):
                        oc = (s4 * 2 + g) * 64
                        qv = Q[:, :, 1024 + s4 * 16:1024 + s4 * 16 + 16]
                        mm(["KCT"] + ["Q%d.8" % j for j in range(4)] + R, [pk(0)], pb[0][:, oc:oc + 64], KCT[:, g, s4, :], qv)
                for g in range(2):
                    mm(["KST"] + ["Q%d.8" % j for j in range(4)] + R, [pk(1)], pb[1][:, g * 256:(g + 1) * 256],
                       KST[:, g, 0:128], Q[:, :, 1024:1088])
                act([pk(0)] + R, ["PT0"], out=PT[:, 0, :], in_=pb[0][:, :], func=AF.Exp, scale=0.125)
                op("vector", "memset", R, ["PT1"], ap=PT[0:64, 1, :], constant=0.0)
                act([pk(1)] + R, ["PT1"], out=PT[64:128, 1, :], in_=pb[1][64:128, :], func=AF.Exp, scale=0.125)
                for g in range(2):
                    op("vector", "tensor_tensor", ["PT1", "SMASK"] + R, ["PT1"], out=PT[64:128, 1, g * 256:(g + 1) * 256],
                       in0=PT[64:128, 1, g * 256:(g + 1) * 256], in1=SMASK[64:128, :], op=ALU.mult)
                for j in range(4):
                    for s4 in range(4):
                        ocol = slice(j * 64 + s4 * 16, j * 64 + s4 * 16 + 16)
                        for g in range(2):
                            sel = slice(0, 128) if g == 0 else slice(64, 192)
                            pc = (s4 * 2 + g) * 64 + j * 16
                            mm(["VC", "PT0"] + R, [pk(6)], pb[6][:, ocol], VC[:, s4, g, sel], PT[:, 0, pc:pc + 16], start=(g == 0), stop=(g == 1))
                            mm(["OPAD", "PT0"] + R, [pk(7)], pb[7][:, ocol], OPAD[:, 0, sel], PT[:, 0, pc:pc + 16], start=(g == 0), stop=(g == 1))
                    for g in range(2):
                        sel = slice(0, 128) if g == 0 else slice(64, 192)
                        pc = g * 256 + j * 64
                        mm(["VS", "PT1"] + R, [pk(4)], pb[4][:, j * 64:(j + 1) * 64], VS[:, g, sel], PT[:, 1, pc:pc + 64], start=(g == 0), stop=(g == 1))
                        mm(["OPAD", "PT1"] + R, [pk(5)], pb[5][:, j * 64:(j + 1) * 64], OPAD[:, 0, sel], PT[:, 1, pc:pc + 64], start=(g == 0), stop=(g == 1))
                Rr, T1, NS, T3 = SF[0], SF[1], SF[2], SF[3]
                op("vector", "tensor_copy", [pk(4)] + R, ["SF1"], out=T1[:, 0:256], in_=pb[4][:, 0:256])
                op("vector", "tensor_tensor", [pk(6), "SF1"] + R, ["SF2"], out=NS[:, 0:256], in0=pb[6][:, 0:256], in1=T1[:, 0:256], op=ALU.add)
                op("vector", "tensor_copy", [pk(5)] + R, ["SF3"], out=T3[:, 0:256], in_=pb[5][:, 0:256])
                op("vector", "tensor_tensor", [pk(7), "SF3"] + R, ["SF0"], out=Rr[:, 0:256], in0=pb[7][:, 0:256], in1=T3[:, 0:256], op=ALU.add)
                for j in range(4):
                    op("vector", "tensor_scalar", ["SF0", "ESINK"] + R, ["SF0"], out=Rr[:, j * 64:(j + 1) * 64],
                       in0=Rr[:, j * 64:(j + 1) * 64], scalar1=ESINK[:, j:j + 1], scalar2=None, op0=ALU.add)
                act(["SF0"] + R, ["SF0"], out=Rr[:, 0:256], in_=Rr[:, 0:256], func=AF.Ln)
                act(["SF0"] + R, ["SF0"], out=Rr[:, 0:256], in_=Rr[:, 0:256], func=AF.Exp, scale=-1.0)
                op("vector", "tensor_tensor", ["SF2", "SF0"] + R + RX, ["MIX%d.8" % (4 + j) for j in range(4)],
                   out=MIX[:, 4:8, 1024:1088], in0=NS[:, 0:256].rearrange("p (j n) -> p j n", j=4),
                   in1=Rr[:, 0:256].rearrange("p (j n) -> p j n", j=4), op=ALU.mult)
            if MIXL < 7:
                fence(ALLXN + ["XF"]); fence(ALLH + ["HF"]); return
            for n in range(4):
                s = ring_load(W_OUT + n)
                W = ring[s][:, :].rearrange("p (k n) -> p k n", k=8)
                rk = "ring%d" % s
                for (c0, c1) in main_blocks + ([(SAMP0, TC)] if first else []):
                    N = c1 - c0
                    mc0 = c0 if c0 < 1024 else 1024
                    for mi in range(2):
                        m = 2 * n + mi
                        i = bcnt["d"]
                        bcnt["d"] += 1
                        bi = 4 + i % 4
                        for kc in range(8):
                            mm([rk] + ["MIX%d.%d" % (kc, g) for g in range(mc0 // 128, (mc0 + N - 1) // 128 + 1)] + R + RX, [pk(bi)],
                               pb[bi][:, 0:N], W[:, kc, mi * 128:(mi + 1) * 128], MIX[:, kc, mc0:mc0 + N], start=(kc == 0), stop=(kc == 7))
                        op("vector", "tensor_tensor", [pk(bi)] + gk("RT%d" % m, c0, c1), gk("RT%d" % m, c0, c1),
                           out=RT[:, m, c0:c1], in0=pb[bi][:, 0:N], in1=RT[:, m, c0:c1], op=ALU.add)
            if MIXL < 8:
                fence(ALLXN + ["XF"]); fence(ALLH + ["HF"]); return
            if first:
                for c in range(4):
                    tr(["USAMP32", "CM"] + R, [pk(0)], pb[0][0:64, c * 128:(c + 1) * 128], USAMP32[:, c, :], IDF)
                op("vector", "tensor_copy", [pk(0)], ["OST"], out=OST[0:64, :], in_=pb[0][0:64, :])
                for s4 in range(4):
                    dma("sync", "o0", ["OST"], [], ocs_s[s4, 14:30, :], OST[s4 * 16:(s4 + 1) * 16, :])
                tr(["K32b", "CM"], [pk(1)], pb[1][0:64, 0:128], K32[:, 128:192], IDF)
                op("vector", "tensor_copy", [pk(1)], ["V32L"], out=V32L[0:64, :], in_=pb[1][0:64, 0:128])
                for s4 in range(4):
                    dma("sync", "o1", ["V32L"], [], okw_s[s4, 112:128, :], V32L[s4 * 16:(s4 + 1) * 16, :])
                    dma("sync", "o2", ["V32S"], [], ovw_s[s4, 112:128, :], V32S[64 + s4 * 16:64 + (s4 + 1) * 16, :])
            if last:
                for c in range(4):
                    tr(["U32M", "CM"], [pk(0)], pb[0][0:32, c * 128:(c + 1) * 128], U32M[:, c, :], IDF)
                op("vector", "tensor_copy", [pk(0)], ["OST"], out=OST[0:32, :], in_=pb[0][0:32, :])
                dma("sync", "o0", ["OST"], [], ocs_p[:, :], OST[2:32, :])
                tr(["K32a", "CM"], [pk(1)], pb[1][:, 0:128], K32[:, 0:128], IDF)
                op("vector", "tensor_copy", [pk(1)], ["USAMP32"], out=USAMP32[:, 0:2, :].rearrange("p a n -> p (a n)"), in_=pb[1][:, 0:128])
                dma("sync", "o1", ["USAMP32"], [], okw_p[:, :], USAMP32[:, 0:2, :].rearrange("p a n -> p (a n)"))
                dma("sync", "o2", ["V32L"], [], ovw_p[:, :], V32L[:, :])
            fence(ALLXN + ["XF"])
            fence(ALLH + ["HF"])

        for t in range(NTILES):
            first = (t == 0)
            if first:
                hs = [load_dma(0, 128), load_dma(128, 128)]
                for g in range(8):
                    load_compute(hs[g], 128, g * 128)
                    if g + 2 < 8:
                        hs.append(load_dma((g + 2) * 128, 128))
            if first:
                load_group(HALF, 128, HALO0)
                load_group(HALF + 128, 64, SAMP0)
            if first and "pass" not in SKIP:
                for s in range(4):
                    dma("sync", "o0", [], [], ocs_s[s, 0:14, :], sconv[s, 16:30, :])
                    dma("sync", "o1", [], [], okw_s[s, 0:112, :], ck[s, 16:128, :])
                    dma("sync", "o2", [], [], ovw_s[s, 0:112, :], cv[s, 16:128, :])
            b1 = [(0, 512), (512, 1024)] + ([(HALO0, TC)] if first else [])
            b2 = [(0, 512), (512, 1024)] + ([(SAMP0, TC)] if first else [])
            if LEVEL >= 1:
                norm(b1, PG1)
            if LEVEL >= 2:
                ff(b1, W_FF1U, W_FF1D, 0.5, hook=((lambda: diag_some(2)) if first else None))
            if first:
                late_setup()
            if LEVEL >= 3:
                mixer(t)
            if LEVEL >= 4:
                norm(b2, PG2)
                ff(b2, W_FF2U, W_FF2D, 0.5)
            if first:
                store_group(HALF, 64, SAMP0)
            nxt = t + 1 < NTILES
            hs = [load_dma((t + 1) * TM, 128), load_dma((t + 1) * TM + 128, 128)] if nxt else []
            for g in range(8):
                store_group(t * TM + g * 128, 128, g * 128)
                if nxt:
                    load_compute(hs[g], 128, g * 128)
                    if g + 2 < 8:
                        hs.append(load_dma((t + 1) * TM + (g + 2) * 128, 128))
        print("sbuf remaining", nc.sbuf_bytes_remaining, "ops", len(P.ops))
        counts = P.emit(nc, st)
        print(counts)
    return nc, counts


_CACHE = {}


def _pack_weights(w_ff1_in, w_ff1_out, w_in, w_out, w_ff2_in, w_ff2_out):
    wall = np.zeros((NCHUNK, 128, 2048), np.float32)

    def up(w, base):
        w4 = w.reshape(8, 128, 2, NJ, 128)
        wall[base:base + NJ] = w4.transpose(3, 1, 0, 2, 4).reshape(NJ, 128, 2048)

    def down(w, base):
        w4 = w.reshape(NJ, 128, 8, 128)
        for m in range(8):
            wall[base + 2 * m] = w4[0:16, :, m, :].transpose(1, 0, 2).reshape(128, 2048)
            wall[base + 2 * m + 1, :, 0:768] = w4[16:22, :, m, :].transpose(1, 0, 2).reshape(128, 768)

    up(w_ff1_in, W_FF1U)
    down(w_ff1_out, W_FF1D)
    up(w_ff2_in, W_FF2U)
    down(w_ff2_out, W_FF2D)
    wi = w_in.reshape(8, 128, 1792)
    for i in range(4):
        blk = np.concatenate([wi[:, :, i * 128:(i + 1) * 128], wi[:, :, 512 + i * 128:512 + (i + 1) * 128]], axis=2)
        wall[W_IN + i] = blk.transpose(1, 0, 2).reshape(128, 2048)
    qcols = []
    for j in range(4):
        qcols += list(range(1024 + j * 64, 1024 + (j + 1) * 64)) + list(range(1024 + (4 + j) * 64, 1024 + (5 + j) * 64))
    qcols = np.array(qcols)
    for i in range(2):
        blk = wi[:, :, qcols[i * 256:(i + 1) * 256]]
        wall[W_IN + 4 + i] = blk.transpose(1, 0, 2).reshape(128, 2048)
    wall[W_IN + 6] = wi[:, :, 1536:1792].transpose(1, 0, 2).reshape(128, 2048)
    rows = list(range(512))
    for j in range(4):
        rows += list(range(512 + j * 64, 512 + (j + 1) * 64)) + list(range(512 + (4 + j) * 64, 512 + (5 + j) * 64))
    wo = w_out[np.array(rows)].reshape(8, 128, 1024)
    for n in range(4):
        wall[W_OUT + n] = wo[:, :, n * 256:(n + 1) * 256].transpose(1, 0, 2).reshape(128, 2048)
    return wall


def _rope_tables(base):
    half = 8
    inv = np.power(np.float32(500000.0), -np.arange(half, dtype=np.float32) / np.float32(half)).astype(np.float32)
    pos = np.concatenate([base + np.arange(HALF), base - 128 + np.arange(128), 4096 + (np.arange(64) % 16)]).astype(np.float32)
    ang = pos[None, :] * inv[:, None]
    cos = np.cos(ang).astype(np.float32)
    sin = np.sin(ang).astype(np.float32)
    tab = np.zeros((2, 128, NROWS), np.float32)
    tab[0] = 1.0
    for h in range(2):
        tab[0, h * 64:h * 64 + 8] = cos
        tab[0, h * 64 + 8:h * 64 + 16] = cos
        tab[1, h * 64:h * 64 + 8] = sin
        tab[1, h * 64 + 8:h * 64 + 16] = sin
    return tab


def kernel(x_prompt, x_sample, state_conv, cache_k_win, cache_v_win, g_ff1, w_ff1_in, w_ff1_out, g_mix, w_in, g_q, g_k,
           sinks, w_dw, b_dw, g_cn, b_cn, w_out, g_ff2, w_ff2_in, w_ff2_out):
    f = lambda a: np.ascontiguousarray(np.asarray(a, dtype=np.float32))
    x_prompt, x_sample, state_conv, cache_k_win, cache_v_win = map(f, (x_prompt, x_sample, state_conv, cache_k_win, cache_v_win))
    if "nc" not in _CACHE:
        _CACHE["nc"] = build_nc()[0]
    nc = _CACHE["nc"]
    wall = _pack_weights(f(w_ff1_in)[0], f(w_ff1_out)[0], f(w_in)[0], f(w_out)[0], f(w_ff2_in)[0], f(w_ff2_out)[0])
    gq, gk_ = f(g_q)[0], f(g_k)[0]
    setup_rows = np.concatenate([f(g_ff1)[0].reshape(8, 128), f(g_mix)[0].reshape(8, 128), f(g_ff2)[0].reshape(8, 128),
                                 f(b_dw)[0].reshape(4, 128), f(g_cn)[0].reshape(4, 128), f(b_cn)[0].reshape(4, 128),
                                 np.concatenate([gq, gq])[None], np.concatenate([gk_, gk_])[None]], axis=0)
    cmat = np.zeros((4, 128, 128), np.float32)
    cmat[0] = np.eye(128)
    cmat[1] = 1.0
    cmat[2, 0:64, 0:64] = 1.0
    cmat[2, 64:128, 64:128] = 1.0
    for h in range(2):
        for m in range(8):
            cmat[3, h * 64 + m + 8, h * 64 + m] = -1.0
            cmat[3, h * 64 + m, h * 64 + m + 8] = 1.0
    opad1 = np.zeros((128, 192), np.float32)
    opad1[:, 0:64] = 1.0
    opad1[:, 128:192] = 1.0
    smask = np.zeros((64, 256), np.float32)
    for s_ in range(4):
        for j in range(4):
            smask[s_ * 16:(s_ + 1) * 16, j * 64 + s_ * 16:j * 64 + (s_ + 1) * 16] = 1.0
    in_maps = []
    for c in range(8):
        b, hf = c // 2, c % 2
        base = hf * HALF
        xin = np.zeros((NROWS, D), np.float32)
        xin[0:HALF] = x_prompt[b, base:base + HALF]
        if hf == 1:
            xin[HALF:HALF + 128] = x_prompt[b, base - 128:base]
        xin[HALF + 128:] = x_sample[4 * c:4 * c + 4].reshape(64, D)
        opad = np.stack([opad1, opad1 * np.float32(hf)])
        in_maps.append(dict(
            xin=xin, wall=wall, rope=_rope_tables(base), setup_rows=setup_rows, wdw=f(w_dw)[0], sinks=f(sinks),
            cmat=cmat, opad=opad, smask=smask, sconv=state_conv[0, 4 * c:4 * c + 4], ck=cache_k_win[0, 4 * c:4 * c + 4].reshape(4, 128, 128),
            cv=cache_v_win[0, 4 * c:4 * c + 4].reshape(4, 128, 128)))
    res = run_bass_kernel_spmd(nc, in_maps, core_ids=list(range(8)))
    r = res.results
    y_prompt = np.zeros((4, SEQ, D), np.float32)
    y_sample = np.zeros((32, 16, D), np.float32)
    csp = np.zeros((1, 4, 30, 512), np.float32)
    kwp = np.zeros((1, 4, 128, 2, 64), np.float32)
    vwp = np.zeros((1, 4, 128, 2, 64), np.float32)
    css = np.zeros((1, 32, 30, 512), np.float32)
    kws = np.zeros((1, 32, 128, 2, 64), np.float32)
    vws = np.zeros((1, 32, 128, 2, 64), np.float32)
    for c in range(8):
        b, hf = c // 2, c % 2
        y_prompt[b, hf * HALF:(hf + 1) * HALF] = r[c]["yout"][0:HALF]
        y_sample[4 * c:4 * c + 4] = r[c]["yout"][HALF:].reshape(4, 16, D)
        if hf == 1:
            csp[0, b] = r[c]["ocs_p"]
            kwp[0, b] = r[c]["okw_p"].reshape(128, 2, 64)
            vwp[0, b] = r[c]["ovw_p"].reshape(128, 2, 64)
        css[0, 4 * c:4 * c + 4] = r[c]["ocs_s"]
        kws[0, 4 * c:4 * c + 4] = r[c]["okw_s"].reshape(4, 128, 2, 64)
        vws[0, 4 * c:4 * c + 4] = r[c]["ovw_s"].reshape(4, 128, 2, 64)
    return (y_prompt, y_sample, csp, kwp, vwp, css, kws, vws)
```

```python
import numpy as np
from contextlib import ExitStack
import concourse.bass as bass
import concourse.mybir as mybir
from concourse.bass_utils import run_bass_kernel_spmd

F32 = mybir.dt.float32
BF16 = mybir.dt.bfloat16
AF = mybir.ActivationFunctionType
ALU = mybir.AluOpType

ENGS = ("sync", "scalar", "vector", "gpsimd", "tensor")
SEM_MAX = 30000


class _Op:
    __slots__ = ("eng", "fn", "deps", "idx", "dma_sem", "tok", "signal")

    def __init__(self, eng, fn, idx, dma_sem):
        self.eng, self.fn, self.idx, self.dma_sem = eng, fn, idx, dma_sem
        self.deps = set()
        self.tok = None
        self.signal = False


class Prog:
    def __init__(self):
        self.ops = []
        self.last_w = {}
        self.readers = {}
        self.last_dma_on_sem = {}

    def add(self, eng, fn, reads=(), writes=(), dma=None):
        op = _Op(eng, fn, len(self.ops), dma)
        self.ops.append(op)
        for k in reads:
            w = self.last_w.get(k)
            if w is not None:
                op.deps.add(w)
        for k in writes:
            w = self.last_w.get(k)
            if w is not None:
                op.deps.add(w)
            r = self.readers.get(k)
            if r:
                op.deps.update(r)
        for k in reads:
            self.readers.setdefault(k, []).append(op.idx)
        for k in writes:
            self.last_w[k] = op.idx
            self.readers[k] = []
        if dma is not None:
            p = self.last_dma_on_sem.get(dma)
            if p is not None:
                op.deps.add(p)
            self.last_dma_on_sem[dma] = op.idx
        op.deps.discard(op.idx)
        return op

    def emit(self, nc, stack):
        ops = self.ops
        for op in ops:
            keep = set()
            for d in op.deps:
                p = ops[d]
                if p.dma_sem is None and op.dma_sem is None and p.eng == op.eng == "tensor":
                    continue
                keep.add(d)
            best = {}
            for d in keep:
                p = ops[d]
                k = ("d", p.dma_sem) if p.dma_sem is not None else ("e", p.eng)
                if k not in best or best[k] < d:
                    best[k] = d
            op.deps = set(best.values())
            for d in op.deps:
                ops[d].signal = True
        dma_names = sorted({op.dma_sem for op in ops if op.dma_sem is not None})
        dma_sems = {n: stack.enter_context(nc.semaphore("d_" + n)) for n in dma_names}
        dma_cnt = {n: 0 for n in dma_names}
        eng_sems = {e: [] for e in ENGS}
        eng_cnt = {e: 0 for e in ENGS}
        for op in ops:
            if op.dma_sem is not None:
                dma_cnt[op.dma_sem] += 16
                op.tok = (dma_sems[op.dma_sem], dma_cnt[op.dma_sem])
            elif op.signal:
                c = eng_cnt[op.eng]
                k, v = divmod(c, SEM_MAX)
                if k >= len(eng_sems[op.eng]):
                    eng_sems[op.eng].append(stack.enter_context(nc.semaphore("e_%s%d" % (op.eng, k))))
                op.tok = (eng_sems[op.eng][k], v + 1)
                eng_cnt[op.eng] = c + 1
        print('signals', eng_cnt, 'dma', dma_cnt)
        block = stack.enter_context(nc.Block())
        per_eng = {e: [op for op in ops if op.eng == e] for e in ENGS}
        final_dma = {}
        for op in ops:
            if op.dma_sem is not None:
                final_dma[op.dma_sem] = op.tok

        def make(e):
            def body(eng):
                waited = {}
                for op in per_eng[e]:
                    need = {}
                    for d in op.deps:
                        s, v = ops[d].tok
                        if need.get(id(s), (None, 0))[1] < v:
                            need[id(s)] = (s, v)
                    for s, v in need.values():
                        if waited.get(id(s), 0) < v:
                            eng.wait_ge(s, v)
                            waited[id(s)] = v
                    ins = op.fn(eng)
                    if op.tok is not None:
                        ins.then_inc(op.tok[0], 16 if op.dma_sem is not None else 1)
                if e == "sync":
                    for s, v in final_dma.values():
                        if waited.get(id(s), 0) < v:
                            eng.wait_ge(s, v)
                            waited[id(s)] = v
            return body

        block.sync(make("sync"))
        block.scalar(make("scalar"))
        block.vector(make("vector"))
        block.gpsimd(make("gpsimd"))
        block.tensor(make("tensor"))
        return {e: len(per_eng[e]) for e in ENGS}


D = 1024
DFF = 2816
NJ = 22
SEQ = 8192
HALF = 4096
NT = 4
TM = 1024
TC = 1216
HALO0, SAMP0 = 1024, 1152
NROWS = HALF + 128 + 64
EPS = 1e-6
NSLOT = 4
W_FF1U, W_FF1D, W_IN, W_OUT, W_FF2U, W_FF2D, NCHUNK = 0, 22, 38, 45, 49, 71, 87
PG1, PGM, PG2, PBDW, PGCN, PBCN, PGQ, PGK, PNG, PNB = 0, 8, 16, 24, 28, 32, 36, 37, 40, 44


LEVEL = 9
MIXL = 99
SKIP = set()
NTILES = NT


def build_nc():
    nc = bass.Bass("TRN2", target_bir_lowering=False)
    din = lambda n, s: nc.dram_tensor(n, s, F32, kind="ExternalInput").ap()
    dout = lambda n, s: nc.dram_tensor(n, s, F32, kind="ExternalOutput").ap()
    xin = din("xin", [NROWS, D])
    wall = din("wall", [NCHUNK, 128, 2048])
    rope = din("rope", [2, 128, NROWS])
    setup_rows = din("setup_rows", [38, 128])
    wdw = din("wdw", [31, 512])
    sinks = din("sinks", [1, 8])
    cmat = din("cmat", [4, 128, 128])
    opad = din("opad", [2, 128, 192])
    smask = din("smask", [64, 256])
    sconv = din("sconv", [4, 30, 512])
    ck = din("ck", [4, 128, 128])
    cv = din("cv", [4, 128, 128])
    yout = dout("yout", [HALF + 64, D])
    ocs_p = dout("ocs_p", [30, 512])
    okw_p = dout("okw_p", [128, 128])
    ovw_p = dout("ovw_p", [128, 128])
    ocs_s = dout("ocs_s", [4, 30, 512])
    okw_s = dout("okw_s", [4, 128, 128])
    ovw_s = dout("ovw_s", [4, 128, 128])

    P = Prog()
    with ExitStack() as st:
        def sb(name, shape, dt):
            return st.enter_context(nc.sbuf_tensor(name, shape, dt))

        RT = sb("RT", [128, 8, TC], F32)
        XNR = sb("XNR", [128, 8 * TC], BF16)
        HR = sb("HR", [128, NJ * TC], BF16)
        XN = XNR[:, :].rearrange("p (c n) -> p c n", c=8)
        MIX = XNR[:, 0:8 * 1152].rearrange("p (c n) -> p c n", c=8)
        H = HR[:, :].rearrange("p (c n) -> p c n", c=NJ)
        hoff = [0]

        def carve(nelem_bf16):
            a = HR[:, hoff[0]:hoff[0] + nelem_bf16]
            hoff[0] += nelem_bf16
            return a

        U = carve(4 * 1056).rearrange("p (c n) -> p c n", c=4)
        Q = carve(4 * 1088).rearrange("p (c n) -> p c n", c=4)
        KT = carve(2 * 1152).rearrange("p (g n) -> p g n", g=2)
        VP = carve(9 * 2 * 192).rearrange("p (k g n) -> p k g n", k=9, g=2)
        YFraw = carve(2 * 4 * 512)
        YF = YFraw.bitcast(F32).rearrange("p (c n) -> p c n", c=4)
        PT2 = YFraw[:, 0:4 * 512].rearrange("p (c n) -> p c n", c=4)
        PT = carve(4 * 512).rearrange("p (c n) -> p c n", c=4)
        SF = [carve(2 * 512).bitcast(F32) for _ in range(4)]
        SB = [carve(512) for _ in range(3)]
        assert hoff[0] <= NJ * TC, hoff[0]

        ring = [sb("ring%d" % i, [128, 2048], BF16) for i in range(NSLOT)]
        XS = [sb("XS%d" % i, [128, D], F32) for i in range(2)]
        YS = [sb("YS%d" % i, [128, D], F32) for i in range(2)]
        OST = sb("OST", [128, 512], F32)
        DIAG = sb("DIAG", [128, 4, 31, 128], BF16)
        CT = sb("CT", [128, TC], BF16)
        ST = sb("ST", [128, TC], BF16)
        CM = sb("CM", [128, 4, 128], F32)
        IDF = CM[:, 0, :]
        IDB = sb("IDB", [128, 128], BF16)
        ONES = sb("ONES", [128, 128], BF16)
        ONESBD = sb("ONESBD", [128, 128], BF16)
        PROT = sb("PROT", [128, 128], BF16)
        OPAD = sb("OPAD", [128, 2, 192], BF16)
        PARAMS = sb("PARAMS", [128, 48], F32)
        WDW = sb("WDW", [128, 4, 31], F32)
        ESINK = sb("ESINK", [128, 4], F32)
        SKROW = sb("SKROW", [1, 8], F32)
        CONSTS = sb("CONSTS", [128, 2], F32)
        DUMMY = sb("DUMMY", [128, 2], F32)
        SQ = [sb("SQ%d" % i, [128, 512], BF16) for i in range(2)]
        RSTD = sb("RSTD", [128, 512], F32)
        SA = [sb("SA%d" % i, [128, 512], F32) for i in range(2)]
        UCAR = sb("UCAR", [128, 4, 32], BF16)
        KCAR = sb("KCAR", [128, 2, 128], BF16)
        VCAR = sb("VCAR", [128, 2, 192], BF16)
        K32 = sb("K32", [128, 192], F32)
        U32M = sb("U32M", [128, 4, 32], F32)
        V32L = sb("V32L", [128, 128], F32)
        USAMP = sb("USAMP", [128, 4, 8, 46], BF16)
        USAMP32 = sb("USAMP32", [128, 4, 64], F32)
        KCT = sb("KCT", [128, 2, 4, 128], BF16)
        KST = sb("KST", [128, 2, 128], BF16)
        VC = sb("VC", [128, 4, 2, 192], BF16)
        VS = sb("VS", [128, 2, 192], BF16)
        V32S = sb("V32S", [128, 128], F32)
        SMASK = sb("SMASK", [128, 256], BF16)
        pb = [st.enter_context(nc.psum_tensor("pb%d" % i, [128, 512], F32)) for i in range(8)]
        EPSC = CONSTS[:, 0:1]
        ONEC = CONSTS[:, 1:2]

        def op(eng, meth, reads, writes, **kw):
            P.add(eng, lambda e: getattr(e, meth)(**kw), reads, writes)

        def act(reads, writes, **kw):
            op("scalar", "activation", reads, writes, **kw)

        def mm(reads, writes, out, lhsT, rhs, start=True, stop=True):
            P.add("tensor", lambda e: e.matmul(out, lhsT=lhsT, rhs=rhs, start=start, stop=stop), reads, writes)

        def tr(reads, writes, out, in_, identity):
            P.add("tensor", lambda e: e.transpose(out=out, in_=in_, identity=identity), reads, writes)

        def dma(eng, sem, reads, writes, out, in_):
            P.add(eng, lambda e: e.dma_start(out=out, in_=in_), reads, writes, dma=sem)

        def gk(name, c0, c1):
            return ["%s.%d" % (name, g) for g in range(c0 // 128, (c1 - 1) // 128 + 1)]

        def pk(i):
            return "pb%d" % i

        dma("sync", "su0", [], ["CM"], CM[:], cmat.rearrange("k p n -> p k n"))
        SROW = XS[0][0:38, 0:128]
        WROW = XS[1][0:31, 0:512]
        dma("sync", "xs0", [], ["XS0"], SROW, setup_rows[:, :])
        dma("sync", "xs1", [], ["XS1"], WROW, wdw[:, :])
        dma("sync", "su3", [], ["SKROW"], SKROW[:], sinks[:, :])
        if "opad" not in SKIP:
            dma("gpsimd", "su4", [], ["OPAD"], OPAD[:], opad.rearrange("k p n -> p k n"))
        op("vector", "memset", [], ["CONSTS"], ap=CONSTS[:, 0:1], constant=EPS)
        op("vector", "memset", [], ["CONSTS"], ap=CONSTS[:, 1:2], constant=1.0)
        op("vector", "memset", [], ["DUMMY"], ap=DUMMY[:], constant=0.0)
        op("vector", "tensor_copy", ["CM"], ["IDB"], out=IDB[:], in_=CM[:, 0, :])
        op("vector", "tensor_copy", ["CM"], ["ONES"], out=ONES[:], in_=CM[:, 1, :])
        op("vector", "tensor_copy", ["CM"], ["ONESBD"], out=ONESBD[:], in_=CM[:, 2, :])
        op("vector", "tensor_copy", ["CM"], ["PROT"], out=PROT[:], in_=CM[:, 3, :])
        if "params" not in SKIP:
            tr(["XS0", "CM"], [pk(0)], pb[0][:, 0:38], SROW, CM[0:38, 0, 0:38])
            op("vector", "tensor_copy", [pk(0)], ["PARAMS"], out=PARAMS[:, 0:38], in_=pb[0][:, 0:38])
            op("vector", "tensor_scalar", ["PARAMS"], ["PARAMS"], out=PARAMS[:, PNG:PNG + 8], in0=PARAMS[:, PGCN:PGCN + 8],
               scalar1=-1.0, scalar2=None, op0=ALU.mult)
            for c in range(4):
                tr(["XS1", "CM"], [pk(1)], pb[1][:, c * 31:(c + 1) * 31], WROW[:, c * 128:(c + 1) * 128], CM[0:31, 0, 0:31])
            op("vector", "tensor_copy", [pk(1)], ["WDW"], out=WDW[:].rearrange("p c t -> p (c t)"), in_=pb[1][:, 0:124])
        if "esink" not in SKIP:
            act(["SKROW"], ["SKROW"], out=SKROW[:], in_=SKROW[:], func=AF.Exp)
            mm(["SKROW", "CM"], [pk(2)], pb[2][:, 0:8], CM[0:1, 1, :], SKROW[0:1, :])
            op("vector", "tensor_copy", [pk(2)], ["ESINK"], out=ESINK[0:64, :], in_=pb[2][0:64, 0:4])
            op("vector", "tensor_copy", [pk(2)], ["ESINK"], out=ESINK[64:128, :], in_=pb[2][64:128, 4:8])
        diag_todo = [(c, t) for c in range(4) for t in range(31)]

        def diag_some(n):
            for _ in range(n):
                if diag_todo:
                    c, t = diag_todo.pop(0)
                    op("vector", "tensor_scalar", ["WDW", "IDB"], ["DIAG"], out=DIAG[:, c, t, :], in0=IDB[:],
                       scalar1=WDW[:, c, t:t + 1], scalar2=None, op0=ALU.mult)

        def late_setup():
            diag_some(len(diag_todo))
            if "kct" not in SKIP:
                for s in range(4):
                    dma("sync", "xs0", [], ["XS0"], XS[0][:, s * 128:(s + 1) * 128], ck[s])
                for s in range(4):
                    tr(["XS0", "CM"], [pk(3)], pb[3][:, s * 128:(s + 1) * 128], XS[0][:, s * 128:(s + 1) * 128], IDF)
                op("vector", "memset", [], ["KCT"], ap=KCT[:].rearrange("p g s n -> p (g s n)"), constant=0.0)
                act([pk(3)], ["KCT"], out=KCT[0:64, 0, :, :], in_=pb[3][0:64, :].rearrange("p (s n) -> p s n", s=4), func=AF.Copy)
                act([pk(3)], ["KCT"], out=KCT[64:128, 1, :, :], in_=pb[3][64:128, :].rearrange("p (s n) -> p s n", s=4), func=AF.Copy)
            if "vc" not in SKIP:
                op("vector", "memset", [], ["VC"], ap=VC[:].rearrange("p s g n -> p (s g n)"), constant=0.0)
                op("vector", "memset", [], ["VS"], ap=VS[:].rearrange("p g n -> p (g n)"), constant=0.0)
                dma("gpsimd", "su5", [], ["SMASK"], SMASK[64:128, :], smask[:, :])
                op("vector", "memset", [], ["KST"], ap=KST[:].rearrange("p g n -> p (g n)"), constant=0.0)
                for s in range(4):
                    dma("sync", "xs0", [], ["XS0"], XS[0][:, s * 128:(s + 1) * 128], cv[s])
                vsrc = XS[0][:, 0:512].rearrange("p (s g n) -> p s g n", s=4, g=2)
                op("vector", "tensor_copy", ["XS0"], ["VC"], out=VC[:, :, :, 0:64], in_=vsrc)
                op("vector", "tensor_copy", ["XS0"], ["VC"], out=VC[:, :, :, 128:192], in_=vsrc)
            if "usamp" not in SKIP:
                op("vector", "memset", [], ["USAMP"], ap=USAMP[:].rearrange("p c s n -> p (c s n)"), constant=0.0)
                for s in range(4):
                    dma("sync", "xs1", [], ["XS1"], XS[1][0:30, 0:512], sconv[s])
                    for c in range(4):
                        tr(["XS1", "CM"], [pk(4)], pb[4][:, (s * 4 + c) * 30:(s * 4 + c + 1) * 30],
                           XS[1][0:30, c * 128:(c + 1) * 128], CM[0:30, 0, 0:30])
                    op("vector", "tensor_copy", [pk(4)], ["USAMP"], out=USAMP[:, :, s, 0:30],
                       in_=pb[4][:, s * 120:(s + 1) * 120].rearrange("p (c t) -> p c t", c=4))


        rcnt = [0]

        def ring_load(idx, ncols=2048):
            s = rcnt[0] % NSLOT
            rcnt[0] += 1
            dma("gpsimd", "ring%d" % s, [], ["ring%d" % s], ring[s][:, 0:ncols], wall[idx, :, 0:ncols])
            return s

        bcnt = {"ab": 0, "d": 0, "x": 0, "y": 0}

        def load_dma(row0, n):
            i = bcnt["x"]
            bcnt["x"] += 1
            sl = i % 2
            dma("sync", "xs%d" % sl, [], ["XS%d" % sl], XS[sl][0:n, :], xin[row0:row0 + n, :])
            return i

        def load_compute(i, n, col0):
            sl = i % 2
            xk = "XS%d" % sl
            bA, bB = (4, 5) if i % 2 == 0 else (6, 7)
            for c in range(8):
                bank = pb[bA] if c < 4 else pb[bB]
                tr([xk, "CM"], [pk(bA if c < 4 else bB)], bank[:, (c % 4) * 128:(c % 4) * 128 + n],
                   XS[sl][0:n, c * 128:(c + 1) * 128], CM[0:n, 0, 0:n])
            g = col0 // 128
            act([pk(bA)], ["RT%d.%d" % (c, g) for c in range(4)], out=RT[:, 0:4, col0:col0 + n],
                in_=pb[bA][:, :].rearrange("p (c n) -> p c n", c=4)[:, :, 0:n], func=AF.Copy)
            op("vector", "tensor_copy", [pk(bB)], ["RT%d.%d" % (c, g) for c in range(4, 8)],
               out=RT[:, 4:8, col0:col0 + n], in_=pb[bB][:, :].rearrange("p (c n) -> p c n", c=4)[:, :, 0:n])

        def load_group(row0, n, col0):
            load_compute(load_dma(row0, n), n, col0)

        def store_group(row0, n, col0):
            i = bcnt["y"]
            bcnt["y"] += 1
            bA, bB = (0, 1) if i % 2 == 0 else (2, 3)
            g = col0 // 128
            for c in range(8):
                bi = bA if c < 4 else bB
                tr(["RT%d.%d" % (c, g), "CM"], [pk(bi)], pb[bi][0:n, (c % 4) * 128:(c % 4 + 1) * 128],
                   RT[:, c, col0:col0 + n], IDF)
            ys, yk_ = YS[i % 2], "YS%d" % (i % 2)
            act([pk(bA)], [yk_], out=ys[0:n, 0:512], in_=pb[bA][0:n, :], func=AF.Copy)
            op("vector", "tensor_copy", [pk(bB)], [yk_], out=ys[0:n, 512:1024], in_=pb[bB][0:n, :])
            dma("sync", "ys%d" % (i % 2), [yk_], [], yout[row0:row0 + n, :], ys[0:n, :])

        def norm(blocks, gcol, extra_r=(), extra_w=()):
            for (c0, c1) in blocks:
                N = c1 - c0
                for c in range(8):
                    q = SQ[c % 2]
                    act(["RT%d.%d" % (c, g) for g in range(c0 // 128, (c1 - 1) // 128 + 1)], ["SQ%d" % (c % 2)],
                        out=q[:, 0:N], in_=RT[:, c, c0:c1], func=AF.Square, scale=1.0 / 32.0)
                    mm(["SQ%d" % (c % 2), "ONES"], [pk(7)], pb[7][:, 0:N], ONES[:], q[:, 0:N], start=(c == 0), stop=(c == 7))
                act([pk(7), "CONSTS"], ["RSTD"], out=RSTD[:, 0:N], in_=pb[7][:, 0:N], func=AF.Ln, bias=EPSC)
                act(["RSTD"], ["RSTD"], out=RSTD[:, 0:N], in_=RSTD[:, 0:N], func=AF.Exp, scale=-0.5)
                for c in range(8):
                    op("vector", "scalar_tensor_tensor",
                       ["RSTD", "PARAMS"] + gk("RT%d" % c, c0, c1) + list(extra_r),
                       gk("XN%d" % c, c0, c1) + list(extra_w),
                       out=XN[:, c, c0:c1], in0=RT[:, c, c0:c1], scalar=PARAMS[:, gcol + c:gcol + c + 1],
                       in1=RSTD[:, 0:N], op0=ALU.mult, op1=ALU.mult)

        def ff(blocks, wu, wd, scale, hook=None):
            for j in range(NJ):
                s = ring_load(wu + j)
                W = ring[s][:, :].rearrange("p (k n) -> p k n", k=8)
                rk = "ring%d" % s
                for (c0, c1) in blocks:
                    N = c1 - c0
                    i = bcnt["ab"]
                    bcnt["ab"] += 1
                    a, b = i % 2, 2 + i % 2
                    for kc in range(8):
                        mm([rk] + gk("XN%d" % kc, c0, c1), [pk(a)], pb[a][:, 0:N], W[:, kc, 0:128], XN[:, kc, c0:c1],
                           start=(kc == 0), stop=(kc == 7))
                    for kc in range(8):
                        mm([rk] + gk("XN%d" % kc, c0, c1), [pk(b)], pb[b][:, 0:N], W[:, kc, 128:256], XN[:, kc, c0:c1],
                           start=(kc == 0), stop=(kc == 7))
                    act([pk(a)], ["SA%d" % (i % 2)], out=SA[i % 2][:, 0:N], in_=pb[a][:, 0:N], func=AF.Silu)
                    op("vector", "tensor_tensor", [pk(b), "SA%d" % (i % 2)], gk("H%d" % j, c0, c1),
                       out=H[:, j, c0:c1], in0=pb[b][:, 0:N], in1=SA[i % 2][:, 0:N], op=ALU.mult)
                    if hook is not None:
                        hook()
            for m in range(8):
                s1 = ring_load(wd + 2 * m)
                s2 = ring_load(wd + 2 * m + 1, 768)
                W1 = ring[s1][:, :].rearrange("p (k n) -> p k n", k=16)
                W2 = ring[s2][:, 0:768].rearrange("p (k n) -> p k n", k=6)
                for (c0, c1) in blocks:
                    N = c1 - c0
                    i = bcnt["d"]
                    bcnt["d"] += 1
                    bi = 4 + i % 4
                    for j in range(NJ):
                        w = W1[:, j, :] if j < 16 else W2[:, j - 16, :]
                        mm(["ring%d" % (s1 if j < 16 else s2)] + gk("H%d" % j, c0, c1), [pk(bi)], pb[bi][:, 0:N], w,
                           H[:, j, c0:c1], start=(j == 0), stop=(j == NJ - 1))
                    op("vector", "scalar_tensor_tensor", [pk(bi)] + gk("RT%d" % m, c0, c1), gk("RT%d" % m, c0, c1),
                       out=RT[:, m, c0:c1], in0=pb[bi][:, 0:N], scalar=scale, in1=RT[:, m, c0:c1],
                       op0=ALU.mult, op1=ALU.add)

        ALLH = ["H%d.%d" % (j, g) for j in range(NJ) for g in range(10)]
        ALLXN = ["XN%d.%d" % (c, g) for c in range(8) for g in range(10)]

        def fence(writes):
            op("vector", "memset", [], list(writes) + ["DUMMY"], ap=DUMMY[:, 0:1], constant=0.0)

        def sigmoid_inplace(buf, key, N, r=(), scale_ap=None, bias_ap=None, src=None, src_keys=()):
            kw = {}
            if scale_ap is not None:
                kw = dict(scale=scale_ap, bias=bias_ap)
            else:
                kw = dict(scale=-1.0)
            act(list(src_keys) + [key] + list(r), [key], out=buf, in_=(src if src is not None else buf), func=AF.Exp, **kw)
            act([key, "CONSTS"] + list(r), [key], out=buf, in_=buf, func=AF.Ln, bias=ONEC[0:buf.shape[0], :])
            act([key] + list(r), [key], out=buf, in_=buf, func=AF.Exp, scale=-1.0)

        def qk_sets(ci):
            if ci == 0:
                return dict(QF=SF[0], RS=SF[1], T1=SF[2], T2=SF[3], kQF="SF0", kRS="SF1", kT1="SF2", kT2="SF3",
                            SQb=SB[0], QNb=SB[1], kSQ="SB0", kQN="SB1", b1=6, b2=7)
            return dict(QF=YF[:, 0, :], RS=YF[:, 1, :], T1=YF[:, 2, :], T2=YF[:, 3, :], kQF="YF0", kRS="YF1", kT1="YF2", kT2="YF3",
                        SQb=PT[:, 0, :], QNb=PT[:, 1, :], kSQ="PT0", kQN="PT1", b1=4, b2=5)

        def qk_s1(ch, R):
            z = qk_sets(ch["ci"])
            bi, N, gcol = ch["bank"], ch["N"], ch["gcol"]
            gs = PARAMS[:, gcol:gcol + 1]
            ch["proj"](bi)
            act([pk(bi), "PARAMS"] + R, [z["kQN"]], out=z["QNb"][:, 0:N], in_=pb[bi][:, 0:N], func=AF.Identity, scale=gs)
            act([pk(bi)] + R, [z["kSQ"]], out=z["SQb"][:, 0:N], in_=pb[bi][:, 0:N], func=AF.Square, scale=0.125)
            mm([z["kQN"], "PROT"] + R, [pk(z["b2"])], pb[z["b2"]][:, 0:N], PROT[:], z["QNb"][:, 0:N])
            mm([z["kSQ"], "ONESBD"] + R, [pk(z["b1"])], pb[z["b1"]][:, 0:N], ONESBD[:], z["SQb"][:, 0:N])

        def qk_s2(ch, R):
            z = qk_sets(ch["ci"])
            N, rc0, dest, dkeys = ch["N"], ch["rc0"], ch["dest"], ch["dkeys"]
            QF, RS, T2, b1, b2 = z["QF"], z["RS"], z["T2"], z["b1"], z["b2"]
            kQF, kRS, kT2 = z["kQF"], z["kRS"], z["kT2"]
            act([pk(b1), "CONSTS"] + R, [kRS], out=RS[:, 0:N], in_=pb[b1][:, 0:N], func=AF.Ln, bias=EPSC)
            act([kRS] + R, [kRS], out=RS[:, 0:N], in_=RS[:, 0:N], func=AF.Exp, scale=-0.5)
            op("vector", "tensor_tensor", [pk(b2), "ST"] + R, [kT2], out=T2[:, 0:N], in0=pb[b2][:, 0:N], in1=ST[:, rc0:rc0 + N],
               op=ALU.mult)
            bi = ch["bank"]
            op("vector", "scalar_tensor_tensor", [pk(bi), "CT", "PARAMS"] + R, [kQF], out=QF[:, 0:N], in0=pb[bi][:, 0:N],
               scalar=PARAMS[:, ch["gcol"]:ch["gcol"] + 1], in1=CT[:, rc0:rc0 + N], op0=ALU.mult, op1=ALU.mult)
            op("vector", "tensor_tensor", [kQF, kT2] + R, [kQF], out=QF[:, 0:N], in0=QF[:, 0:N], in1=T2[:, 0:N], op=ALU.add)
            if isinstance(dest, tuple):
                for hh_, dd in enumerate(dest):
                    op("vector", "tensor_tensor", [kQF, kRS] + R, list(dkeys), out=dd, in0=QF[hh_ * 64:(hh_ + 1) * 64, 0:N],
                       in1=RS[hh_ * 64:(hh_ + 1) * 64, 0:N], op=ALU.mult)
            else:
                op("vector", "tensor_tensor", [kQF, kRS] + R, list(dkeys), out=dest, in0=QF[:, 0:N], in1=RS[:, 0:N], op=ALU.mult)
            if ch.get("dest32") is not None:
                lo, hi, d32 = ch["dest32"]
                op("vector", "tensor_tensor", [kQF, kRS] + R, list(ch["d32keys"]), out=d32, in0=QF[:, lo:hi], in1=RS[:, lo:hi],
                   op=ALU.mult)

        def qk_run(chains, R):
            for i, ch in enumerate(chains):
                ch["ci"] = i % 2
                ch["bank"] = i % 4
            for i in range(len(chains) + 1):
                if i < len(chains):
                    qk_s1(chains[i], R)
                if i >= 1:
                    qk_s2(chains[i - 1], R)

        def ln_stats(sub, R):
            N = sub["N"]
            M1, MSQ, VR = SF[0], SF[1], SF[2]
            op("vector", "tensor_scalar", [pk(4)] + R, ["SF0"], out=M1[:, 0:N], in0=pb[4][:, 0:N], scalar1=1.0 / 512, scalar2=None,
               op0=ALU.mult)
            op("vector", "tensor_tensor", ["SF0"] + R, ["SF1"], out=MSQ[:, 0:N], in0=M1[:, 0:N], in1=M1[:, 0:N], op=ALU.mult)
            op("vector", "scalar_tensor_tensor", [pk(5), "SF1"] + R, ["SF2"], out=VR[:, 0:N], in0=pb[5][:, 0:N], scalar=1.0 / 512,
               in1=MSQ[:, 0:N], op0=ALU.mult, op1=ALU.subtract)
            act(["SF2", "CONSTS"] + R, ["SF2"], out=VR[:, 0:N], in_=VR[:, 0:N], func=AF.Ln, bias=EPSC)
            act(["SF2"] + R, ["SF2"], out=VR[:, 0:N], in_=VR[:, 0:N], func=AF.Exp, scale=-0.5)

        def ln_parts(sub, c, R, RX):
            N, h, mixc0 = sub["N"], sub["h"], sub["mixc0"]
            M1, VR = SF[0], SF[2]
            eh = c % 2
            E = SF[3][:, eh * 256:eh * 256 + N]
            ek = "SF3.%d" % eh
            yv = YF[:, c, h * 256:h * 256 + N]
            yk = "YF%d.%d" % (c, h)

            def A():
                op("vector", "tensor_tensor", [yk, "SF0"] + R, [yk], out=yv, in0=yv, in1=M1[:, 0:N], op=ALU.subtract)
                op("vector", "scalar_tensor_tensor", [yk, "SF2", "PARAMS"] + R, [yk], out=yv, in0=yv,
                   scalar=PARAMS[:, PGCN + c:PGCN + c + 1], in1=VR[:, 0:N], op0=ALU.mult, op1=ALU.mult)

            def B():
                sigmoid_inplace(E, ek, N, r=R + ["PARAMS", "SF3"], scale_ap=-1.0,
                                bias_ap=PARAMS[:, PNB + c:PNB + c + 1], src=yv, src_keys=[yk])

            def C():
                op("vector", "scalar_tensor_tensor", [yk, ek, "SF3", "PARAMS"] + R + RX,
                   ["MIX%d.%d" % (c, g) for g in range(mixc0 // 128, (mixc0 + N - 1) // 128 + 1)],
                   out=MIX[:, c, mixc0:mixc0 + N], in0=yv, scalar=PARAMS[:, PBCN + c:PBCN + c + 1], in1=E,
                   op0=ALU.add, op1=ALU.mult)
            return A, B, C

        conv_pending = []

        def conv_flush():
            for f in conv_pending:
                f()
            del conv_pending[:]

        def conv_chunk(sub, c, R):
            N, h, rhs_fn, ukeys = sub["N"], sub["h"], sub["rhs_fn"], sub["ukeys"]
            i = bcnt["ab"]
            bcnt["ab"] += 1
            yb = i % 2
            for t in range(31):
                mm(["DIAG"] + ukeys(c) + R, [pk(yb)], pb[yb][:, 0:N], DIAG[:, c, t, :], rhs_fn(c, t), start=(t == 0), stop=(t == 30))
            conv_flush()
            yv = YF[:, c, h * 256:h * 256 + N]
            yk = "YF%d.%d" % (c, h)
            act([pk(yb), "PARAMS", "YF%d" % c] + R, [yk], out=yv, in_=pb[yb][:, 0:N], func=AF.Identity,
                bias=PARAMS[:, PBDW + c:PBDW + c + 1])
            op("vector", "tensor_copy", [yk] + R, ["SB%d" % yb], out=SB[yb][:, 0:N], in_=yv)
            act([yk] + R, ["SB2"], out=SB[2][:, 0:N], in_=yv, func=AF.Square)
            def stats(yb=yb, c=c, N=N):
                mm(["SB%d" % yb, "ONES"] + R, [pk(4)], pb[4][:, 0:N], ONES[:], SB[yb][:, 0:N], start=(c == 0), stop=(c == 3))
                mm(["SB2", "ONES"] + R, [pk(5)], pb[5][:, 0:N], ONES[:], SB[2][:, 0:N], start=(c == 0), stop=(c == 3))
            conv_pending.append(stats)

        def conv_run(subs, R, RX, tail=None):
            for i in range(len(subs) + 1):
                cur = subs[i] if i < len(subs) else None
                prev = subs[i - 1] if i >= 1 else None
                if prev is not None:
                    conv_flush()
                    ln_stats(prev, R)
                pendC = None
                for c in range(4):
                    if cur is not None:
                        conv_chunk(cur, c, R)
                    elif tail is not None:
                        tail(c, "pre")
                    if prev is not None:
                        A, B, C = ln_parts(prev, c, R, RX)
                        A()
                        B()
                        if pendC is not None:
                            pendC()
                        pendC = C
                    if cur is None and tail is not None:
                        tail(c, "post")
                if pendC is not None:
                    pendC()

        def mixer(t):
            first, last = (t == 0), (t == NT - 1)
            main_blocks = [(0, 512), (512, 1024)]
            all_blocks = main_blocks + ([(HALO0, SAMP0), (SAMP0, TC)] if first else [])
            R = ["HF"]
            RX = ["XF"]
            fence(ALLH + ["HF"])
            dma("gpsimd", "rope0", [], ["CT"], CT[:, 0:TM], rope[0, :, t * TM:(t + 1) * TM])
            dma("gpsimd", "rope1", [], ["ST"], ST[:, 0:TM], rope[1, :, t * TM:(t + 1) * TM])
            if first:
                dma("gpsimd", "rope0", [], ["CT"], CT[:, TM:TC], rope[0, :, HALF:NROWS])
                dma("gpsimd", "rope1", [], ["ST"], ST[:, TM:TC], rope[1, :, HALF:NROWS])
            norm([(0, 512), (512, 1024)] + ([(HALO0, TC)] if first else []), PGM)
            op("vector", "memset", R, ["VP"], ap=VP[:].rearrange("p k g n -> p (k g n)"), constant=0.0)
            op("vector", "memset", R, ["KT.%d" % g for g in range(9)], ap=KT[:].rearrange("p g n -> p (g n)"), constant=0.0)
            if not first:
                op("vector", "tensor_copy", ["UCAR"] + R, ["Upre"], out=U[:, :, 0:32], in_=UCAR[:])
                op("vector", "tensor_copy", ["KCAR"] + R, ["KT.0"], out=KT[:, :, 0:128], in_=KCAR[:])
                op("vector", "tensor_copy", ["VCAR"] + R, ["VP"], out=VP[:, 0, :, :], in_=VCAR[:])
            for i in range(4):
                s = ring_load(W_IN + i)
                W = ring[s][:, :].rearrange("p (k n) -> p k n", k=8)
                rk = "ring%d" % s
                for (c0, c1) in all_blocks:
                    N = c1 - c0
                    k = bcnt["ab"]
                    bcnt["ab"] += 1
                    a, b = k % 2, 2 + k % 2
                    for kc in range(8):
                        mm([rk] + gk("XN%d" % kc, c0, c1) + R, [pk(a)], pb[a][:, 0:N], W[:, kc, 0:128], XN[:, kc, c0:c1],
                           start=(kc == 0), stop=(kc == 7))
                    for kc in range(8):
                        mm([rk] + gk("XN%d" % kc, c0, c1) + R, [pk(b)], pb[b][:, 0:N], W[:, kc, 128:256], XN[:, kc, c0:c1],
                           start=(kc == 0), stop=(kc == 7))
                    E = SF[3]
                    sigmoid_inplace(E[:, 0:N], "SF3", N, r=R, src=pb[b][:, 0:N], src_keys=[pk(b)])
                    if c0 < 1024:
                        op("vector", "tensor_tensor", [pk(a), "SF3"] + R, ["U%d.%d" % (i, g) for g in range(c0 // 128, c1 // 128)],
                           out=U[:, i, 32 + c0:32 + c1], in0=pb[a][:, 0:N], in1=E[:, 0:N], op=ALU.mult)
                        if last and c1 == 1024:
                            op("vector", "tensor_tensor", [pk(a), "SF3"] + R, ["U32M"], out=U32M[:, i, :], in0=pb[a][:, N - 32:N],
                               in1=E[:, N - 32:N], op=ALU.mult)
                    elif c0 == HALO0:
                        op("vector", "tensor_tensor", [pk(a), "SF3"] + R, ["Upre"], out=U[:, i, 0:32], in0=pb[a][:, 96:128],
                           in1=E[:, 96:128], op=ALU.mult)
                    else:
                        op("vector", "tensor_tensor", [pk(a), "SF3"] + R, ["USAMP"], out=USAMP[:, i, 0:4, 30:46],
                           in0=pb[a][:, 0:64].rearrange("p (s n) -> p s n", s=4), in1=E[:, 0:64].rearrange("p (s n) -> p s n", s=4),
                           op=ALU.mult)
                        op("vector", "tensor_tensor", [pk(a), "SF3"] + R, ["USAMP32"], out=USAMP32[:, i, :], in0=pb[a][:, 0:64],
                           in1=E[:, 0:64], op=ALU.mult)
            if MIXL < 2:
                fence(ALLXN + ["XF"]); fence(ALLH + ["HF"]); return
            chains = []

            def mkproj(rk, W, col0, c0, c1):
                def proj(bi):
                    for kc in range(8):
                        mm([rk] + gk("XN%d" % kc, c0, c1) + R, [pk(bi)], pb[bi][:, 0:c1 - c0], W[:, kc, col0:col0 + 128],
                           XN[:, kc, c0:c1], start=(kc == 0), stop=(kc == 7))
                return proj

            for i in range(2):
                s = ring_load(W_IN + 4 + i)
                W = ring[s][:, :].rearrange("p (k n) -> p k n", k=8)
                rk = "ring%d" % s
                for (c0, c1) in main_blocks + ([(SAMP0, TC)] if first else []):
                    N = c1 - c0
                    qc0 = c0 if c0 < 1024 else 1024
                    for hh in range(2):
                        j = 2 * i + hh
                        chains.append(dict(proj=mkproj(rk, W, hh * 128, c0, c1), N=N, gcol=PGQ, rc0=c0, dest=Q[:, j, qc0:qc0 + N],
                                           dkeys=["Q%d.%d" % (j, g) for g in range(qc0 // 128, (qc0 + N - 1) // 128 + 1)]))
            s = ring_load(W_IN + 6)
            W = ring[s][:, :].rearrange("p (k n) -> p k n", k=8)
            rk = "ring%d" % s
            for (c0, c1) in all_blocks:
                N = c1 - c0
                ch = dict(proj=mkproj(rk, W, 0, c0, c1), N=N, gcol=PGK, rc0=c0)
                if c0 < 1024:
                    ch.update(dest=(KT[0:64, 0, 128 + c0:128 + c1], KT[64:128, 1, 128 + c0:128 + c1]),
                              dkeys=["KT.%d" % g for g in range(1 + c0 // 128, 1 + c1 // 128)],
                              dest32=((N - 128, N, K32[:, 0:128]) if (last and c1 == 1024) else None), d32keys=["K32a"])
                elif c0 == HALO0:
                    ch.update(dest=(KT[0:64, 0, 0:128], KT[64:128, 1, 0:128]), dkeys=["KT.0"])
                else:
                    ch.update(dest=(KST[0:64, 0, 64:128], KST[64:128, 1, 64:128]), dkeys=["KST"], dest32=(0, 64, K32[:, 128:192]),
                              d32keys=["K32b"])
                chains.append(ch)
            qk_run(chains, R)
            vgroups = [(g * 128, g + 1) for g in range(8)] + ([(HALO0, 0)] if first else [])
            for (c0, kg) in ([] if "nov" in SKIP else vgroups):
                k = bcnt["ab"]
                bcnt["ab"] += 1
                a = k % 4
                for kc in range(8):
                    mm([rk] + gk("XN%d" % kc, c0, c0 + 128) + R, [pk(a)], pb[a][:, 0:128], XN[:, kc, c0:c0 + 128], W[:, kc, 128:256],
                       start=(kc == 0), stop=(kc == 7))
                pv = pb[a][:, 0:128].rearrange("p (g n) -> p g n", g=2)
                act([pk(a)] + R, ["VP"], out=VP[:, kg, :, 0:64], in_=pv, func=AF.Copy)
                op("vector", "tensor_copy", [pk(a)] + R, ["VP"], out=VP[:, kg, :, 128:192], in_=pv)
                if last and kg == 8:
                    op("vector", "tensor_copy", [pk(a)] + R, ["V32L"], out=V32L[:], in_=pb[a][:, 0:128])
            if first and "novs" not in SKIP:
                k = bcnt["ab"]
                bcnt["ab"] += 1
                a = k % 4
                for kc in range(8):
                    mm([rk] + gk("XN%d" % kc, 1088, TC) + R, [pk(a)], pb[a][:, 0:128], XN[:, kc, 1088:TC],
                       W[:, kc, 128:256], start=(kc == 0), stop=(kc == 7))
                pv = pb[a][64:128, 0:128].rearrange("p (g n) -> p g n", g=2)
                if "vsA" not in SKIP:
                    act([pk(a)] + R, ["VS"], out=VS[64:128, :, 0:64], in_=pv, func=AF.Copy)
                if "vsB" not in SKIP:
                    op("vector", "tensor_copy", [pk(a)] + R, ["VS"], out=VS[64:128, :, 128:192], in_=pv)
                if "vsC" not in SKIP:
                    op("vector", "tensor_copy", [pk(a)] + R, ["V32S"], out=V32S[64:128, :], in_=pb[a][64:128, 0:128])
            if MIXL < 4:
                fence(ALLXN + ["XF"]); fence(ALLH + ["HF"]); return
            fence(ALLXN + ["XF"])
            if not last:
                op("vector", "tensor_copy", ["U%d.7" % c for c in range(4)] + R, ["UCAR"], out=UCAR[:], in_=U[:, :, 1024:1056])
                op("vector", "tensor_copy", ["KT.8"] + R, ["KCAR"], out=KCAR[:], in_=KT[:, :, 1024:1152])
                op("vector", "tensor_copy", ["VP"] + R, ["VCAR"], out=VCAR[:], in_=VP[:, 8, :, :])

            def att_scores(G):
                PTb, pkey = (PT, "PT%d") if G % 2 == 0 else (PT2, "PT2.%d")
                for kgi in range(2):
                    kg = G + kgi
                    for g in range(2):
                        bi = kgi * 2 + g
                        mm(["KT.%d" % kg] + ["Q%d.%d" % (j, G) for j in range(4)] + R, [pk(bi)], pb[bi][:, :],
                           KT[:, g, kg * 128:(kg + 1) * 128], Q[:, :, G * 128:(G + 1) * 128])
                        part = (0, 64) if kgi == 0 else (64, 128)
                        qmask = 1 if kgi == 0 else 0
                        act([pk(bi)] + R, [pkey % bi], out=PTb[:, bi, :], in_=pb[bi][:, :], func=AF.Exp, scale=0.125)
                        op("vector", "memset", R, [pkey % bi],
                           ap=PTb[part[0]:part[1], bi, :].rearrange("p (j q n) -> p j q n", j=4, q=2)[:, :, qmask, :], constant=0.0)

            def att_pv(G):
                PTb, pkey = (PT, "PT%d") if G % 2 == 0 else (PT2, "PT2.%d")
                bn, bd = (6, 7) if G % 2 == 0 else (4, 5)
                n_ = 0
                for g in range(2):
                    for kgi in range(2):
                        kg = G + kgi
                        bi = kgi * 2 + g
                        sel = slice(0, 128) if g == 0 else slice(64, 192)
                        rhs = PTb[:, bi, :]
                        mm(["VP", pkey % bi] + R, [pk(bn)], pb[bn][:, :], VP[:, kg, g, sel], rhs, start=(n_ == 0), stop=(n_ == 3))
                        ow = OPAD[:, (1 if (first and kg == 0) else 0), sel]
                        mm(["OPAD", pkey % bi] + R, [pk(bd)], pb[bd][:, :], ow, rhs, start=(n_ == 0), stop=(n_ == 3))
                        n_ += 1

            def att_norm(G):
                bn, bd = (6, 7) if G % 2 == 0 else (4, 5)
                Rr, rkey = (SF[1], "SF1") if G % 2 == 0 else (SF[0], "SF0")
                for j in range(4):
                    act([pk(bd), "ESINK"] + R, [rkey], out=Rr[:, j * 128:(j + 1) * 128], in_=pb[bd][:, j * 128:(j + 1) * 128],
                        func=AF.Ln, bias=ESINK[:, j:j + 1])
                act([rkey] + R, [rkey], out=Rr[:, :], in_=Rr[:, :], func=AF.Exp, scale=-1.0)
                op("vector", "tensor_tensor", [pk(bn), rkey] + R + RX, ["MIX%d.%d" % (4 + j, G) for j in range(4)],
                   out=MIX[:, 4:8, G * 128:(G + 1) * 128], in0=pb[bn][:, :].rearrange("p (j n) -> p j n", j=4),
                   in1=Rr[:, :].rearrange("p (j n) -> p j n", j=4), op=ALU.mult)

            subs = []
            for c0 in range(0, 1024, 256):
                subs.append(dict(N=256, h=len(subs) % 2, mixc0=c0,
                                 rhs_fn=(lambda c, tt, c0=c0: U[:, c, c0 + 2 + tt:c0 + 2 + tt + 256]),
                                 ukeys=(lambda c, c0=c0: ["Upre"] + ["U%d.%d" % (c, g) for g in range(max(0, c0 // 128 - 1), (c0 + 256) // 128)])))
            if first:
                subs.append(dict(N=128, h=len(subs) % 2, mixc0=1024, rhs_fn=(lambda c, tt: USAMP[:, c, :, tt:tt + 16]),
                                 ukeys=(lambda c: ["USAMP"])))
            op("vector", "memset", R, ["PT%d" % i for i in range(4)], ap=PT[:].rearrange("p c n -> p (c n)"), constant=0.0)

            def att_tail(c, when):
                if c == 0 and when == "pre":
                    att_scores(0)
                elif c == 2 and when == "post":
                    op("vector", "memset", R, ["YF0", "YF1"] + ["YF%d.%d" % (i, hh) for i in range(2) for hh in range(2)] + ["PT2.%d" % i for i in range(4)],
                       ap=PT2[:].rearrange("p c n -> p (c n)"), constant=0.0)
                    att_scores(1)
                elif c == 3 and when == "post":
                    att_pv(0)

            conv_run(subs, R, RX, tail=att_tail)
            if MIXL < 5:
                fence(ALLXN + ["XF"]); fence(ALLH + ["HF"]); return
            for G in range(1, 8):
                if G + 1 < 8:
                    att_scores(G + 1)
                att_pv(G)
                att_norm(G - 1)
            att_norm(7)
            if MIXL < 6:
                fence(ALLXN + ["XF"]); fence(ALLH + ["HF"]); return
            if first:
                for s4 in range(4):
                    for g in range(2):
                        oc = (s4 * 2 + g) * 64
                        qv = Q[:, :, 1024 + s4 * 16:1024 + s4 * 16 + 16]
                        mm(["KCT"] + ["Q%d.8" % j for j in range(4)] + R, [pk(0)], pb[0][:, oc:oc + 64], KCT[:, g, s4, :], qv)
                for g in range(2):
                    mm(["KST"] + ["Q%d.8" % j for j in range(4)] + R, [pk(1)], pb[1][:, g * 256:(g + 1) * 256],
                       KST[:, g, 0:128], Q[:, :, 1024:1088])
                act([pk(0)] + R, ["PT0"], out=PT[:, 0, :], in_=pb[0][:, :], func=AF.Exp, scale=0.125)
                op("vector", "memset", R, ["PT1"], ap=PT[0:64, 1, :], constant=0.0)
                act([pk(1)] + R, ["PT1"], out=PT[64:128, 1, :], in_=pb[1][64:128, :], func=AF.Exp, scale=0.125)
                for g in range(2):
                    op("vector", "tensor_tensor", ["PT1", "SMASK"] + R, ["PT1"], out=PT[64:128, 1, g * 256:(g + 1) * 256],
                       in0=PT[64:128, 1, g * 256:(g + 1) * 256], in1=SMASK[64:128, :], op=ALU.mult)
                for j in range(4):
                    for s4 in range(4):
                        ocol = slice(j * 64 + s4 * 16, j * 64 + s4 * 16 + 16)
                        for g in range(2):
                            sel = slice(0, 128) if g == 0 else slice(64, 192)
                            pc = (s4 * 2 + g) * 64 + j * 16
                            mm(["VC", "PT0"] + R, [pk(6)], pb[6][:, ocol], VC[:, s4, g, sel], PT[:, 0, pc:pc + 16], start=(g == 0), stop=(g == 1))
                            mm(["OPAD", "PT0"] + R, [pk(7)], pb[7][:, ocol], OPAD[:, 0, sel], PT[:, 0, pc:pc + 16], start=(g == 0), stop=(g == 1))
                    for g in range(2):
                        sel = slice(0, 128) if g == 0 else slice(64, 192)
                        pc = g * 256 + j * 64
                        mm(["VS", "PT1"] + R, [pk(4)], pb[4][:, j * 64:(j + 1) * 64], VS[:, g, sel], PT[:, 1, pc:pc + 64], start=(g == 0), stop=(g == 1))
                        mm(["OPAD", "PT1"] + R, [pk(5)], pb[5][:, j * 64:(j + 1) * 64], OPAD[:, 0, sel], PT[:, 1, pc:pc + 64], start=(g == 0), stop=(g == 1))
                Rr, T1, NS, T3 = SF[0], SF[1], SF[2], SF[3]
                op("vector", "tensor_copy", [pk(4)] + R, ["SF1"], out=T1[:, 0:256], in_=pb[4][:, 0:256])
                op("vector", "tensor_tensor", [pk(6), "SF1"] + R, ["SF2"], out=NS[:, 0:256], in0=pb[6][:, 0:256], in1=T1[:, 0:256], op=ALU.add)
                op("vector", "tensor_copy", [pk(5)] + R, ["SF3"], out=T3[:, 0:256], in_=pb[5][:, 0:256])
                op("vector", "tensor_tensor", [pk(7), "SF3"] + R, ["SF0"], out=Rr[:, 0:256], in0=pb[7][:, 0:256], in1=T3[:, 0:256], op=ALU.add)
                for j in range(4):
                    op("vector", "tensor_scalar", ["SF0", "ESINK"] + R, ["SF0"], out=Rr[:, j * 64:(j + 1) * 64],
                       in0=Rr[:, j * 64:(j + 1) * 64], scalar1=ESINK[:, j:j + 1], scalar2=None, op0=ALU.add)
                act(["SF0"] + R, ["SF0"], out=Rr[:, 0:256], in_=Rr[:, 0:256], func=AF.Ln)
                act(["SF0"] + R, ["SF0"], out=Rr[:, 0:256], in_=Rr[:, 0:256], func=AF.Exp, scale=-1.0)
                op("vector", "tensor_tensor", ["SF2", "SF0"] + R + RX, ["MIX%d.8" % (4 + j) for j in range(4)],
                   out=MIX[:, 4:8, 1024:1088], in0=NS[:, 0:256].rearrange("p (j n) -> p j n", j=4),
                   in1=Rr[:, 0:256].rearrange("p (j n) -> p j n", j=4), op=ALU.mult)
            if MIXL < 7:
                fence(ALLXN + ["XF"]); fence(ALLH + ["HF"]); return
            wo = []
            for n in range(4):
                s = ring_load(W_OUT + n)
                wo.append((ring[s][:, :].rearrange("p (k n) -> p k n", k=8), "ring%d" % s))
            for (c0, c1) in main_blocks + ([(SAMP0, TC)] if first else []):
                N = c1 - c0
                mc0 = c0 if c0 < 1024 else 1024
                for n in range(4):
                    W, rk = wo[n]
                    for mi in range(2):
                        m = 2 * n + mi
                        i = bcnt["d"]
                        bcnt["d"] += 1
                        bi = i % 4
                        for kc in range(8):
                            mm([rk] + ["MIX%d.%d" % (kc, g) for g in range(mc0 // 128, (mc0 + N - 1) // 128 + 1)] + R + RX, [pk(bi)],
                               pb[bi][:, 0:N], W[:, kc, mi * 128:(mi + 1) * 128], MIX[:, kc, mc0:mc0 + N], start=(kc == 0), stop=(kc == 7))
                        op("vector", "tensor_tensor", [pk(bi)] + gk("RT%d" % m, c0, c1), gk("RT%d" % m, c0, c1),
                           out=RT[:, m, c0:c1], in0=pb[bi][:, 0:N], in1=RT[:, m, c0:c1], op=ALU.add)
            if MIXL < 8:
                fence(ALLXN + ["XF"]); fence(ALLH + ["HF"]); return
            if first:
                for c in range(4):
                    tr(["USAMP32", "CM"] + R, [pk(0)], pb[0][0:64, c * 128:(c + 1) * 128], USAMP32[:, c, :], IDF)
                op("vector", "tensor_copy", [pk(0)], ["OST"], out=OST[0:64, :], in_=pb[0][0:64, :])
                for s4 in range(4):
                    dma("sync", "o0", ["OST"], [], ocs_s[s4, 14:30, :], OST[s4 * 16:(s4 + 1) * 16, :])
                tr(["K32b", "CM"], [pk(1)], pb[1][0:64, 0:128], K32[:, 128:192], IDF)
                op("vector", "tensor_copy", [pk(1)], ["V32L"], out=V32L[0:64, :], in_=pb[1][0:64, 0:128])
                for s4 in range(4):
                    dma("sync", "o1", ["V32L"], [], okw_s[s4, 112:128, :], V32L[s4 * 16:(s4 + 1) * 16, :])
                    dma("sync", "o2", ["V32S"], [], ovw_s[s4, 112:128, :], V32S[64 + s4 * 16:64 + (s4 + 1) * 16, :])
            if last:
                for c in range(4):
                    tr(["U32M", "CM"], [pk(0)], pb[0][0:32, c * 128:(c + 1) * 128], U32M[:, c, :], IDF)
                op("vector", "tensor_copy", [pk(0)], ["OST"], out=OST[0:32, :], in_=pb[0][0:32, :])
                dma("sync", "o0", ["OST"], [], ocs_p[:, :], OST[2:32, :])
                tr(["K32a", "CM"], [pk(1)], pb[1][:, 0:128], K32[:, 0:128], IDF)
                op("vector", "tensor_copy", [pk(1)], ["USAMP32"], out=USAMP32[:, 0:2, :].rearrange("p a n -> p (a n)"), in_=pb[1][:, 0:128])
                dma("sync", "o1", ["USAMP32"], [], okw_p[:, :], USAMP32[:, 0:2, :].rearrange("p a n -> p (a n)"))
                dma("sync", "o2", ["V32L"], [], ovw_p[:, :], V32L[:, :])
            fence(ALLXN + ["XF"])
            fence(ALLH + ["HF"])

        for t in range(NTILES):
            first = (t == 0)
            if first:
                hs = [load_dma(0, 128), load_dma(128, 128)]
                for g in range(8):
                    load_compute(hs[g], 128, g * 128)
                    if g + 2 < 8:
                        hs.append(load_dma((g + 2) * 128, 128))
            if first:
                load_group(HALF, 128, HALO0)
                load_group(HALF + 128, 64, SAMP0)
            if first and "pass" not in SKIP:
                for s in range(4):
                    dma("sync", "o0", [], [], ocs_s[s, 0:14, :], sconv[s, 16:30, :])
                    dma("sync", "o1", [], [], okw_s[s, 0:112, :], ck[s, 16:128, :])
                    dma("sync", "o2", [], [], ovw_s[s, 0:112, :], cv[s, 16:128, :])
            b1 = [(0, 512), (512, 1024)] + ([(HALO0, TC)] if first else [])
            b2 = [(0, 512), (512, 1024)] + ([(SAMP0, TC)] if first else [])
            if LEVEL >= 1:
                norm(b1, PG1)
            if LEVEL >= 2:
                ff(b1, W_FF1U, W_FF1D, 0.5, hook=((lambda: diag_some(2)) if first else None))
            if first:
                late_setup()
            if LEVEL >= 3:
                mixer(t)
            if LEVEL >= 4:
                norm(b2, PG2)
                ff(b2, W_FF2U, W_FF2D, 0.5)
            if first:
                store_group(HALF, 64, SAMP0)
            nxt = t + 1 < NTILES
            hs = [load_dma((t + 1) * TM, 128), load_dma((t + 1) * TM + 128, 128)] if nxt else []
            for g in range(8):
                store_group(t * TM + g * 128, 128, g * 128)
                if nxt:
                    load_compute(hs[g], 128, g * 128)
                    if g + 2 < 8:
                        hs.append(load_dma((t + 1) * TM + (g + 2) * 128, 128))
        print("sbuf remaining", nc.sbuf_bytes_remaining, "ops", len(P.ops))
        counts = P.emit(nc, st)
        print(counts)
    return nc, counts


_CACHE = {}


def _pack_weights(w_ff1_in, w_ff1_out, w_in, w_out, w_ff2_in, w_ff2_out):
    wall = np.zeros((NCHUNK, 128, 2048), np.float32)

    def up(w, base):
        w4 = w.reshape(8, 128, 2, NJ, 128)
        wall[base:base + NJ] = w4.transpose(3, 1, 0, 2, 4).reshape(NJ, 128, 2048)

    def down(w, base):
        w4 = w.reshape(NJ, 128, 8, 128)
        for m in range(8):
            wall[base + 2 * m] = w4[0:16, :, m, :].transpose(1, 0, 2).reshape(128, 2048)
            wall[base + 2 * m + 1, :, 0:768] = w4[16:22, :, m, :].transpose(1, 0, 2).reshape(128, 768)

    up(w_ff1_in, W_FF1U)
    down(w_ff1_out, W_FF1D)
    up(w_ff2_in, W_FF2U)
    down(w_ff2_out, W_FF2D)
    wi = w_in.reshape(8, 128, 1792)
    for i in range(4):
        blk = np.concatenate([wi[:, :, i * 128:(i + 1) * 128], wi[:, :, 512 + i * 128:512 + (i + 1) * 128]], axis=2)
        wall[W_IN + i] = blk.transpose(1, 0, 2).reshape(128, 2048)
    qcols = []
    for j in range(4):
        qcols += list(range(1024 + j * 64, 1024 + (j + 1) * 64)) + list(range(1024 + (4 + j) * 64, 1024 + (5 + j) * 64))
    qcols = np.array(qcols)
    for i in range(2):
        blk = wi[:, :, qcols[i * 256:(i + 1) * 256]]
        wall[W_IN + 4 + i] = blk.transpose(1, 0, 2).reshape(128, 2048)
    wall[W_IN + 6] = wi[:, :, 1536:1792].transpose(1, 0, 2).reshape(128, 2048)
    rows = list(range(512))
    for j in range(4):
        rows += list(range(512 + j * 64, 512 + (j + 1) * 64)) + list(range(512 + (4 + j) * 64, 512 + (5 + j) * 64))
    wo = w_out[np.array(rows)].reshape(8, 128, 1024)
    for n in range(4):
        wall[W_OUT + n] = wo[:, :, n * 256:(n + 1) * 256].transpose(1, 0, 2).reshape(128, 2048)
    return wall


def _rope_tables(base):
    half = 8
    inv = np.power(np.float32(500000.0), -np.arange(half, dtype=np.float32) / np.float32(half)).astype(np.float32)
    pos = np.concatenate([base + np.arange(HALF), base - 128 + np.arange(128), 4096 + (np.arange(64) % 16)]).astype(np.float32)
    ang = pos[None, :] * inv[:, None]
    cos = np.cos(ang).astype(np.float32)
    sin = np.sin(ang).astype(np.float32)
    tab = np.zeros((2, 128, NROWS), np.float32)
    tab[0] = 1.0
    for h in range(2):
        tab[0, h * 64:h * 64 + 8] = cos
        tab[0, h * 64 + 8:h * 64 + 16] = cos
        tab[1, h * 64:h * 64 + 8] = sin
        tab[1, h * 64 + 8:h * 64 + 16] = sin
    return tab


def kernel(x_prompt, x_sample, state_conv, cache_k_win, cache_v_win, g_ff1, w_ff1_in, w_ff1_out, g_mix, w_in, g_q, g_k,
           sinks, w_dw, b_dw, g_cn, b_cn, w_out, g_ff2, w_ff2_in, w_ff2_out):
    f = lambda a: np.ascontiguousarray(np.asarray(a, dtype=np.float32))
    x_prompt, x_sample, state_conv, cache_k_win, cache_v_win = map(f, (x_prompt, x_sample, state_conv, cache_k_win, cache_v_win))
    if "nc" not in _CACHE:
        _CACHE["nc"] = build_nc()[0]
    nc = _CACHE["nc"]
    wall = _pack_weights(f(w_ff1_in)[0], f(w_ff1_out)[0], f(w_in)[0], f(w_out)[0], f(w_ff2_in)[0], f(w_ff2_out)[0])
    gq, gk_ = f(g_q)[0], f(g_k)[0]
    setup_rows = np.concatenate([f(g_ff1)[0].reshape(8, 128), f(g_mix)[0].reshape(8, 128), f(g_ff2)[0].reshape(8, 128),
                                 f(b_dw)[0].reshape(4, 128), f(g_cn)[0].reshape(4, 128), f(b_cn)[0].reshape(4, 128),
                                 np.concatenate([gq, gq])[None], np.concatenate([gk_, gk_])[None]], axis=0)
    cmat = np.zeros((4, 128, 128), np.float32)
    cmat[0] = np.eye(128)
    cmat[1] = 1.0
    cmat[2, 0:64, 0:64] = 1.0
    cmat[2, 64:128, 64:128] = 1.0
    for h in range(2):
        for m in range(8):
            cmat[3, h * 64 + m + 8, h * 64 + m] = -1.0
            cmat[3, h * 64 + m, h * 64 + m + 8] = 1.0
    opad1 = np.zeros((128, 192), np.float32)
    opad1[:, 0:64] = 1.0
    opad1[:, 128:192] = 1.0
    smask = np.zeros((64, 256), np.float32)
    for s_ in range(4):
        for j in range(4):
            smask[s_ * 16:(s_ + 1) * 16, j * 64 + s_ * 16:j * 64 + (s_ + 1) * 16] = 1.0
    in_maps = []
    for c in range(8):
        b, hf = c // 2, c % 2
        base = hf * HALF
        xin = np.zeros((NROWS, D), np.float32)
        xin[0:HALF] = x_prompt[b, base:base + HALF]
        if hf == 1:
            xin[HALF:HALF + 128] = x_prompt[b, base - 128:base]
        xin[HALF + 128:] = x_sample[4 * c:4 * c + 4].reshape(64, D)
        opad = np.stack([opad1, opad1 * np.float32(hf)])
        in_maps.append(dict(
            xin=xin, wall=wall, rope=_rope_tables(base), setup_rows=setup_rows, wdw=f(w_dw)[0], sinks=f(sinks),
            cmat=cmat, opad=opad, smask=smask, sconv=state_conv[0, 4 * c:4 * c + 4], ck=cache_k_win[0, 4 * c:4 * c + 4].reshape(4, 128, 128),
            cv=cache_v_win[0, 4 * c:4 * c + 4].reshape(4, 128, 128)))
    res = run_bass_kernel_spmd(nc, in_maps, core_ids=list(range(8)))
    r = res.results
    y_prompt = np.zeros((4, SEQ, D), np.float32)
    y_sample = np.zeros((32, 16, D), np.float32)
    csp = np.zeros((1, 4, 30, 512), np.float32)
    kwp = np.zeros((1, 4, 128, 2, 64), np.float32)
    vwp = np.zeros((1, 4, 128, 2, 64), np.float32)
    css = np.zeros((1, 32, 30, 512), np.float32)
    kws = np.zeros((1, 32, 128, 2, 64), np.float32)
    vws = np.zeros((1, 32, 128, 2, 64), np.float32)
    for c in range(8):
        b, hf = c // 2, c % 2
        y_prompt[b, hf * HALF:(hf + 1) * HALF] = r[c]["yout"][0:HALF]
        y_sample[4 * c:4 * c + 4] = r[c]["yout"][HALF:].reshape(4, 16, D)
        if hf == 1:
            csp[0, b] = r[c]["ocs_p"]
            kwp[0, b] = r[c]["okw_p"].reshape(128, 2, 64)
            vwp[0, b] = r[c]["ovw_p"].reshape(128, 2, 64)
        css[0, 4 * c:4 * c + 4] = r[c]["ocs_s"]
        kws[0, 4 * c:4 * c + 4] = r[c]["okw_s"].reshape(4, 128, 2, 64)
        vws[0, 4 * c:4 * c + 4] = r[c]["ovw_s"].reshape(4, 128, 2, 64)
    return (y_prompt, y_sample, csp, kwp, vwp, css, kws, vws)
```

```python
import numpy as np
from contextlib import ExitStack
import concourse.bass as bass
import concourse.mybir as mybir
from concourse.bass_utils import run_bass_kernel_spmd

F32 = mybir.dt.float32
BF16 = mybir.dt.bfloat16
AF = mybir.ActivationFunctionType
ALU = mybir.AluOpType

ENGS = ("sync", "scalar", "vector", "gpsimd", "tensor")
SEM_MAX = 30000


class _Op:
    __slots__ = ("eng", "fn", "deps", "idx", "dma_sem", "tok", "signal")

    def __init__(self, eng, fn, idx, dma_sem):
        self.eng, self.fn, self.idx, self.dma_sem = eng, fn, idx, dma_sem
        self.deps = set()
        self.tok = None
        self.signal = False


class Prog:
    def __init__(self):
        self.ops = []
        self.last_w = {}
        self.readers = {}
        self.last_dma_on_sem = {}

    def add(self, eng, fn, reads=(), writes=(), dma=None):
        op = _Op(eng, fn, len(self.ops), dma)
        self.ops.append(op)
        for k in reads:
            w = self.last_w.get(k)
            if w is not None:
                op.deps.add(w)
        for k in writes:
            w = self.last_w.get(k)
            if w is not None:
                op.deps.add(w)
            r = self.readers.get(k)
            if r:
                op.deps.update(r)
        for k in reads:
            self.readers.setdefault(k, []).append(op.idx)
        for k in writes:
            self.last_w[k] = op.idx
            self.readers[k] = []
        if dma is not None:
            p = self.last_dma_on_sem.get(dma)
            if p is not None:
                op.deps.add(p)
            self.last_dma_on_sem[dma] = op.idx
        op.deps.discard(op.idx)
        return op

    def emit(self, nc, stack):
        ops = self.ops
        for op in ops:
            keep = set()
            for d in op.deps:
                p = ops[d]
                if p.dma_sem is None and op.dma_sem is None and p.eng == op.eng == "tensor":
                    continue
                keep.add(d)
            best = {}
            for d in keep:
                p = ops[d]
                k = ("d", p.dma_sem) if p.dma_sem is not None else ("e", p.eng)
                if k not in best or best[k] < d:
                    best[k] = d
            op.deps = set(best.values())
            for d in op.deps:
                ops[d].signal = True
        dma_names = sorted({op.dma_sem for op in ops if op.dma_sem is not None})
        dma_sems = {n: stack.enter_context(nc.semaphore("d_" + n)) for n in dma_names}
        dma_cnt = {n: 0 for n in dma_names}
        eng_sems = {e: [] for e in ENGS}
        eng_cnt = {e: 0 for e in ENGS}
        for op in ops:
            if op.dma_sem is not None:
                dma_cnt[op.dma_sem] += 16
                op.tok = (dma_sems[op.dma_sem], dma_cnt[op.dma_sem])
            elif op.signal:
                c = eng_cnt[op.eng]
                k, v = divmod(c, SEM_MAX)
                if k >= len(eng_sems[op.eng]):
                    eng_sems[op.eng].append(stack.enter_context(nc.semaphore("e_%s%d" % (op.eng, k))))
                op.tok = (eng_sems[op.eng][k], v + 1)
                eng_cnt[op.eng] = c + 1
        print('signals', eng_cnt, 'dma', dma_cnt)
        block = stack.enter_context(nc.Block())
        per_eng = {e: [op for op in ops if op.eng == e] for e in ENGS}
        final_dma = {}
        for op in ops:
            if op.dma_sem is not None:
                final_dma[op.dma_sem] = op.tok

        def make(e):
            def body(eng):
                waited = {}
                for op in per_eng[e]:
                    need = {}
                    for d in op.deps:
                        s, v = ops[d].tok
                        if need.get(id(s), (None, 0))[1] < v:
                            need[id(s)] = (s, v)
                    for s, v in need.values():
                        if waited.get(id(s), 0) < v:
                            eng.wait_ge(s, v)
                            waited[id(s)] = v
                    ins = op.fn(eng)
                    if op.tok is not None:
                        ins.then_inc(op.tok[0], 16 if op.dma_sem is not None else 1)
                if e == "sync":
                    for s, v in final_dma.values():
                        if waited.get(id(s), 0) < v:
                            eng.wait_ge(s, v)
                            waited[id(s)] = v
            return body

        block.sync(make("sync"))
        block.scalar(make("scalar"))
        block.vector(make("vector"))
        block.gpsimd(make("gpsimd"))
        block.tensor(make("tensor"))
        return {e: len(per_eng[e]) for e in ENGS}


D = 1024
DFF = 2816
NJ = 22
SEQ = 8192
HALF = 4096
NT = 4
TM = 1024
TC = 1216
HALO0, SAMP0 = 1024, 1152
NROWS = HALF + 128 + 64
EPS = 1e-6
NSLOT = 4
W_FF1U, W_FF1D, W_IN, W_OUT, W_FF2U, W_FF2D, NCHUNK = 0, 22, 38, 45, 49, 71, 87
PG1, PGM, PG2, PBDW, PGCN, PBCN, PGQ, PGK, PNG, PNB = 0, 8, 16, 24, 28, 32, 36, 37, 40, 44


LEVEL = 9
MIXL = 99
SKIP = set()
NTILES = NT


def build_nc():
    nc = bass.Bass("TRN2", target_bir_lowering=False)
    din = lambda n, s: nc.dram_tensor(n, s, F32, kind="ExternalInput").ap()
    dout = lambda n, s: nc.dram_tensor(n, s, F32, kind="ExternalOutput").ap()
    xin = din("xin", [NROWS, D])
    wall = din("wall", [NCHUNK, 128, 2048])
    rope = din("rope", [2, 128, NROWS])
    setup_rows = din("setup_rows", [38, 128])
    wdw = din("wdw", [31, 512])
    sinks = din("sinks", [1, 8])
    cmat = din("cmat", [4, 128, 128])
    opad = din("opad", [2, 128, 192])
    smask = din("smask", [64, 256])
    sconv = din("sconv", [4, 30, 512])
    ck = din("ck", [4, 128, 128])
    cv = din("cv", [4, 128, 128])
    yout = dout("yout", [HALF + 64, D])
    ocs_p = dout("ocs_p", [30, 512])
    okw_p = dout("okw_p", [128, 128])
    ovw_p = dout("ovw_p", [128, 128])
    ocs_s = dout("ocs_s", [4, 30, 512])
    okw_s = dout("okw_s", [4, 128, 128])
    ovw_s = dout("ovw_s", [4, 128, 128])

    P = Prog()
    with ExitStack() as st:
        def sb(name, shape, dt):
            return st.enter_context(nc.sbuf_tensor(name, shape, dt))

        RT = sb("RT", [128, 8, TC], F32)
        XNR = sb("XNR", [128, 8 * TC], BF16)
        HR = sb("HR", [128, NJ * TC], BF16)
        XN = XNR[:, :].rearrange("p (c n) -> p c n", c=8)
        MIX = XNR[:, 0:8 * 1152].rearrange("p (c n) -> p c n", c=8)
        H = HR[:, :].rearrange("p (c n) -> p c n", c=NJ)
        hoff = [0]

        def carve(nelem_bf16):
            a = HR[:, hoff[0]:hoff[0] + nelem_bf16]
            hoff[0] += nelem_bf16
            return a

        U = carve(4 * 1056).rearrange("p (c n) -> p c n", c=4)
        Q = carve(4 * 1088).rearrange("p (c n) -> p c n", c=4)
        KT = carve(2 * 1152).rearrange("p (g n) -> p g n", g=2)
        VP = carve(9 * 2 * 192).rearrange("p (k g n) -> p k g n", k=9, g=2)
        YFraw = carve(2 * 4 * 512)
        YF = YFraw.bitcast(F32).rearrange("p (c n) -> p c n", c=4)
        PT2 = YFraw[:, 0:4 * 512].rearrange("p (c n) -> p c n", c=4)
        PT = carve(4 * 512).rearrange("p (c n) -> p c n", c=4)
        SF = [carve(2 * 512).bitcast(F32) for _ in range(4)]
        SB = [carve(512) for _ in range(3)]
        assert hoff[0] <= NJ * TC, hoff[0]

        ring = [sb("ring%d" % i, [128, 2048], BF16) for i in range(NSLOT)]
        XS = [sb("XS%d" % i, [128, D], F32) for i in range(2)]
        YS = [sb("YS%d" % i, [128, D], F32) for i in range(2)]
        OST = sb("OST", [128, 512], F32)
        DIAG = sb("DIAG", [128, 4, 31, 128], BF16)
        CT = sb("CT", [128, TC], BF16)
        ST = sb("ST", [128, TC], BF16)
        CM = sb("CM", [128, 4, 128], F32)
        IDF = CM[:, 0, :]
        IDB = sb("IDB", [128, 128], BF16)
        ONES = sb("ONES", [128, 128], BF16)
        ONESBD = sb("ONESBD", [128, 128], BF16)
        PROT = sb("PROT", [128, 128], BF16)
        OPAD = sb("OPAD", [128, 2, 192], BF16)
        PARAMS = sb("PARAMS", [128, 48], F32)
        WDW = sb("WDW", [128, 4, 31], F32)
        ESINK = sb("ESINK", [128, 4], F32)
        SKROW = sb("SKROW", [1, 8], F32)
        CONSTS = sb("CONSTS", [128, 2], F32)
        DUMMY = sb("DUMMY", [128, 2], F32)
        SQ = [sb("SQ%d" % i, [128, 512], BF16) for i in range(2)]
        RSTD = sb("RSTD", [128, 512], F32)
        SA = [sb("SA%d" % i, [128, 512], F32) for i in range(2)]
        UCAR = sb("UCAR", [128, 4, 32], BF16)
        KCAR = sb("KCAR", [128, 2, 128], BF16)
        VCAR = sb("VCAR", [128, 2, 192], BF16)
        K32 = sb("K32", [128, 192], F32)
        U32M = sb("U32M", [128, 4, 32], F32)
        V32L = sb("V32L", [128, 128], F32)
        USAMP = sb("USAMP", [128, 4, 8, 46], BF16)
        USAMP32 = sb("USAMP32", [128, 4, 64], F32)
        KCT = sb("KCT", [128, 2, 4, 128], BF16)
        KST = sb("KST", [128, 2, 128], BF16)
        VC = sb("VC", [128, 4, 2, 192], BF16)
        VS = sb("VS", [128, 2, 192], BF16)
        V32S = sb("V32S", [128, 128], F32)
        SMASK = sb("SMASK", [128, 256], BF16)
        pb = [st.enter_context(nc.psum_tensor("pb%d" % i, [128, 512], F32)) for i in range(8)]
        EPSC = CONSTS[:, 0:1]
        ONEC = CONSTS[:, 1:2]

        def op(eng, meth, reads, writes, **kw):
            P.add(eng, lambda e: getattr(e, meth)(**kw), reads, writes)

        def act(reads, writes, **kw):
            op("scalar", "activation", reads, writes, **kw)

        def mm(reads, writes, out, lhsT, rhs, start=True, stop=True):
            P.add("tensor", lambda e: e.matmul(out, lhsT=lhsT, rhs=rhs, start=start, stop=stop), reads, writes)

        def tr(reads, writes, out, in_, identity):
            P.add("tensor", lambda e: e.transpose(out=out, in_=in_, identity=identity), reads, writes)

        def dma(eng, sem, reads, writes, out, in_):
            P.add(eng, lambda e: e.dma_start(out=out, in_=in_), reads, writes, dma=sem)

        def gk(name, c0, c1):
            return ["%s.%d" % (name, g) for g in range(c0 // 128, (c1 - 1) // 128 + 1)]

        def pk(i):
            return "pb%d" % i

        dma("sync", "su0", [], ["CM"], CM[:], cmat.rearrange("k p n -> p k n"))
        SROW = XS[0][0:38, 0:128]
        WROW = XS[1][0:31, 0:512]
        dma("sync", "xs0", [], ["XS0"], SROW, setup_rows[:, :])
        dma("sync", "xs1", [], ["XS1"], WROW, wdw[:, :])
        dma("sync", "su3", [], ["SKROW"], SKROW[:], sinks[:, :])
        if "opad" not in SKIP:
            dma("gpsimd", "su4", [], ["OPAD"], OPAD[:], opad.rearrange("k p n -> p k n"))
        op("vector", "memset", [], ["CONSTS"], ap=CONSTS[:, 0:1], constant=EPS)
        op("vector", "memset", [], ["CONSTS"], ap=CONSTS[:, 1:2], constant=1.0)
        op("vector", "memset", [], ["DUMMY"], ap=DUMMY[:], constant=0.0)
        op("vector", "tensor_copy", ["CM"], ["IDB"], out=IDB[:], in_=CM[:, 0, :])
        op("vector", "tensor_copy", ["CM"], ["ONES"], out=ONES[:], in_=CM[:, 1, :])
        op("vector", "tensor_copy", ["CM"], ["ONESBD"], out=ONESBD[:], in_=CM[:, 2, :])
        op("vector", "tensor_copy", ["CM"], ["PROT"], out=PROT[:], in_=CM[:, 3, :])
        if "params" not in SKIP:
            tr(["XS0", "CM"], [pk(0)], pb[0][:, 0:38], SROW, CM[0:38, 0, 0:38])
            op("vector", "tensor_copy", [pk(0)], ["PARAMS"], out=PARAMS[:, 0:38], in_=pb[0][:, 0:38])
            op("vector", "tensor_scalar", ["PARAMS"], ["PARAMS"], out=PARAMS[:, PNG:PNG + 8], in0=PARAMS[:, PGCN:PGCN + 8],
               scalar1=-1.0, scalar2=None, op0=ALU.mult)
            for c in range(4):
                tr(["XS1", "CM"], [pk(1)], pb[1][:, c * 31:(c + 1) * 31], WROW[:, c * 128:(c + 1) * 128], CM[0:31, 0, 0:31])
            op("vector", "tensor_copy", [pk(1)], ["WDW"], out=WDW[:].rearrange("p c t -> p (c t)"), in_=pb[1][:, 0:124])
        if "esink" not in SKIP:
            act(["SKROW"], ["SKROW"], out=SKROW[:], in_=SKROW[:], func=AF.Exp)
            mm(["SKROW", "CM"], [pk(2)], pb[2][:, 0:8], CM[0:1, 1, :], SKROW[0:1, :])
            op("vector", "tensor_copy", [pk(2)], ["ESINK"], out=ESINK[0:64, :], in_=pb[2][0:64, 0:4])
            op("vector", "tensor_copy", [pk(2)], ["ESINK"], out=ESINK[64:128, :], in_=pb[2][64:128, 4:8])
        diag_todo = [(c, t) for c in range(4) for t in range(31)]

        def diag_some(n):
            for _ in range(n):
                if diag_todo:
                    c, t = diag_todo.pop(0)
                    op("vector", "tensor_scalar", ["WDW", "IDB"], ["DIAG"], out=DIAG[:, c, t, :], in0=IDB[:],
                       scalar1=WDW[:, c, t:t + 1], scalar2=None, op0=ALU.mult)

        def late_setup():
            diag_some(len(diag_todo))
            if "kct" not in SKIP:
                for s in range(4):
                    dma("sync", "xs0", [], ["XS0"], XS[0][:, s * 128:(s + 1) * 128], ck[s])
                for s in range(4):
                    tr(["XS0", "CM"], [pk(3)], pb[3][:, s * 128:(s + 1) * 128], XS[0][:, s * 128:(s + 1) * 128], IDF)
                op("vector", "memset", [], ["KCT"], ap=KCT[:].rearrange("p g s n -> p (g s n)"), constant=0.0)
                act([pk(3)], ["KCT"], out=KCT[0:64, 0, :, :], in_=pb[3][0:64, :].rearrange("p (s n) -> p s n", s=4), func=AF.Copy)
                act([pk(3)], ["KCT"], out=KCT[64:128, 1, :, :], in_=pb[3][64:128, :].rearrange("p (s n) -> p s n", s=4), func=AF.Copy)
            if "vc" not in SKIP:
                op("vector", "memset", [], ["VC"], ap=VC[:].rearrange("p s g n -> p (s g n)"), constant=0.0)
                op("vector", "memset", [], ["VS"], ap=VS[:].rearrange("p g n -> p (g n)"), constant=0.0)
                dma("gpsimd", "su5", [], ["SMASK"], SMASK[64:128, :], smask[:, :])
                op("vector", "memset", [], ["KST"], ap=KST[:].rearrange("p g n -> p (g n)"), constant=0.0)
                for s in range(4):
                    dma("sync", "xs0", [], ["XS0"], XS[0][:, s * 128:(s + 1) * 128], cv[s])
                vsrc = XS[0][:, 0:512].rearrange("p (s g n) -> p s g n", s=4, g=2)
                op("vector", "tensor_copy", ["XS0"], ["VC"], out=VC[:, :, :, 0:64], in_=vsrc)
                op("vector", "tensor_copy", ["XS0"], ["VC"], out=VC[:, :, :, 128:192], in_=vsrc)
            if "usamp" not in SKIP:
                op("vector", "memset", [], ["USAMP"], ap=USAMP[:].rearrange("p c s n -> p (c s n)"), constant=0.0)
                for s in range(4):
                    dma("sync", "xs1", [], ["XS1"], XS[1][0:30, 0:512], sconv[s])
                    for c in range(4):
                        tr(["XS1", "CM"], [pk(4)], pb[4][:, (s * 4 + c) * 30:(s * 4 + c + 1) * 30],
                           XS[1][0:30, c * 128:(c + 1) * 128], CM[0:30, 0, 0:30])
                    op("vector", "tensor_copy", [pk(4)], ["USAMP"], out=USAMP[:, :, s, 0:30],
                       in_=pb[4][:, s * 120:(s + 1) * 120].rearrange("p (c t) -> p c t", c=4))


        rcnt = [0]

        def ring_load(idx, ncols=2048):
            s = rcnt[0] % NSLOT
            rcnt[0] += 1
            dma("gpsimd", "ring%d" % s, [], ["ring%d" % s], ring[s][:, 0:ncols], wall[idx, :, 0:ncols])
            return s

        bcnt = {"ab": 0, "d": 0, "x": 0, "y": 0}

        def load_dma(row0, n):
            i = bcnt["x"]
            bcnt["x"] += 1
            sl = i % 2
            dma("sync", "xs%d" % sl, [], ["XS%d" % sl], XS[sl][0:n, :], xin[row0:row0 + n, :])
            return i

        def load_compute(i, n, col0):
            sl = i % 2
            xk = "XS%d" % sl
            bA, bB = (4, 5) if i % 2 == 0 else (6, 7)
            for c in range(8):
                bank = pb[bA] if c < 4 else pb[bB]
                tr([xk, "CM"], [pk(bA if c < 4 else bB)], bank[:, (c % 4) * 128:(c % 4) * 128 + n],
                   XS[sl][0:n, c * 128:(c + 1) * 128], CM[0:n, 0, 0:n])
            g = col0 // 128
            act([pk(bA)], ["RT%d.%d" % (c, g) for c in range(4)], out=RT[:, 0:4, col0:col0 + n],
                in_=pb[bA][:, :].rearrange("p (c n) -> p c n", c=4)[:, :, 0:n], func=AF.Copy)
            op("vector", "tensor_copy", [pk(bB)], ["RT%d.%d" % (c, g) for c in range(4, 8)],
               out=RT[:, 4:8, col0:col0 + n], in_=pb[bB][:, :].rearrange("p (c n) -> p c n", c=4)[:, :, 0:n])

        def load_group(row0, n, col0):
            load_compute(load_dma(row0, n), n, col0)

        def store_group(row0, n, col0):
            i = bcnt["y"]
            bcnt["y"] += 1
            bA, bB = (0, 1) if i % 2 == 0 else (2, 3)
            g = col0 // 128
            for c in range(8):
                bi = bA if c < 4 else bB
                tr(["RT%d.%d" % (c, g), "CM"], [pk(bi)], pb[bi][0:n, (c % 4) * 128:(c % 4 + 1) * 128],
                   RT[:, c, col0:col0 + n], IDF)
            ys, yk_ = YS[i % 2], "YS%d" % (i % 2)
            act([pk(bA)], [yk_], out=ys[0:n, 0:512], in_=pb[bA][0:n, :], func=AF.Copy)
            op("vector", "tensor_copy", [pk(bB)], [yk_], out=ys[0:n, 512:1024], in_=pb[bB][0:n, :])
            dma("sync", "ys%d" % (i % 2), [yk_], [], yout[row0:row0 + n, :], ys[0:n, :])

        def norm(blocks, gcol, extra_r=(), extra_w=()):
            for (c0, c1) in blocks:
                N = c1 - c0
                for c in range(8):
                    q = SQ[c % 2]
                    act(["RT%d.%d" % (c, g) for g in range(c0 // 128, (c1 - 1) // 128 + 1)], ["SQ%d" % (c % 2)],
                        out=q[:, 0:N], in_=RT[:, c, c0:c1], func=AF.Square, scale=1.0 / 32.0)
                    mm(["SQ%d" % (c % 2), "ONES"], [pk(7)], pb[7][:, 0:N], ONES[:], q[:, 0:N], start=(c == 0), stop=(c == 7))
                act([pk(7), "CONSTS"], ["RSTD"], out=RSTD[:, 0:N], in_=pb[7][:, 0:N], func=AF.Ln, bias=EPSC)
                act(["RSTD"], ["RSTD"], out=RSTD[:, 0:N], in_=RSTD[:, 0:N], func=AF.Exp, scale=-0.5)
                for c in range(8):
                    op("vector", "scalar_tensor_tensor",
                       ["RSTD", "PARAMS"] + gk("RT%d" % c, c0, c1) + list(extra_r),
                       gk("XN%d" % c, c0, c1) + list(extra_w),
                       out=XN[:, c, c0:c1], in0=RT[:, c, c0:c1], scalar=PARAMS[:, gcol + c:gcol + c + 1],
                       in1=RSTD[:, 0:N], op0=ALU.mult, op1=ALU.mult)

        def ff(blocks, wu, wd, scale, hook=None):
            for j in range(NJ):
                s = ring_load(wu + j)
                W = ring[s][:, :].rearrange("p (k n) -> p k n", k=8)
                rk = "ring%d" % s
                for (c0, c1) in blocks:
                    N = c1 - c0
                    i = bcnt["ab"]
                    bcnt["ab"] += 1
                    a, b = i % 2, 2 + i % 2
                    for kc in range(8):
                        mm([rk] + gk("XN%d" % kc, c0, c1), [pk(a)], pb[a][:, 0:N], W[:, kc, 0:128], XN[:, kc, c0:c1],
                           start=(kc == 0), stop=(kc == 7))
                    for kc in range(8):
                        mm([rk] + gk("XN%d" % kc, c0, c1), [pk(b)], pb[b][:, 0:N], W[:, kc, 128:256], XN[:, kc, c0:c1],
                           start=(kc == 0), stop=(kc == 7))
                    act([pk(a)], ["SA%d" % (i % 2)], out=SA[i % 2][:, 0:N], in_=pb[a][:, 0:N], func=AF.Silu)
                    op("vector", "tensor_tensor", [pk(b), "SA%d" % (i % 2)], gk("H%d" % j, c0, c1),
                       out=H[:, j, c0:c1], in0=pb[b][:, 0:N], in1=SA[i % 2][:, 0:N], op=ALU.mult)
                    if hook is not None:
                        hook()
            for m in range(8):
                s1 = ring_load(wd + 2 * m)
                s2 = ring_load(wd + 2 * m + 1, 768)
                W1 = ring[s1][:, :].rearrange("p (k n) -> p k n", k=16)
                W2 = ring[s2][:, 0:768].rearrange("p (k n) -> p k n", k=6)
                for (c0, c1) in blocks:
                    N = c1 - c0
                    i = bcnt["d"]
                    bcnt["d"] += 1
                    bi = 4 + i % 4
                    for j in range(NJ):
                        w = W1[:, j, :] if j < 16 else W2[:, j - 16, :]
                        mm(["ring%d" % (s1 if j < 16 else s2)] + gk("H%d" % j, c0, c1), [pk(bi)], pb[bi][:, 0:N], w,
                           H[:, j, c0:c1], start=(j == 0), stop=(j == NJ - 1))
                    op("vector", "scalar_tensor_tensor", [pk(bi)] + gk("RT%d" % m, c0, c1), gk("RT%d" % m, c0, c1),
                       out=RT[:, m, c0:c1], in0=pb[bi][:, 0:N], scalar=scale, in1=RT[:, m, c0:c1],
                       op0=ALU.mult, op1=ALU.add)

        ALLH = ["H%d.%d" % (j, g) for j in range(NJ) for g in range(10)]
        ALLXN = ["XN%d.%d" % (c, g) for c in range(8) for g in range(10)]

        def fence(writes):
            op("vector", "memset", [], list(writes) + ["DUMMY"], ap=DUMMY[:, 0:1], constant=0.0)

        def sigmoid_inplace(buf, key, N, r=(), scale_ap=None, bias_ap=None, src=None, src_keys=()):
            kw = {}
            if scale_ap is not None:
                kw = dict(scale=scale_ap, bias=bias_ap)
            else:
                kw = dict(scale=-1.0)
            act(list(src_keys) + [key] + list(r), [key], out=buf, in_=(src if src is not None else buf), func=AF.Exp, **kw)
            act([key, "CONSTS"] + list(r), [key], out=buf, in_=buf, func=AF.Ln, bias=ONEC[0:buf.shape[0], :])
            act([key] + list(r), [key], out=buf, in_=buf, func=AF.Exp, scale=-1.0)

        def qk_sets(ci):
            if ci == 0:
                return dict(QF=SF[0], RS=SF[1], T1=SF[2], T2=SF[3], kQF="SF0", kRS="SF1", kT1="SF2", kT2="SF3",
                            SQb=SB[0], QNb=SB[1], kSQ="SB0", kQN="SB1", b1=6, b2=7)
            return dict(QF=YF[:, 0, :], RS=YF[:, 1, :], T1=YF[:, 2, :], T2=YF[:, 3, :], kQF="YF0", kRS="YF1", kT1="YF2", kT2="YF3",
                        SQb=PT[:, 0, :], QNb=PT[:, 1, :], kSQ="PT0", kQN="PT1", b1=4, b2=5)

        def qk_s1(ch, R):
            z = qk_sets(ch["ci"])
            bi, N, gcol = ch["bank"], ch["N"], ch["gcol"]
            gs = PARAMS[:, gcol:gcol + 1]
            ch["proj"](bi)
            act([pk(bi), "PARAMS"] + R, [z["kQN"]], out=z["QNb"][:, 0:N], in_=pb[bi][:, 0:N], func=AF.Identity, scale=gs)
            act([pk(bi)] + R, [z["kSQ"]], out=z["SQb"][:, 0:N], in_=pb[bi][:, 0:N], func=AF.Square, scale=0.125)
            mm([z["kQN"], "PROT"] + R, [pk(z["b2"])], pb[z["b2"]][:, 0:N], PROT[:], z["QNb"][:, 0:N])
            mm([z["kSQ"], "ONESBD"] + R, [pk(z["b1"])], pb[z["b1"]][:, 0:N], ONESBD[:], z["SQb"][:, 0:N])

        def qk_s2(ch, R):
            z = qk_sets(ch["ci"])
            N, rc0, dest, dkeys = ch["N"], ch["rc0"], ch["dest"], ch["dkeys"]
            QF, RS, T2, b1, b2 = z["QF"], z["RS"], z["T2"], z["b1"], z["b2"]
            kQF, kRS, kT2 = z["kQF"], z["kRS"], z["kT2"]
            act([pk(b1), "CONSTS"] + R, [kRS], out=RS[:, 0:N], in_=pb[b1][:, 0:N], func=AF.Ln, bias=EPSC)
            act([kRS] + R, [kRS], out=RS[:, 0:N], in_=RS[:, 0:N], func=AF.Exp, scale=-0.5)
            op("vector", "tensor_tensor", [pk(b2), "ST"] + R, [kT2], out=T2[:, 0:N], in0=pb[b2][:, 0:N], in1=ST[:, rc0:rc0 + N],
               op=ALU.mult)
            bi = ch["bank"]
            op("vector", "scalar_tensor_tensor", [pk(bi), "CT", "PARAMS"] + R, [kQF], out=QF[:, 0:N], in0=pb[bi][:, 0:N],
               scalar=PARAMS[:, ch["gcol"]:ch["gcol"] + 1], in1=CT[:, rc0:rc0 + N], op0=ALU.mult, op1=ALU.mult)
            op("vector", "tensor_tensor", [kQF, kT2] + R, [kQF], out=QF[:, 0:N], in0=QF[:, 0:N], in1=T2[:, 0:N], op=ALU.add)
            if isinstance(dest, tuple):
                for hh_, dd in enumerate(dest):
                    op("vector", "tensor_tensor", [kQF, kRS] + R, list(dkeys), out=dd, in0=QF[hh_ * 64:(hh_ + 1) * 64, 0:N],
                       in1=RS[hh_ * 64:(hh_ + 1) * 64, 0:N], op=ALU.mult)
            else:
                op("vector", "tensor_tensor", [kQF, kRS] + R, list(dkeys), out=dest, in0=QF[:, 0:N], in1=RS[:, 0:N], op=ALU.mult)
            if ch.get("dest32") is not None:
                lo, hi, d32 = ch["dest32"]
                op("vector", "tensor_tensor", [kQF, kRS] + R, list(ch["d32keys"]), out=d32, in0=QF[:, lo:hi], in1=RS[:, lo:hi],
                   op=ALU.mult)

        def qk_run(chains, R):
            for i, ch in enumerate(chains):
                ch["ci"] = i % 2
                ch["bank"] = i % 4
            for i in range(len(chains) + 1):
                if i < len(chains):
                    qk_s1(chains[i], R)
                if i >= 1:
                    qk_s2(chains[i - 1], R)

        def ln_stats(sub, R):
            N = sub["N"]
            M1, MSQ, VR = SF[0], SF[1], SF[2]
            op("vector", "tensor_scalar", [pk(4)] + R, ["SF0"], out=M1[:, 0:N], in0=pb[4][:, 0:N], scalar1=1.0 / 512, scalar2=None,
               op0=ALU.mult)
            op("vector", "tensor_tensor", ["SF0"] + R, ["SF1"], out=MSQ[:, 0:N], in0=M1[:, 0:N], in1=M1[:, 0:N], op=ALU.mult)
            op("vector", "scalar_tensor_tensor", [pk(5), "SF1"] + R, ["SF2"], out=VR[:, 0:N], in0=pb[5][:, 0:N], scalar=1.0 / 512,
               in1=MSQ[:, 0:N], op0=ALU.mult, op1=ALU.subtract)
            act(["SF2", "CONSTS"] + R, ["SF2"], out=VR[:, 0:N], in_=VR[:, 0:N], func=AF.Ln, bias=EPSC)
            act(["SF2"] + R, ["SF2"], out=VR[:, 0:N], in_=VR[:, 0:N], func=AF.Exp, scale=-0.5)

        def ln_parts(sub, c, R, RX):
            N, h, mixc0 = sub["N"], sub["h"], sub["mixc0"]
            M1, VR = SF[0], SF[2]
            eh = c % 2
            E = SF[3][:, eh * 256:eh * 256 + N]
            ek = "SF3.%d" % eh
            yv = YF[:, c, h * 256:h * 256 + N]
            yk = "YF%d.%d" % (c, h)

            def A():
                op("vector", "tensor_tensor", [yk, "SF0"] + R, [yk], out=yv, in0=yv, in1=M1[:, 0:N], op=ALU.subtract)
                op("vector", "scalar_tensor_tensor", [yk, "SF2", "PARAMS"] + R, [yk], out=yv, in0=yv,
                   scalar=PARAMS[:, PGCN + c:PGCN + c + 1], in1=VR[:, 0:N], op0=ALU.mult, op1=ALU.mult)

            def B():
                sigmoid_inplace(E, ek, N, r=R + ["PARAMS", "SF3"], scale_ap=-1.0,
                                bias_ap=PARAMS[:, PNB + c:PNB + c + 1], src=yv, src_keys=[yk])

            def C():
                op("vector", "scalar_tensor_tensor", [yk, ek, "SF3", "PARAMS"] + R + RX,
                   ["MIX%d.%d" % (c, g) for g in range(mixc0 // 128, (mixc0 + N - 1) // 128 + 1)],
                   out=MIX[:, c, mixc0:mixc0 + N], in0=yv, scalar=PARAMS[:, PBCN + c:PBCN + c + 1], in1=E,
                   op0=ALU.add, op1=ALU.mult)
            return A, B, C

        conv_pending = []

        def conv_flush():
            for f in conv_pending:
                f()
            del conv_pending[:]

        def conv_chunk(sub, c, R):
            N, h, rhs_fn, ukeys = sub["N"], sub["h"], sub["rhs_fn"], sub["ukeys"]
            i = bcnt["ab"]
            bcnt["ab"] += 1
            yb = i % 2
            for t in range(31):
                mm(["DIAG"] + ukeys(c) + R, [pk(yb)], pb[yb][:, 0:N], DIAG[:, c, t, :], rhs_fn(c, t), start=(t == 0), stop=(t == 30))
            conv_flush()
            yv = YF[:, c, h * 256:h * 256 + N]
            yk = "YF%d.%d" % (c, h)
            act([pk(yb), "PARAMS", "YF%d" % c] + R, [yk], out=yv, in_=pb[yb][:, 0:N], func=AF.Identity,
                bias=PARAMS[:, PBDW + c:PBDW + c + 1])
            op("vector", "tensor_copy", [yk] + R, ["SB%d" % yb], out=SB[yb][:, 0:N], in_=yv)
            act([yk] + R, ["SB2"], out=SB[2][:, 0:N], in_=yv, func=AF.Square)
            def stats(yb=yb, c=c, N=N):
                mm(["SB%d" % yb, "ONES"] + R, [pk(4)], pb[4][:, 0:N], ONES[:], SB[yb][:, 0:N], start=(c == 0), stop=(c == 3))
                mm(["SB2", "ONES"] + R, [pk(5)], pb[5][:, 0:N], ONES[:], SB[2][:, 0:N], start=(c == 0), stop=(c == 3))
            conv_pending.append(stats)

        def conv_run(subs, R, RX, tail=None):
            for i in range(len(subs) + 1):
                cur = subs[i] if i < len(subs) else None
                prev = subs[i - 1] if i >= 1 else None
                if prev is not None:
                    conv_flush()
                    ln_stats(prev, R)
                pendC = None
                for c in range(4):
                    if cur is not None:
                        conv_chunk(cur, c, R)
                    elif tail is not None:
                        tail(c, "pre")
                    if prev is not None:
                        A, B, C = ln_parts(prev, c, R, RX)
                        A()
                        B()
                        if pendC is not None:
                            pendC()
                        pendC = C
                    if cur is None and tail is not None:
                        tail(c, "post")
                if pendC is not None:
                    pendC()

        def mixer(t):
            first, last = (t == 0), (t == NT - 1)
            main_blocks = [(0, 512), (512, 1024)]
            all_blocks = main_blocks + ([(HALO0, SAMP0), (SAMP0, TC)] if first else [])
            R = ["HF"]
            RX = ["XF"]
            fence(ALLH + ["HF"])
            dma("gpsimd", "rope0", [], ["CT"], CT[:, 0:TM], rope[0, :, t * TM:(t + 1) * TM])
            dma("gpsimd", "rope1", [], ["ST"], ST[:, 0:TM], rope[1, :, t * TM:(t + 1) * TM])
            if first:
                dma("gpsimd", "rope0", [], ["CT"], CT[:, TM:TC], rope[0, :, HALF:NROWS])
                dma("gpsimd", "rope1", [], ["ST"], ST[:, TM:TC], rope[1, :, HALF:NROWS])
            norm([(0, 512), (512, 1024)] + ([(HALO0, TC)] if first else []), PGM)
            op("vector", "memset", R, ["VP"], ap=VP[:].rearrange("p k g n -> p (k g n)"), constant=0.0)
            op("vector", "memset", R, ["KT.%d" % g for g in range(9)], ap=KT[:].rearrange("p g n -> p (g n)"), constant=0.0)
            if not first:
                op("vector", "tensor_copy", ["UCAR"] + R, ["Upre"], out=U[:, :, 0:32], in_=UCAR[:])
                op("vector", "tensor_copy", ["KCAR"] + R, ["KT.0"], out=KT[:, :, 0:128], in_=KCAR[:])
                op("vector", "tensor_copy", ["VCAR"] + R, ["VP"], out=VP[:, 0, :, :], in_=VCAR[:])
            for i in range(4):
                s = ring_load(W_IN + i)
                W = ring[s][:, :].rearrange("p (k n) -> p k n", k=8)
                rk = "ring%d" % s
                for (c0, c1) in all_blocks:
                    N = c1 - c0
                    k = bcnt["ab"]
                    bcnt["ab"] += 1
                    a, b = k % 2, 2 + k % 2
                    for kc in range(8):
                        mm([rk] + gk("XN%d" % kc, c0, c1) + R, [pk(a)], pb[a][:, 0:N], W[:, kc, 0:128], XN[:, kc, c0:c1],
                           start=(kc == 0), stop=(kc == 7))
                    for kc in range(8):
                        mm([rk] + gk("XN%d" % kc, c0, c1) + R, [pk(b)], pb[b][:, 0:N], W[:, kc, 128:256], XN[:, kc, c0:c1],
                           start=(kc == 0), stop=(kc == 7))
                    E = SF[3]
                    sigmoid_inplace(E[:, 0:N], "SF3", N, r=R, src=pb[b][:, 0:N], src_keys=[pk(b)])
                    if c0 < 1024:
                        op("vector", "tensor_tensor", [pk(a), "SF3"] + R, ["U%d.%d" % (i, g) for g in range(c0 // 128, c1 // 128)],
                           out=U[:, i, 32 + c0:32 + c1], in0=pb[a][:, 0:N], in1=E[:, 0:N], op=ALU.mult)
                        if last and c1 == 1024:
                            op("vector", "tensor_tensor", [pk(a), "SF3"] + R, ["U32M"], out=U32M[:, i, :], in0=pb[a][:, N - 32:N],
                               in1=E[:, N - 32:N], op=ALU.mult)
                    elif c0 == HALO0:
                        op("vector", "tensor_tensor", [pk(a), "SF3"] + R, ["Upre"], out=U[:, i, 0:32], in0=pb[a][:, 96:128],
                           in1=E[:, 96:128], op=ALU.mult)
                    else:
                        op("vector", "tensor_tensor", [pk(a), "SF3"] + R, ["USAMP"], out=USAMP[:, i, 0:4, 30:46],
                           in0=pb[a][:, 0:64].rearrange("p (s n) -> p s n", s=4), in1=E[:, 0:64].rearrange("p (s n) -> p s n", s=4),
                           op=ALU.mult)
                        op("vector", "tensor_tensor", [pk(a), "SF3"] + R, ["USAMP32"], out=USAMP32[:, i, :], in0=pb[a][:, 0:64],
                           in1=E[:, 0:64], op=ALU.mult)
            if MIXL < 2:
                fence(ALLXN + ["XF"]); fence(ALLH + ["HF"]); return
            chains = []

            def mkproj(rk, W, col0, c0, c1):
                def proj(bi):
                    for kc in range(8):
                        mm([rk] + gk("XN%d" % kc, c0, c1) + R, [pk(bi)], pb[bi][:, 0:c1 - c0], W[:, kc, col0:col0 + 128],
                           XN[:, kc, c0:c1], start=(kc == 0), stop=(kc == 7))
                return proj

            for i in range(2):
                s = ring_load(W_IN + 4 + i)
                W = ring[s][:, :].rearrange("p (k n) -> p k n", k=8)
                rk = "ring%d" % s
                for (c0, c1) in main_blocks + ([(SAMP0, TC)] if first else []):
                    N = c1 - c0
                    qc0 = c0 if c0 < 1024 else 1024
                    for hh in range(2):
                        j = 2 * i + hh
                        chains.append(dict(proj=mkproj(rk, W, hh * 128, c0, c1), N=N, gcol=PGQ, rc0=c0, dest=Q[:, j, qc0:qc0 + N],
                                           dkeys=["Q%d.%d" % (j, g) for g in range(qc0 // 128, (qc0 + N - 1) // 128 + 1)]))
            s = ring_load(W_IN + 6)
            W = ring[s][:, :].rearrange("p (k n) -> p k n", k=8)
            rk = "ring%d" % s
            for (c0, c1) in all_blocks:
                N = c1 - c0
                ch = dict(proj=mkproj(rk, W, 0, c0, c1), N=N, gcol=PGK, rc0=c0)
                if c0 < 1024:
                    ch.update(dest=(KT[0:64, 0, 128 + c0:128 + c1], KT[64:128, 1, 128 + c0:128 + c1]),
                              dkeys=["KT.%d" % g for g in range(1 + c0 // 128, 1 + c1 // 128)],
                              dest32=((N - 128, N, K32[:, 0:128]) if (last and c1 == 1024) else None), d32keys=["K32a"])
                elif c0 == HALO0:
                    ch.update(dest=(KT[0:64, 0, 0:128], KT[64:128, 1, 0:128]), dkeys=["KT.0"])
                else:
                    ch.update(dest=(KST[0:64, 0, 64:128], KST[64:128, 1, 64:128]), dkeys=["KST"], dest32=(0, 64, K32[:, 128:192]),
                              d32keys=["K32b"])
                chains.append(ch)
            qk_run(chains, R)
            vgroups = [(g * 128, g + 1) for g in range(8)] + ([(HALO0, 0)] if first else [])
            for (c0, kg) in ([] if "nov" in SKIP else vgroups):
                k = bcnt["ab"]
                bcnt["ab"] += 1
                a = k % 4
                for kc in range(8):
                    mm([rk] + gk("XN%d" % kc, c0, c0 + 128) + R, [pk(a)], pb[a][:, 0:128], XN[:, kc, c0:c0 + 128], W[:, kc, 128:256],
                       start=(kc == 0), stop=(kc == 7))
                pv = pb[a][:, 0:128].rearrange("p (g n) -> p g n", g=2)
                act([pk(a)] + R, ["VP"], out=VP[:, kg, :, 0:64], in_=pv, func=AF.Copy)
                op("vector", "tensor_copy", [pk(a)] + R, ["VP"], out=VP[:, kg, :, 128:192], in_=pv)
                if last and kg == 8:
                    op("vector", "tensor_copy", [pk(a)] + R, ["V32L"], out=V32L[:], in_=pb[a][:, 0:128])
            if first and "novs" not in SKIP:
                k = bcnt["ab"]
                bcnt["ab"] += 1
                a = k % 4
                for kc in range(8):
                    mm([rk] + gk("XN%d" % kc, 1088, TC) + R, [pk(a)], pb[a][:, 0:128], XN[:, kc, 1088:TC],
                       W[:, kc, 128:256], start=(kc == 0), stop=(kc == 7))
                pv = pb[a][64:128, 0:128].rearrange("p (g n) -> p g n", g=2)
                if "vsA" not in SKIP:
                    act([pk(a)] + R, ["VS"], out=VS[64:128, :, 0:64], in_=pv, func=AF.Copy)
                if "vsB" not in SKIP:
                    op("vector", "tensor_copy", [pk(a)] + R, ["VS"], out=VS[64:128, :, 128:192], in_=pv)
                if "vsC" not in SKIP:
                    op("vector", "tensor_copy", [pk(a)] + R, ["V32S"], out=V32S[64:128, :], in_=pb[a][64:128, 0:128])
            if MIXL < 4:
                fence(ALLXN + ["XF"]); fence(ALLH + ["HF"]); return
            fence(ALLXN + ["XF"])
            if not last:
                op("vector", "tensor_copy", ["U%d.7" % c for c in range(4)] + R, ["UCAR"], out=UCAR[:], in_=U[:, :, 1024:1056])
                op("vector", "tensor_copy", ["KT.8"] + R, ["KCAR"], out=KCAR[:], in_=KT[:, :, 1024:1152])
                op("vector", "tensor_copy", ["VP"] + R, ["VCAR"], out=VCAR[:], in_=VP[:, 8, :, :])

            nsub = 4 + (1 if first else 0)
            fh = 1 - ((nsub - 1) % 2)
            PT2t = YFraw[:, :].rearrange("p (c h n) -> p c h n", c=4, h=2)[:, :, fh, :]
            pt2key = "YF%%d.%d" % fh

            def att_scores(G):
                PTb, pkey = (PT, "PT%d") if G % 2 == 0 else (PT2t, pt2key)
                for kgi in range(2):
                    kg = G + kgi
                    for g in range(2):
                        bi = kgi * 2 + g
                        mm(["KT.%d" % kg] + ["Q%d.%d" % (j, G) for j in range(4)] + R, [pk(bi)], pb[bi][:, :],
                           KT[:, g, kg * 128:(kg + 1) * 128], Q[:, :, G * 128:(G + 1) * 128])
                        part = (0, 64) if kgi == 0 else (64, 128)
                        qmask = 1 if kgi == 0 else 0
                        act([pk(bi)] + R, [pkey % bi], out=PTb[:, bi, :], in_=pb[bi][:, :], func=AF.Exp, scale=0.125)
                        op("vector", "memset", R, [pkey % bi],
                           ap=PTb[part[0]:part[1], bi, :].rearrange("p (j q n) -> p j q n", j=4, q=2)[:, :, qmask, :], constant=0.0)

            def att_pv(G):
                PTb, pkey = (PT, "PT%d") if G % 2 == 0 else (PT2t, pt2key)
                bn, bd = (6, 7) if G % 2 == 0 else (4, 5)
                n_ = 0
                for g in range(2):
                    for kgi in range(2):
                        kg = G + kgi
                        bi = kgi * 2 + g
                        sel = slice(0, 128) if g == 0 else slice(64, 192)
                        rhs = PTb[:, bi, :]
                        mm(["VP", pkey % bi] + R, [pk(bn)], pb[bn][:, :], VP[:, kg, g, sel], rhs, start=(n_ == 0), stop=(n_ == 3))
                        ow = OPAD[:, (1 if (first and kg == 0) else 0), sel]
                        mm(["OPAD", pkey % bi] + R, [pk(bd)], pb[bd][:, :], ow, rhs, start=(n_ == 0), stop=(n_ == 3))
                        n_ += 1

            def att_norm(G):
                bn, bd = (6, 7) if G % 2 == 0 else (4, 5)
                Rr, rkey = (SF[1], "SF1") if G % 2 == 0 else (SF[0], "SF0")
                for j in range(4):
                    act([pk(bd), "ESINK"] + R, [rkey], out=Rr[:, j * 128:(j + 1) * 128], in_=pb[bd][:, j * 128:(j + 1) * 128],
                        func=AF.Ln, bias=ESINK[:, j:j + 1])
                act([rkey] + R, [rkey], out=Rr[:, :], in_=Rr[:, :], func=AF.Exp, scale=-1.0)
                op("vector", "tensor_tensor", [pk(bn), rkey] + R + RX, ["MIX%d.%d" % (4 + j, G) for j in range(4)],
                   out=MIX[:, 4:8, G * 128:(G + 1) * 128], in0=pb[bn][:, :].rearrange("p (j n) -> p j n", j=4),
                   in1=Rr[:, :].rearrange("p (j n) -> p j n", j=4), op=ALU.mult)

            subs = []
            for c0 in range(0, 1024, 256):
                subs.append(dict(N=256, h=len(subs) % 2, mixc0=c0,
                                 rhs_fn=(lambda c, tt, c0=c0: U[:, c, c0 + 2 + tt:c0 + 2 + tt + 256]),
                                 ukeys=(lambda c, c0=c0: ["Upre"] + ["U%d.%d" % (c, g) for g in range(max(0, c0 // 128 - 1), (c0 + 256) // 128)])))
            if first:
                subs.append(dict(N=128, h=len(subs) % 2, mixc0=1024, rhs_fn=(lambda c, tt: USAMP[:, c, :, tt:tt + 16]),
                                 ukeys=(lambda c: ["USAMP"])))
            op("vector", "memset", R, ["PT%d" % i for i in range(4)], ap=PT[:].rearrange("p c n -> p (c n)"), constant=0.0)

            def att_tail(c, when):
                if c == 0 and when == "pre":
                    att_scores(0)
                    att_scores(1)
                elif c == 1 and when == "pre":
                    att_pv(0)

            conv_run(subs, R, RX, tail=att_tail)
            if MIXL < 5:
                fence(ALLXN + ["XF"]); fence(ALLH + ["HF"]); return
            for G in range(1, 8):
                if G + 1 < 8:
                    att_scores(G + 1)
                att_pv(G)
                att_norm(G - 1)
            att_norm(7)
            if MIXL < 6:
                fence(ALLXN + ["XF"]); fence(ALLH + ["HF"]); return
            if first:
                for s4 in range(4):
                    for g in range(2):
                        oc = (s4 * 2 + g) * 64
                        qv = Q[:, :, 1024 + s4 * 16:1024 + s4 * 16 + 16]
                        mm(["KCT"] + ["Q%d.8" % j for j in range(4)] + R, [pk(0)], pb[0][:, oc:oc + 64], KCT[:, g, s4, :], qv)
                for g in range(2):
                    mm(["KST"] + ["Q%d.8" % j for j in range(4)] + R, [pk(1)], pb[1][:, g * 256:(g + 1) * 256],
                       KST[:, g, 0:128], Q[:, :, 1024:1088])
                act([pk(0)] + R, ["PT0"], out=PT[:, 0, :], in_=pb[0][:, :], func=AF.Exp, scale=0.125)
                op("vector", "memset", R, ["PT1"], ap=PT[0:64, 1, :], constant=0.0)
                act([pk(1)] + R, ["PT1"], out=PT[64:128, 1, :], in_=pb[1][64:128, :], func=AF.Exp, scale=0.125)
                for g in range(2):
                    op("vector", "tensor_tensor", ["PT1", "SMASK"] + R, ["PT1"], out=PT[64:128, 1, g * 256:(g + 1) * 256],
                       in0=PT[64:128, 1, g * 256:(g + 1) * 256], in1=SMASK[64:128, :], op=ALU.mult)
                for j in range(4):
                    for s4 in range(4):
                        ocol = slice(j * 64 + s4 * 16, j * 64 + s4 * 16 + 16)
                        for g in range(2):
                            sel = slice(0, 128) if g == 0 else slice(64, 192)
                            pc = (s4 * 2 + g) * 64 + j * 16
                            mm(["VC", "PT0"] + R, [pk(6)], pb[6][:, ocol], VC[:, s4, g, sel], PT[:, 0, pc:pc + 16], start=(g == 0), stop=(g == 1))
                            mm(["OPAD", "PT0"] + R, [pk(7)], pb[7][:, ocol], OPAD[:, 0, sel], PT[:, 0, pc:pc + 16], start=(g == 0), stop=(g == 1))
                    for g in range(2):
                        sel = slice(0, 128) if g == 0 else slice(64, 192)
                        pc = g * 256 + j * 64
                        mm(["VS", "PT1"] + R, [pk(4)], pb[4][:, j * 64:(j + 1) * 64], VS[:, g, sel], PT[:, 1, pc:pc + 64], start=(g == 0), stop=(g == 1))
                        mm(["OPAD", "PT1"] + R, [pk(5)], pb[5][:, j * 64:(j + 1) * 64], OPAD[:, 0, sel], PT[:, 1, pc:pc + 64], start=(g == 0), stop=(g == 1))
                Rr, T1, NS, T3 = SF[0], SF[1], SF[2], SF[3]
                op("vector", "tensor_copy", [pk(4)] + R, ["SF1"], out=T1[:, 0:256], in_=pb[4][:, 0:256])
                op("vector", "tensor_tensor", [pk(6), "SF1"] + R, ["SF2"], out=NS[:, 0:256], in0=pb[6][:, 0:256], in1=T1[:, 0:256], op=ALU.add)
                op("vector", "tensor_copy", [pk(5)] + R, ["SF3"], out=T3[:, 0:256], in_=pb[5][:, 0:256])
                op("vector", "tensor_tensor", [pk(7), "SF3"] + R, ["SF0"], out=Rr[:, 0:256], in0=pb[7][:, 0:256], in1=T3[:, 0:256], op=ALU.add)
                for j in range(4):
                    op("vector", "tensor_scalar", ["SF0", "ESINK"] + R, ["SF0"], out=Rr[:, j * 64:(j + 1) * 64],
                       in0=Rr[:, j * 64:(j + 1) * 64], scalar1=ESINK[:, j:j + 1], scalar2=None, op0=ALU.add)
                act(["SF0"] + R, ["SF0"], out=Rr[:, 0:256], in_=Rr[:, 0:256], func=AF.Ln)
                act(["SF0"] + R, ["SF0"], out=Rr[:, 0:256], in_=Rr[:, 0:256], func=AF.Exp, scale=-1.0)
                op("vector", "tensor_tensor", ["SF2", "SF0"] + R + RX, ["MIX%d.8" % (4 + j) for j in range(4)],
                   out=MIX[:, 4:8, 1024:1088], in0=NS[:, 0:256].rearrange("p (j n) -> p j n", j=4),
                   in1=Rr[:, 0:256].rearrange("p (j n) -> p j n", j=4), op=ALU.mult)
            if MIXL < 7:
                fence(ALLXN + ["XF"]); fence(ALLH + ["HF"]); return
            wo = []
            for n in range(4):
                s = ring_load(W_OUT + n)
                wo.append((ring[s][:, :].rearrange("p (k n) -> p k n", k=8), "ring%d" % s))
            for (c0, c1) in main_blocks + ([(SAMP0, TC)] if first else []):
                N = c1 - c0
                mc0 = c0 if c0 < 1024 else 1024
                for n in range(4):
                    W, rk = wo[n]
                    for mi in range(2):
                        m = 2 * n + mi
                        i = bcnt["d"]
                        bcnt["d"] += 1
                        bi = i % 4
                        for kc in range(8):
                            mm([rk] + ["MIX%d.%d" % (kc, g) for g in range(mc0 // 128, (mc0 + N - 1) // 128 + 1)] + R + RX, [pk(bi)],
                               pb[bi][:, 0:N], W[:, kc, mi * 128:(mi + 1) * 128], MIX[:, kc, mc0:mc0 + N], start=(kc == 0), stop=(kc == 7))
                        op("vector", "tensor_tensor", [pk(bi)] + gk("RT%d" % m, c0, c1), gk("RT%d" % m, c0, c1),
                           out=RT[:, m, c0:c1], in0=pb[bi][:, 0:N], in1=RT[:, m, c0:c1], op=ALU.add)
            if MIXL < 8:
                fence(ALLXN + ["XF"]); fence(ALLH + ["HF"]); return
            if first:
                for c in range(4):
                    tr(["USAMP32", "CM"] + R, [pk(0)], pb[0][0:64, c * 128:(c + 1) * 128], USAMP32[:, c, :], IDF)
                op("vector", "tensor_copy", [pk(0)], ["OST"], out=OST[0:64, :], in_=pb[0][0:64, :])
                for s4 in range(4):
                    dma("sync", "o0", ["OST"], [], ocs_s[s4, 14:30, :], OST[s4 * 16:(s4 + 1) * 16, :])
                tr(["K32b", "CM"], [pk(1)], pb[1][0:64, 0:128], K32[:, 128:192], IDF)
                op("vector", "tensor_copy", [pk(1)], ["V32L"], out=V32L[0:64, :], in_=pb[1][0:64, 0:128])
                for s4 in range(4):
                    dma("sync", "o1", ["V32L"], [], okw_s[s4, 112:128, :], V32L[s4 * 16:(s4 + 1) * 16, :])
                    dma("sync", "o2", ["V32S"], [], ovw_s[s4, 112:128, :], V32S[64 + s4 * 16:64 + (s4 + 1) * 16, :])
            if last:
                for c in range(4):
                    tr(["U32M", "CM"], [pk(0)], pb[0][0:32, c * 128:(c + 1) * 128], U32M[:, c, :], IDF)
                op("vector", "tensor_copy", [pk(0)], ["OST"], out=OST[0:32, :], in_=pb[0][0:32, :])
                dma("sync", "o0", ["OST"], [], ocs_p[:, :], OST[2:32, :])
                tr(["K32a", "CM"], [pk(1)], pb[1][:, 0:128], K32[:, 0:128], IDF)
                op("vector", "tensor_copy", [pk(1)], ["USAMP32"], out=USAMP32[:, 0:2, :].rearrange("p a n -> p (a n)"), in_=pb[1][:, 0:128])
                dma("sync", "o1", ["USAMP32"], [], okw_p[:, :], USAMP32[:, 0:2, :].rearrange("p a n -> p (a n)"))
                dma("sync", "o2", ["V32L"], [], ovw_p[:, :], V32L[:, :])
            fence(ALLXN + ["XF"])
            fence(ALLH + ["HF"])

        for t in range(NTILES):
            first = (t == 0)
            if first:
                hs = [load_dma(0, 128), load_dma(128, 128)]
                for g in range(8):
                    load_compute(hs[g], 128, g * 128)
                    if g + 2 < 8:
                        hs.append(load_dma((g + 2) * 128, 128))
            if first:
                load_group(HALF, 128, HALO0)
                load_group(HALF + 128, 64, SAMP0)
            if first and "pass" not in SKIP:
                for s in range(4):
                    dma("sync", "o0", [], [], ocs_s[s, 0:14, :], sconv[s, 16:30, :])
                    dma("sync", "o1", [], [], okw_s[s, 0:112, :], ck[s, 16:128, :])
                    dma("sync", "o2", [], [], ovw_s[s, 0:112, :], cv[s, 16:128, :])
            b1 = [(0, 512), (512, 1024)] + ([(HALO0, TC)] if first else [])
            b2 = [(0, 512), (512, 1024)] + ([(SAMP0, TC)] if first else [])
            if LEVEL >= 1:
                norm(b1, PG1)
            if LEVEL >= 2:
                ff(b1, W_FF1U, W_FF1D, 0.5, hook=((lambda: diag_some(2)) if first else None))
            if first:
                late_setup()
            if LEVEL >= 3:
                mixer(t)
            if LEVEL >= 4:
                norm(b2, PG2)
                ff(b2, W_FF2U, W_FF2D, 0.5)
            if first:
                store_group(HALF, 64, SAMP0)
            nxt = t + 1 < NTILES
            hs = [load_dma((t + 1) * TM, 128), load_dma((t + 1) * TM + 128, 128)] if nxt else []
            for g in range(8):
                store_group(t * TM + g * 128, 128, g * 128)
                if nxt:
                    load_compute(hs[g], 128, g * 128)
                    if g + 2 < 8:
                        hs.append(load_dma((t + 1) * TM + (g + 2) * 128, 128))
        print("sbuf remaining", nc.sbuf_bytes_remaining, "ops", len(P.ops))
        counts = P.emit(nc, st)
        print(counts)
    return nc, counts


_CACHE = {}


def _pack_weights(w_ff1_in, w_ff1_out, w_in, w_out, w_ff2_in, w_ff2_out):
    wall = np.zeros((NCHUNK, 128, 2048), np.float32)

    def up(w, base):
        w4 = w.reshape(8, 128, 2, NJ, 128)
        wall[base:base + NJ] = w4.transpose(3, 1, 0, 2, 4).reshape(NJ, 128, 2048)

    def down(w, base):
        w4 = w.reshape(NJ, 128, 8, 128)
        for m in range(8):
            wall[base + 2 * m] = w4[0:16, :, m, :].transpose(1, 0, 2).reshape(128, 2048)
            wall[base + 2 * m + 1, :, 0:768] = w4[16:22, :, m, :].transpose(1, 0, 2).reshape(128, 768)

    up(w_ff1_in, W_FF1U)
    down(w_ff1_out, W_FF1D)
    up(w_ff2_in, W_FF2U)
    down(w_ff2_out, W_FF2D)
    wi = w_in.reshape(8, 128, 1792)
    for i in range(4):
        blk = np.concatenate([wi[:, :, i * 128:(i + 1) * 128], wi[:, :, 512 + i * 128:512 + (i + 1) * 128]], axis=2)
        wall[W_IN + i] = blk.transpose(1, 0, 2).reshape(128, 2048)
    qcols = []
    for j in range(4):
        qcols += list(range(1024 + j * 64, 1024 + (j + 1) * 64)) + list(range(1024 + (4 + j) * 64, 1024 + (5 + j) * 64))
    qcols = np.array(qcols)
    for i in range(2):
        blk = wi[:, :, qcols[i * 256:(i + 1) * 256]]
        wall[W_IN + 4 + i] = blk.transpose(1, 0, 2).reshape(128, 2048)
    wall[W_IN + 6] = wi[:, :, 1536:1792].transpose(1, 0, 2).reshape(128, 2048)
    rows = list(range(512))
    for j in range(4):
        rows += list(range(512 + j * 64, 512 + (j + 1) * 64)) + list(range(512 + (4 + j) * 64, 512 + (5 + j) * 64))
    wo = w_out[np.array(rows)].reshape(8, 128, 1024)
    for n in range(4):
        wall[W_OUT + n] = wo[:, :, n * 256:(n + 1) * 256].transpose(1, 0, 2).reshape(128, 2048)
    return wall


def _rope_tables(base):
    half = 8
    inv = np.power(np.float32(500000.0), -np.arange(half, dtype=np.float32) / np.float32(half)).astype(np.float32)
    pos = np.concatenate([base + np.arange(HALF), base - 128 + np.arange(128), 4096 + (np.arange(64) % 16)]).astype(np.float32)
    ang = pos[None, :] * inv[:, None]
    cos = np.cos(ang).astype(np.float32)
    sin = np.sin(ang).astype(np.float32)
    tab = np.zeros((2, 128, NROWS), np.float32)
    tab[0] = 1.0
    for h in range(2):
        tab[0, h * 64:h * 64 + 8] = cos
        tab[0, h * 64 + 8:h * 64 + 16] = cos
        tab[1, h * 64:h * 64 + 8] = sin
        tab[1, h * 64 + 8:h * 64 + 16] = sin
    return tab


def kernel(x_prompt, x_sample, state_conv, cache_k_win, cache_v_win, g_ff1, w_ff1_in, w_ff1_out, g_mix, w_in, g_q, g_k,
           sinks, w_dw, b_dw, g_cn, b_cn, w_out, g_ff2, w_ff2_in, w_ff2_out):
    f = lambda a: np.ascontiguousarray(np.asarray(a, dtype=np.float32))
    x_prompt, x_sample, state_conv, cache_k_win, cache_v_win = map(f, (x_prompt, x_sample, state_conv, cache_k_win, cache_v_win))
    if "nc" not in _CACHE:
        _CACHE["nc"] = build_nc()[0]
    nc = _CACHE["nc"]
    wall = _pack_weights(f(w_ff1_in)[0], f(w_ff1_out)[0], f(w_in)[0], f(w_out)[0], f(w_ff2_in)[0], f(w_ff2_out)[0])
    gq, gk_ = f(g_q)[0], f(g_k)[0]
    setup_rows = np.concatenate([f(g_ff1)[0].reshape(8, 128), f(g_mix)[0].reshape(8, 128), f(g_ff2)[0].reshape(8, 128),
                                 f(b_dw)[0].reshape(4, 128), f(g_cn)[0].reshape(4, 128), f(b_cn)[0].reshape(4, 128),
                                 np.concatenate([gq, gq])[None], np.concatenate([gk_, gk_])[None]], axis=0)
    cmat = np.zeros((4, 128, 128), np.float32)
    cmat[0] = np.eye(128)
    cmat[1] = 1.0
    cmat[2, 0:64, 0:64] = 1.0
    cmat[2, 64:128, 64:128] = 1.0
    for h in range(2):
        for m in range(8):
            cmat[3, h * 64 + m + 8, h * 64 + m] = -1.0
            cmat[3, h * 64 + m, h * 64 + m + 8] = 1.0
    opad1 = np.zeros((128, 192), np.float32)
    opad1[:, 0:64] = 1.0
    opad1[:, 128:192] = 1.0
    smask = np.zeros((64, 256), np.float32)
    for s_ in range(4):
        for j in range(4):
            smask[s_ * 16:(s_ + 1) * 16, j * 64 + s_ * 16:j * 64 + (s_ + 1) * 16] = 1.0
    in_maps = []
    for c in range(8):
        b, hf = c // 2, c % 2
        base = hf * HALF
        xin = np.zeros((NROWS, D), np.float32)
        xin[0:HALF] = x_prompt[b, base:base + HALF]
        if hf == 1:
            xin[HALF:HALF + 128] = x_prompt[b, base - 128:base]
        xin[HALF + 128:] = x_sample[4 * c:4 * c + 4].reshape(64, D)
        opad = np.stack([opad1, opad1 * np.float32(hf)])
        in_maps.append(dict(
            xin=xin, wall=wall, rope=_rope_tables(base), setup_rows=setup_rows, wdw=f(w_dw)[0], sinks=f(sinks),
            cmat=cmat, opad=opad, smask=smask, sconv=state_conv[0, 4 * c:4 * c + 4], ck=cache_k_win[0, 4 * c:4 * c + 4].reshape(4, 128, 128),
            cv=cache_v_win[0, 4 * c:4 * c + 4].reshape(4, 128, 128)))
    res = run_bass_kernel_spmd(nc, in_maps, core_ids=list(range(8)))
    r = res.results
    y_prompt = np.zeros((4, SEQ, D), np.float32)
    y_sample = np.zeros((32, 16, D), np.float32)
    csp = np.zeros((1, 4, 30, 512), np.float32)
    kwp = np.zeros((1, 4, 128, 2, 64), np.float32)
    vwp = np.zeros((1, 4, 128, 2, 64), np.float32)
    css = np.zeros((1, 32, 30, 512), np.float32)
    kws = np.zeros((1, 32, 128, 2, 64), np.float32)
    vws = np.zeros((1, 32, 128, 2, 64), np.float32)
    for c in range(8):
        b, hf = c // 2, c % 2
        y_prompt[b, hf * HALF:(hf + 1) * HALF] = r[c]["yout"][0:HALF]
        y_sample[4 * c:4 * c + 4] = r[c]["yout"][HALF:].reshape(4, 16, D)
        if hf == 1:
            csp[0, b] = r[c]["ocs_p"]
            kwp[0, b] = r[c]["okw_p"].reshape(128, 2, 64)
            vwp[0, b] = r[c]["ovw_p"].reshape(128, 2, 64)
        css[0, 4 * c:4 * c + 4] = r[c]["ocs_s"]
        kws[0, 4 * c:4 * c + 4] = r[c]["okw_s"].reshape(4, 128, 2, 64)
        vws[0, 4 * c:4 * c + 4] = r[c]["ovw_s"].reshape(4, 128, 2, 64)
    return (y_prompt, y_sample, csp, kwp, vwp, css, kws, vws)
```
